# Optimizing a Trainium2 kernel written in Bass

```python
import math
import jax, jax.numpy as jnp
from jax import lax
import numpy as np

D_MODEL = 1024
BATCH = 16
SEQ = 4096
DEPTH = 4

MLA_HEADS = 8
MLA_NOPE = 64
MLA_ROPE = 32
MLA_V = 64
MLA_QK = MLA_NOPE + MLA_ROPE
MLA_Q_RANK = 256
MLA_KV_RANK = 128
ROPE_THETA = 10000.0
Q_BLOCK = 128
MOBA_HEADS = 8
MOBA_HEAD_DIM = 64
MOBA_W = MOBA_HEADS * MOBA_HEAD_DIM
MOBA_BLOCK = 256
MOBA_TOPK = 3
MOBA_Q_CHUNK = 16
S5_GROUP = 16
S5_GROUPS = D_MODEL // S5_GROUP
S5_STATE = 64
S5_CHUNK = 128
DT_MIN = 1e-3
DT_MAX = 1e-1
D_FF = 4 * D_MODEL
EPS = 1e-6

N_EVEN = (DEPTH + 1) // 2
N_ODD = DEPTH // 2
IN_SIZES = [MLA_Q_RANK, MLA_KV_RANK, MLA_ROPE, MOBA_W, MOBA_W, MOBA_W]
IN_COLS = sum(IN_SIZES)
IN_SPLITS = [int(v) for v in np.cumsum(IN_SIZES)[:-1]]
MIX_WIDTH = MLA_HEADS * MLA_V + MOBA_W

kernel_name = "hybrid_mla_moba_s5_block"


def rms_norm(x, g):
    xf = x.astype(jnp.float32)
    y = xf * lax.rsqrt(jnp.mean(xf * xf, axis=-1, keepdims=True) + EPS)
    return (y * g.astype(jnp.float32)).astype(x.dtype)


def apply_rope(x, pos):
    half = x.shape[-1] // 2
    inv = ROPE_THETA ** (-jnp.arange(half, dtype=jnp.float32) / half)
    ang = pos.astype(jnp.float32)[:, None] * inv[None, :]
    cos = jnp.cos(ang).astype(x.dtype)
    sin = jnp.sin(ang).astype(x.dtype)
    x1, x2 = x[..., :half], x[..., half:]
    return jnp.concatenate([x1 * cos - x2 * sin, x1 * sin + x2 * cos], axis=-1)


def causal_attention_blocks(q, k, v):
    S = q.shape[2]
    outs = []
    for i in range(S // Q_BLOCK):
        lo, hi = i * Q_BLOCK, (i + 1) * Q_BLOCK
        s = jnp.einsum('bhqd,bhkd->bhqk', q[:, :, lo:hi], k[:, :, :hi]).astype(jnp.float32)
        mask = jnp.arange(hi)[None, :] <= jnp.arange(lo, hi)[:, None]
        p = jax.nn.softmax(jnp.where(mask, s, -jnp.inf), axis=-1).astype(v.dtype)
        outs.append(jnp.einsum('bhqk,bhkd->bhqd', p, v[:, :, :hi]))
    return jnp.concatenate(outs, axis=2)


def moba_attention(q, k, v):
    B, H, S, d = q.shape
    nb = -(-S // MOBA_BLOCK)
    pad = nb * MOBA_BLOCK - S
    kp = jnp.pad(k, ((0, 0), (0, 0), (0, pad), (0, 0)))
    vp = jnp.pad(v, ((0, 0), (0, 0), (0, pad), (0, 0)))
    k_blocks = kp.reshape(B, H, nb, MOBA_BLOCK, d)
    kv_blocks = jnp.concatenate([k_blocks, vp.reshape(B, H, nb, MOBA_BLOCK, d)], axis=-1)
    k_mean = jnp.mean(k_blocks.astype(jnp.float32), axis=3).astype(k.dtype)
    n_sel = min(MOBA_TOPK, nb - 1)
    b_idx = jnp.arange(B)[:, None, None, None]
    h_idx = jnp.arange(H)[None, :, None, None]
    blk_pos = jnp.arange(MOBA_BLOCK)

    def chunk(c):
        t0 = c * MOBA_Q_CHUNK
        qc = lax.dynamic_slice_in_dim(q, t0, MOBA_Q_CHUNK, axis=2)
        t = t0 + jnp.arange(MOBA_Q_CHUNK)
        cur = t0 // MOBA_BLOCK
        kv_own = lax.dynamic_index_in_dim(kv_blocks, cur, axis=2, keepdims=False)
        s_own = jnp.einsum('bhqd,bhld->bhql', qc, kv_own[..., :d]).astype(jnp.float32)
        s_own = jnp.where(cur * MOBA_BLOCK + blk_pos[None, :] <= t[:, None], s_own, -jnp.inf)
        if n_sel == 0:
            p = jax.nn.softmax(s_own, axis=-1).astype(v.dtype)
            return jnp.einsum('bhql,bhld->bhqd', p, kv_own[..., d:])
        gate = jnp.einsum('bhqd,bhnd->bhqn', qc, k_mean).astype(jnp.float32)
        gate = jnp.where(jnp.arange(nb) < cur, gate, -jnp.inf)
        _, sel = lax.top_k(gate, n_sel)
        sel_ok = jnp.arange(n_sel) < cur
        kv_sel = kv_blocks[b_idx, h_idx, sel]
        s_sel = jnp.einsum('bhqd,bhqnld->bhqnl', qc, kv_sel[..., :d]).astype(jnp.float32)
        s_sel = jnp.where(sel_ok[:, None], s_sel, -jnp.inf)
        s_sel = s_sel.reshape(B, H, MOBA_Q_CHUNK, n_sel * MOBA_BLOCK)
        p = jax.nn.softmax(jnp.concatenate([s_sel, s_own], axis=-1), axis=-1).astype(v.dtype)
        p_sel = p[..., :n_sel * MOBA_BLOCK].reshape(B, H, MOBA_Q_CHUNK, n_sel, MOBA_BLOCK)
        p_own = p[..., n_sel * MOBA_BLOCK:]
        return (jnp.einsum('bhqnl,bhqnld->bhqd', p_sel, kv_sel[..., d:])
                + jnp.einsum('bhql,bhld->bhqd', p_own, kv_own[..., d:]))

    outs = lax.map(chunk, jnp.arange(S // MOBA_Q_CHUNK))
    return outs.transpose(1, 2, 0, 3, 4).reshape(B, H, S, d)


def attn_mixer(h, w_in, g_cq, w_uq, g_ckv, w_ukv, g_qn_mla, g_kn_mla, g_qn_moba, g_kn_moba, w_o):
    B, S, _ = h.shape
    pos = jnp.arange(S)
    c_q, c_kv, k_r, q_b, k_b, v_b = jnp.split(h @ w_in, IN_SPLITS, axis=-1)
    q = (rms_norm(c_q, g_cq) @ w_uq).reshape(B, S, MLA_HEADS, MLA_QK).transpose(0, 2, 1, 3)
    kv = (rms_norm(c_kv, g_ckv) @ w_ukv).reshape(B, S, MLA_HEADS, MLA_NOPE + MLA_V).transpose(0, 2, 1, 3)
    k_nope, v = kv[..., :MLA_NOPE], kv[..., MLA_NOPE:]
    q = jnp.concatenate([q[..., :MLA_NOPE], apply_rope(q[..., MLA_NOPE:], pos)], axis=-1)
    k_rope = jnp.broadcast_to(apply_rope(k_r, pos)[:, None], (B, MLA_HEADS, S, MLA_ROPE))
    k = jnp.concatenate([k_nope, k_rope], axis=-1)
    q = rms_norm(q, g_qn_mla) * (MLA_QK ** -0.5)
    k = rms_norm(k, g_kn_mla)
    o_mla = causal_attention_blocks(q, k, v)
    def heads(t):
        return t.reshape(B, S, MOBA_HEADS, MOBA_HEAD_DIM).transpose(0, 2, 1, 3)
    qm = rms_norm(heads(q_b), g_qn_moba) * (MOBA_HEAD_DIM ** -0.5)
    km = rms_norm(heads(k_b), g_kn_moba)
    o_moba = moba_attention(qm, km, heads(v_b))
    o = jnp.concatenate([o_mla.transpose(0, 2, 1, 3).reshape(B, S, MLA_HEADS * MLA_V),
                         o_moba.transpose(0, 2, 1, 3).reshape(B, S, MOBA_W)], axis=-1)
    return o @ w_o


def s5_mixer(h, lam_re, lam_im, log_dt, b_re, b_im, c_re, c_im, d_skip, w_glu):
    B, S, D = h.shape
    f32 = jnp.float32
    u = h.astype(f32)
    lam = lax.complex(lam_re.astype(f32), lam_im.astype(f32))
    dt = jnp.exp(log_dt.astype(f32))[:, None]
    a_bar = jnp.exp(lam * dt)
    b_bar = ((a_bar - 1.0) / lam)[..., None] * lax.complex(b_re.astype(f32), b_im.astype(f32))
    cmat = lax.complex(c_re.astype(f32), c_im.astype(f32))
    nc = S // S5_CHUNK
    u_chunks = u.reshape(B, nc, S5_CHUNK, S5_GROUPS, S5_GROUP).transpose(1, 2, 0, 3, 4)
    steps = jnp.arange(1, S5_CHUNK + 1, dtype=f32)
    a_pow = jnp.exp(lam[None] * dt[None] * steps[:, None, None])
    a_el = jnp.broadcast_to(a_bar, (S5_CHUNK, 1, S5_GROUPS, S5_STATE))

    def combine(e1, e2):
        a1, b1 = e1
        a2, b2 = e2
        return a1 * a2, a2 * b1 + b2

    def step(state, uc):
        bu = jnp.einsum('gpi,lbgi->lbgp', b_bar, uc.astype(jnp.complex64))
        _, hs = lax.associative_scan(combine, (a_el, bu), axis=0)
        hs = hs + a_pow[:, None] * state[None]
        y = jnp.einsum('gop,lbgp->lbgo', cmat, hs).real
        return hs[-1], y

    state0 = jnp.zeros((B, S5_GROUPS, S5_STATE), jnp.complex64)
    _, ys = lax.scan(step, state0, u_chunks)
    y = ys.transpose(2, 0, 1, 3, 4).reshape(B, S, D) + d_skip.astype(f32) * u
    g = jax.nn.gelu(y).astype(h.dtype)
    val, gate = jnp.split(g @ w_glu, 2, axis=-1)
    return val * jax.nn.sigmoid(gate)


def sq_relu_mlp(h, w1, w2):
    return jnp.square(jax.nn.relu(h @ w1)) @ w2


def setup_inputs(seed: int = 0) -> dict:
    key = jax.random.key(seed)
    ks = iter(jax.random.split(key, 32))
    f32 = jnp.float32

    def nrm(shape, scale):
        return jax.random.normal(next(ks), shape, f32) * scale

    def gain(shape):
        return 1.0 + 0.02 * jax.random.normal(next(ks), shape, f32)

    n_idx = jnp.arange(S5_STATE, dtype=f32)
    return {
        "x": nrm((BATCH, SEQ, D_MODEL), 1.0),
        "mix_norm_g": gain((DEPTH, D_MODEL)),
        "ffn_norm_g": gain((DEPTH, D_MODEL)),
        "w_in": nrm((N_EVEN, D_MODEL, IN_COLS), D_MODEL ** -0.5),
        "g_cq": gain((N_EVEN, MLA_Q_RANK)),
        "w_uq": nrm((N_EVEN, MLA_Q_RANK, MLA_HEADS * MLA_QK), MLA_Q_RANK ** -0.5),
        "g_ckv": gain((N_EVEN, MLA_KV_RANK)),
        "w_ukv": nrm((N_EVEN, MLA_KV_RANK, MLA_HEADS * (MLA_NOPE + MLA_V)), MLA_KV_RANK ** -0.5),
        "g_qn_mla": gain((N_EVEN, MLA_QK)),
        "g_kn_mla": gain((N_EVEN, MLA_QK)),
        "g_qn_moba": gain((N_EVEN, MOBA_HEAD_DIM)),
        "g_kn_moba": gain((N_EVEN, MOBA_HEAD_DIM)),
        "w_o": nrm((N_EVEN, MIX_WIDTH, D_MODEL), MIX_WIDTH ** -0.5),
        "lam_re": -0.5 + nrm((N_ODD, S5_GROUPS, S5_STATE), 0.01),
        "lam_im": math.pi * n_idx + nrm((N_ODD, S5_GROUPS, S5_STATE), 0.01),
        "log_dt": jax.random.uniform(next(ks), (N_ODD, S5_GROUPS), f32, math.log(DT_MIN), math.log(DT_MAX)),
        "b_re": nrm((N_ODD, S5_GROUPS, S5_STATE, S5_GROUP), (2 * S5_GROUP) ** -0.5),
        "b_im": nrm((N_ODD, S5_GROUPS, S5_STATE, S5_GROUP), (2 * S5_GROUP) ** -0.5),
        "c_re": nrm((N_ODD, S5_GROUPS, S5_GROUP, S5_STATE), (2 * S5_STATE) ** -0.5),
        "c_im": nrm((N_ODD, S5_GROUPS, S5_GROUP, S5_STATE), (2 * S5_STATE) ** -0.5),
        "d_skip": nrm((N_ODD, D_MODEL), 1.0),
        "w_glu": nrm((N_ODD, D_MODEL, 2 * D_MODEL), D_MODEL ** -0.5),
        "w_ff1": nrm((DEPTH, D_MODEL, D_FF), D_MODEL ** -0.5),
        "w_ff2": nrm((DEPTH, D_FF, D_MODEL), D_FF ** -0.5),
    }


def reference(x, mix_norm_g, ffn_norm_g, w_in, g_cq, w_uq, g_ckv, w_ukv, g_qn_mla, g_kn_mla,
              g_qn_moba, g_kn_moba, w_o, lam_re, lam_im, log_dt, b_re, b_im, c_re, c_im,
              d_skip, w_glu, w_ff1, w_ff2):
    for layer in range(DEPTH):
        h = rms_norm(x, mix_norm_g[layer])
        i = layer // 2
        if layer % 2 == 0:
            x = x + attn_mixer(h, w_in[i], g_cq[i], w_uq[i], g_ckv[i], w_ukv[i], g_qn_mla[i],
                               g_kn_mla[i], g_qn_moba[i], g_kn_moba[i], w_o[i])
        else:
            x = x + s5_mixer(h, lam_re[i], lam_im[i], log_dt[i], b_re[i], b_im[i], c_re[i],
                             c_im[i], d_skip[i], w_glu[i]).astype(x.dtype)
        x = x + sq_relu_mlp(rms_norm(x, ffn_norm_g[layer]), w_ff1[layer], w_ff2[layer])
    return x
```

```python
import numpy as np
import ml_dtypes
import concourse.bass as bass
import concourse.mybir as mybir
from concourse.bass_utils import run_bass_kernel_spmd

F32 = mybir.dt.float32
BF16 = mybir.dt.bfloat16
ALU = mybir.AluOpType
AF = mybir.ActivationFunctionType
AX = mybir.AxisListType

NCORES = 8
D = 1024
SEQ = 4096
NSEQ = 2
TOK = NSEQ * SEQ
DFF = 4096
DEPTH = 4
EPS = 1e-6
IN_COLS = 1952

SAME_ENGINE_SYNC = True
N_DSEM = 40


class Buf:
    __slots__ = ("name", "lastw", "lastr", "dmar")

    def __init__(self, name=""):
        self.name = name
        self.lastw = {}
        self.lastr = {}
        self.dmar = []


class Op:
    __slots__ = ("eng", "fn", "deps", "is_dma", "signal", "sigval", "dsem", "dval", "idx")

    def __init__(self, eng, fn, is_dma):
        self.eng = eng
        self.fn = fn
        self.is_dma = is_dma
        self.deps = []
        self.signal = False
        self.sigval = None
        self.dsem = None
        self.dval = None


ENGS = ["pe", "act", "dve", "pool", "sp"]


class Sched:
    def __init__(self):
        self.ops = {e: [] for e in ENGS}
        self.n_dma = 0
        self.pending = {}
        self.dma_since = []

    def barrier(self):
        deps = [self.ops[e][-1] for e in ENGS if self.ops[e] and not self.ops[e][-1].is_dma]
        for e in ENGS:
            for o in reversed(self.ops[e]):
                if not o.is_dma:
                    if o not in deps:
                        deps.append(o)
                    break
        deps = deps + list(self.dma_since)
        self.dma_since = []
        for e in ENGS:
            self.pending[e] = list(self.pending.get(e, [])) + deps

    def op(self, eng, fn, reads=(), writes=(), dma=False):
        o = Op(eng, fn, dma)
        deps = {}
        for b in reads:
            for w in b.lastw.values():
                deps[id(w)] = w
        for b in writes:
            for w in b.lastw.values():
                deps[id(w)] = w
            for r in b.lastr.values():
                deps[id(r)] = r
            for r in b.dmar:
                deps[id(r)] = r
        if eng in self.pending:
            for d in self.pending.pop(eng):
                deps[id(d)] = d
        if dma:
            self.dma_since.append(o)
        o.deps = list(deps.values())
        for b in reads:
            if dma:
                b.dmar.append(o)
            else:
                b.lastr[eng] = o
        for b in writes:
            b.lastw = {("dma%d" % id(o)) if dma else eng: o}
            b.lastr = {}
            b.dmar = []
        self.ops[eng].append(o)
        return o

    def emit(self, nc, block):
        for e in ENGS:
            for o in self.ops[e]:
                for d in o.deps:
                    if d.is_dma:
                        continue
                    if d.eng == o.eng and not o.is_dma:
                        if d.eng == "pe" or not SAME_ENGINE_SYNC:
                            continue
                    d.signal = True
        dcount = [0] * N_DSEM
        k = 0
        for e in ENGS:
            c = 0
            for o in self.ops[e]:
                if o.is_dma:
                    pass
                elif o.signal:
                    c += 1
                    o.sigval = c
        sems = {e: nc.alloc_semaphore(name="sem_" + e) for e in ["pe", "act", "dve", "pool"]}
        dsems = [nc.alloc_semaphore(name="dsem%d" % i) for i in range(N_DSEM)]
        pools = {"sp": (0, 24), "act": (24, 8), "pool": (32, 8), "dve": (32, 8), "pe": (32, 8)}
        for e in ENGS:
            base, n = pools[e]
            k = 0
            for o in self.ops[e]:
                if o.is_dma:
                    o.dsem = base + k % n
                    dcount[o.dsem] += 16
                    o.dval = dcount[o.dsem]
                    k += 1
        self.final_dma = [(i, dcount[i]) for i in range(N_DSEM) if dcount[i] > 0]

        def run_engine(ename, eng):
            waited = {}

            def wait(key, sem, val):
                if waited.get(key, 0) >= val:
                    return
                eng.wait_ge(sem, val)
                waited[key] = val

            for o in self.ops[ename]:
                for d in o.deps:
                    if d.is_dma:
                        wait(("d", d.dsem), dsems[d.dsem], d.dval)
                    else:
                        if d.eng == ename and not o.is_dma:
                            if ename == "pe" or not SAME_ENGINE_SYNC:
                                continue
                        wait(d.eng, sems[d.eng], d.sigval)
                if o.is_dma:
                    if o.dval > 16:
                        wait(("d", o.dsem), dsems[o.dsem], o.dval - 16)
                    ins = o.fn(eng)
                    ins.then_inc(dsems[o.dsem], 16)
                else:
                    ins = o.fn(eng)
                    if o.signal:
                        ins.then_inc(sems[ename], 1)
            if ename == "sp":
                for i, v in self.final_dma:
                    eng.wait_ge(dsems[i], v)

        @block.tensor
        def _(eng):
            run_engine("pe", eng)

        @block.scalar
        def _(eng):
            run_engine("act", eng)

        @block.vector
        def _(eng):
            run_engine("dve", eng)

        @block.gpsimd
        def _(eng):
            run_engine("pool", eng)

        @block.sync
        def _(eng):
            run_engine("sp", eng)


import contextlib

NEG = -1.0e30
GROUPS_A = [(0, 416), (416, 928), (928, 1440), (1440, 1952)]


def build(cfg):
    nc = bass.Bass("TRN2", target_bir_lowering=False)
    S = Sched()
    nseq = cfg.get("nseq", NSEQ)
    tok = nseq * SEQ
    layers = cfg.get("layers", list(range(DEPTH)))
    do_ffn = cfg.get("ffn", True)
    do_mix = cfg.get("mix", True)
    dbg = cfg.get("dbg", ())
    NT = tok // 128
    NT5 = tok // 512

    def din(name, shape, dt=F32):
        return nc.dram_tensor(name, list(shape), dt, kind="ExternalInput").ap()

    def dscr(name, shape, dt):
        kind = "ExternalOutput" if name in dbg else "Internal"
        return nc.dram_tensor(name, list(shape), dt, kind=kind).ap()

    x_in = din("x", [tok, D])
    mix_g = din("mix_norm_g", [DEPTH, D])
    ffn_g = din("ffn_norm_g", [DEPTH, D])
    w_in = din("w_in", [2, D, IN_COLS])
    g_cq = din("g_cq", [2, 256])
    w_uq = din("w_uq", [2, 256, 768])
    g_ckv = din("g_ckv", [2, 128])
    w_ukv = din("w_ukv", [2, 128, 1024])
    g_qn_mla = din("g_qn_mla", [2, 96])
    g_kn_mla = din("g_kn_mla", [2, 96])
    g_qn_moba = din("g_qn_moba", [2, 64])
    g_kn_moba = din("g_kn_moba", [2, 64])
    w_o = din("w_o", [2, D, D])
    lam_re = din("lam_re", [2, 64, 64])
    lam_im = din("lam_im", [2, 64, 64])
    log_dt = din("log_dt", [2, 64])
    b_re = din("b_re", [2, 64, 64, 16])
    b_im = din("b_im", [2, 64, 64, 16])
    c_re = din("c_re", [2, 64, 16, 64])
    c_im = din("c_im", [2, 64, 16, 64])
    d_skip = din("d_skip", [2, D])
    w_glu = din("w_glu", [2, D, 2 * D])
    w_ff1 = din("w_ff1", [DEPTH, D, DFF])
    w_ff2 = din("w_ff2", [DEPTH, DFF, D])
    ident_in = din("ident", [128, 128], BF16)
    tri_in = din("trimask", [128, 128], BF16)
    cos_in = din("rope_cos", [SEQ, 16])
    sin_in = din("rope_sin", [SEQ, 16])
    out = nc.dram_tensor("out", [tok, D], F32, kind="ExternalOutput").ap()

    xa = dscr("xa", [tok, D], F32)
    xm = dscr("xm", [tok, D], F32)
    xh = dscr("xh", [tok, D], F32)
    hTd = dscr("hTd", [NT5, 128, 8 * 512], BF16)
    w1c = dscr("w1c", [DEPTH, 2, 128, 8 * 2048], BF16)
    w2c = dscr("w2c", [DEPTH, 2, 128, 16 * 1024], BF16)
    winc = dscr("winc", [2, 128, 8 * IN_COLS], BF16)
    wuqc = dscr("wuqc", [2, 128, 2 * 768], BF16)
    wukvc = dscr("wukvc", [2, 128, 1024], BF16)
    woc = dscr("woc", [2, 128, 8 * 1024], BF16)
    QTm = dscr("QTm", [nseq, 8, 96, SEQ], BF16)
    KTm = dscr("KTm", [nseq, 8, 96, SEQ], BF16)
    Vm = dscr("Vm", [nseq, 8, 128, 32 * 65], BF16)
    QTb = dscr("QTb", [nseq, 4, 128, SEQ], BF16)
    KTb = dscr("KTb", [nseq, 4, 128, SEQ], BF16)
    Vb = dscr("Vb", [nseq, 8, 128, 32 * 65], BF16)
    Od = dscr("Od", [tok, D], BF16)
    uTd = dscr("uTd", [8, 128, tok], BF16)
    yd = dscr("yd", [tok, D], F32)
    BTd = dscr("BTd", [2, 32, 2, 128, 128], BF16)
    wgluc = dscr("wgluc", [2, 128, 8 * 2048], BF16)
    BTd2 = dscr("BTd2", [2, 32, 8, 2, 32, 128], BF16)
    Mlagd = dscr("Mlagd", [2, 8, 128, 1024], BF16)
    uTd2 = dscr("uTd2", [8, 128, nseq, 8, 512], BF16)

    es = contextlib.ExitStack()
    stk = [es]
    uid = [0]

    def sb(name, shape, dt):
        uid[0] += 1
        return stk[-1].enter_context(nc.sbuf_tensor("s%d_%s" % (uid[0], name), list(shape), dt))

    def ps(name, shape, dt):
        uid[0] += 1
        return stk[-1].enter_context(nc.psum_tensor("p%d_%s" % (uid[0], name), list(shape), dt))

    def phase_begin():
        stk.append(contextlib.ExitStack())

    def phase_end():
        S.barrier()
        stk.pop().close()

    def I(eng, name, reads, writes, *args, **kw):
        return S.op(eng, lambda e: getattr(e, name)(*args, **kw), reads, writes)

    def DMA(out_, in_, reads, writes, q="sp", **kw):
        return S.op(q, lambda e: e.dma_start(out=out_, in_=in_, **kw), reads, writes, dma=True)

    class Rot:
        def __init__(self, name, n, shape, dt, psum=False):
            mk = ps if psum else sb
            self.t = [mk("%s%d" % (name, i), shape, dt) for i in range(n)]
            self.b = [Buf("%s%d" % (name, i)) for i in range(n)]
            self.i = 0

        def next(self):
            k = self.i % len(self.t)
            self.i += 1
            return self.t[k], self.b[k]

    def rstd_cols(st_ap, st_buf, div):
        I("dve", "tensor_scalar", [st_buf], [st_buf], out=st_ap, in0=st_ap, scalar1=1.0 / div, scalar2=EPS,
          op0=ALU.mult, op1=ALU.add)
        I("act", "activation", [st_buf], [st_buf], out=st_ap, in_=st_ap, func=AF.Ln)
        I("act", "activation", [st_buf], [st_buf], out=st_ap, in_=st_ap, func=AF.Exp, scale=-0.5)

    with es:
        ident = sb("ident", [128, 128], BF16)
        b_ident = Buf("ident")
        DMA(ident[:], ident_in, [], [b_ident])
        tri = sb("tri", [128, 128], BF16)
        b_tri = Buf("tri")
        DMA(tri[:], tri_in, [], [b_tri])

        def prep_phase():
            phase_begin()
            stage = Rot("stage", 2, [128, 4096], F32)
            stageo = Rot("stageo", 2, [128, 4096], BF16)
            gsb = sb("gsb", [128, 80], F32)
            b_gsb = Buf("gsb")
            DMA(gsb[:, 0:32].rearrange("p (l k) -> p l k", l=DEPTH), ffn_g.rearrange("l (k p) -> p l k", p=128),
                [], [b_gsb], allow_slow_non_contiguous=True)
            DMA(gsb[:, 32:64].rearrange("p (l k) -> p l k", l=DEPTH), mix_g.rearrange("l (k p) -> p l k", p=128),
                [], [b_gsb], allow_slow_non_contiguous=True)
            DMA(gsb[:, 64:68].rearrange("p (l k) -> p l k", l=2), g_cq.rearrange("l (k p) -> p l k", p=128),
                [], [b_gsb], allow_slow_non_contiguous=True)
            DMA(gsb[:, 68:70].rearrange("p (l k) -> p l k", l=2), g_ckv.rearrange("l (k p) -> p l k", p=128),
                [], [b_gsb], allow_slow_non_contiguous=True)
            cnt = [0]

            def prep(src_ap, src_view, ncols, gcol, dsts):
                st, bst = stage.next()
                so, bso = stageo.next()
                eng = ["dve", "pool"][cnt[0] % 2]
                cnt[0] += 1
                DMA(src_view(st), src_ap, [], [bst])
                if gcol is None:
                    I(eng, "tensor_copy", [bst], [bso], out=so[:, :ncols], in_=st[:, :ncols])
                else:
                    I(eng, "tensor_scalar", [bst, b_gsb], [bso], out=so[:, :ncols], in0=st[:, :ncols], scalar1=gcol,
                      scalar2=None, op0=ALU.mult)
                for dap, sl in dsts:
                    DMA(dap, so[:, sl], [bso], [], q="act")

            for L in layers:
                if do_ffn:
                    for k in range(8):
                        dsts = [(w1c[L, h].rearrange("p (k c) -> p k c", k=8)[:, k, :],
                                 slice(h * 2048, (h + 1) * 2048)) for h in range(2)]
                        prep(w_ff1[L, k * 128:(k + 1) * 128, :], lambda st: st[:, :4096], 4096,
                             gsb[:, L * 8 + k:L * 8 + k + 1], dsts)
                    for jj in range(8):
                        h, j0 = divmod(jj * 4, 16)
                        prep(w_ff2[L, jj * 512:(jj + 1) * 512, :].rearrange("(j p) c -> p j c", p=128),
                             lambda st: st[:].rearrange("p (j c) -> p j c", j=4), 4096, None,
                             [(w2c[L, h][:, j0 * 1024:(j0 + 4) * 1024], slice(0, 4096))])
                if do_mix and L % 2 == 0:
                    i = L // 2
                    for k in range(8):
                        prep(w_in[i, k * 128:(k + 1) * 128, :], lambda st: st[:, :IN_COLS], IN_COLS,
                             gsb[:, 32 + L * 8 + k:32 + L * 8 + k + 1],
                             [(winc[i].rearrange("p (k c) -> p k c", k=8)[:, k, :], slice(0, IN_COLS))])
                    for k in range(2):
                        prep(w_uq[i, k * 128:(k + 1) * 128, :], lambda st: st[:, :768], 768,
                             gsb[:, 64 + i * 2 + k:64 + i * 2 + k + 1],
                             [(wuqc[i][:, k * 768:(k + 1) * 768], slice(0, 768))])
                    prep(w_ukv[i], lambda st: st[:, :1024], 1024, gsb[:, 68 + i:69 + i],
                         [(wukvc[i], slice(0, 1024))])
                    for kk in range(2):
                        prep(w_o[i, kk * 512:(kk + 1) * 512, :].rearrange("(j p) c -> p j c", p=128),
                             lambda st: st[:].rearrange("p (j c) -> p j c", j=4), 4096, None,
                             [(woc[i][:, kk * 4096:(kk + 1) * 4096], slice(0, 4096))])
                if do_mix and L % 2 == 1:
                    i = L // 2
                    for k in range(8):
                        prep(w_glu[i, k * 128:(k + 1) * 128, :], lambda st: st[:, :2048], 2048, None,
                             [(wgluc[i].rearrange("p (k c) -> p k c", k=8)[:, k, :], slice(0, 2048))])
            phase_end()

        def ffn_half(L, h, src, dst):
            phase_begin()
            w1s = sb("w1s", [128, 8 * 2048], BF16)
            w2s = sb("w2s", [128, 16 * 1024], BF16)
            b_w1s, b_w2s = Buf(), Buf()
            for k in range(8):
                DMA(w1s[:, k * 2048:(k + 1) * 2048], w1c[L, h][:, k * 2048:(k + 1) * 2048], [], [b_w1s])
            for k in range(4):
                DMA(w2s[:, k * 4096:(k + 1) * 4096], w2c[L, h][:, k * 4096:(k + 1) * 4096], [], [b_w2s])
            xt = Rot("xt", 3, [128, 4 * 1024], F32)
            hT = Rot("hT", 2, [128, 8 * 512], BF16)
            aT = sb("aT", [128, 16 * 512], BF16)
            b_aT = [Buf() for _ in range(16)]
            xn = Rot("xn", 2, [128, 1024], BF16)
            junk = sb("junk", [128, 1024], F32)
            b_junk = Buf()
            rl = Rot("rl", 3, [128, 512], F32)
            stat = Rot("stat", 4, [128, 1], F32)
            ps_h = Rot("ps_h", 3, [128, 512], F32, psum=True)
            ps_y = Rot("ps_y", 2, [128, 512], F32, psum=True)
            ps_t = Rot("ps_t", 2, [128, 1024], BF16, psum=True)
            tiles = {}

            def stage_a(t):
                xti, bxt = xt.next()
                hTi, bhT = hT.next()
                tiles[t] = (xti, bxt, hTi, bhT)
                rows = slice(t * 512, (t + 1) * 512)
                DMA(xti[:].rearrange("p (s c) -> p s c", s=4), src[rows, :].rearrange("(s p) c -> p s c", p=128),
                    [], [bxt])
                if h == 0:
                    for s in range(4):
                        xs = xti[:, s * 1024:(s + 1) * 1024]
                        st, bst = stat.next()
                        I("act", "activation", [bxt], [b_junk, bst], out=junk[:], in_=xs, func=AF.Square,
                          accum_out=st[:, 0:1])
                        rstd_cols(st[:, 0:1], bst, 1024)
                        xnq, bxn = xn.next()
                        I("dve", "tensor_scalar", [bxt, bst], [bxn], out=xnq[:], in0=xs, scalar1=st[:, 0:1],
                          scalar2=None, op0=ALU.mult)
                        pst, bpst = ps_t.next()
                        for k in range(8):
                            I("pe", "transpose", [bxn, b_ident], [bpst], out=pst[:, k * 128:(k + 1) * 128],
                              in_=xnq[:, k * 128:(k + 1) * 128], identity=ident[:])
                        I("act", "activation", [bpst], [bhT],
                          out=hTi[:].rearrange("p (k t) -> p k t", k=8)[:, :, s * 128:(s + 1) * 128],
                          in_=pst[:].rearrange("p (k t) -> p k t", k=8), func=AF.Copy)
                    DMA(hTd[t], hTi[:], [bhT], [], q="act")
                else:
                    DMA(hTi[:], hTd[t], [], [bhT])

            def stage_b(t):
                xti, bxt, hTi, bhT = tiles[t]
                for j in range(16):
                    psh, bpsh = ps_h.next()
                    for k in range(8):
                        I("pe", "matmul", [b_w1s, bhT], [bpsh], psh[:],
                          lhsT=w1s[:, k * 2048 + j * 128:k * 2048 + (j + 1) * 128],
                          rhs=hTi[:, k * 512:(k + 1) * 512], start=(k == 0), stop=(k == 7))
                    rlq, brl = rl.next()
                    I("act", "activation", [bpsh], [brl], out=rlq[:], in_=psh[:], func=AF.Relu)
                    I("pool" if j % 4 == 3 else "dve", "tensor_tensor", [brl], [b_aT[j]],
                      out=aT[:, j * 512:(j + 1) * 512], in0=rlq[:], in1=rlq[:], op=ALU.mult)

            def stage_c(t):
                xti, bxt, hTi, bhT = tiles.pop(t)
                rows = slice(t * 512, (t + 1) * 512)
                for s in range(4):
                    for c in range(2):
                        psy, bpsy = ps_y.next()
                        for j in range(16):
                            I("pe", "matmul", [b_aT[j], b_w2s], [bpsy], psy[:],
                              lhsT=aT[:, j * 512 + s * 128:j * 512 + (s + 1) * 128],
                              rhs=w2s[:, j * 1024 + c * 512:j * 1024 + (c + 1) * 512],
                              start=(j == 0), stop=(j == 15))
                        cs = slice(s * 1024 + c * 512, s * 1024 + (c + 1) * 512)
                        I("dve", "tensor_tensor", [bpsy, bxt], [bxt], out=xti[:, cs], in0=psy[:], in1=xti[:, cs],
                          op=ALU.add)
                DMA(dst[rows, :].rearrange("(s p) c -> p s c", p=128), xti[:].rearrange("p (s c) -> p s c", s=4),
                    [bxt], [], q="pool")

            stage_a(0)
            for t in range(NT5):
                if t + 1 < NT5:
                    stage_a(t + 1)
                stage_b(t)
                stage_c(t)
            phase_end()

        def skew(stages, n):
            ns = len(stages)
            for k in range(n + ns - 1):
                for si in reversed(range(ns)):
                    t = k - si
                    if 0 <= t < n:
                        stages[si](t)

        def attn_phase_a(i, L, src):
            phase_begin()
            win = sb("win", [128, 8 * IN_COLS], BF16)
            wuq = sb("wuq", [128, 2 * 768], BF16)
            wukv = sb("wukv", [128, 1024], BF16)
            b_w = Buf()
            DMA(win[:], winc[i], [], [b_w])
            DMA(wuq[:], wuqc[i], [], [b_w])
            DMA(wukv[:], wukvc[i], [], [b_w])
            cosT = sb("cosT", [128, 32 * 16], F32)
            sinT = sb("sinT", [128, 32 * 16], F32)
            b_cs = Buf()
            DMA(cosT[:].rearrange("p (t f) -> p t f", f=16), cos_in.rearrange("(t p) f -> p t f", p=128), [], [b_cs])
            DMA(sinT[:].rearrange("p (t f) -> p t f", f=16), sin_in.rearrange("(t p) f -> p t f", p=128), [], [b_cs])
            gq = sb("gq", [128, 4], F32)
            b_gq = Buf()
            DMA(gq[0:96, 0:1], g_qn_mla[i].rearrange("(d o) -> d o", o=1), [], [b_gq], allow_slow_non_contiguous=True)
            DMA(gq[0:96, 1:2], g_kn_mla[i].rearrange("(d o) -> d o", o=1), [], [b_gq], allow_slow_non_contiguous=True)
            for hh in range(2):
                DMA(gq[hh * 64:(hh + 1) * 64, 2:3], g_qn_moba[i].rearrange("(d o) -> d o", o=1), [], [b_gq],
                    allow_slow_non_contiguous=True)
                DMA(gq[hh * 64:(hh + 1) * 64, 3:4], g_kn_moba[i].rearrange("(d o) -> d o", o=1), [], [b_gq],
                    allow_slow_non_contiguous=True)
            sq = sb("sq", [128, 2], F32)
            b_sq = Buf()
            I("dve", "scalar_tensor_tensor", [b_gq], [b_sq], out=sq[0:96, 0:1], in0=gq[0:96, 0:1], scalar=96.0 ** -0.5,
              in1=gq[0:96, 1:2], op0=ALU.mult, op1=ALU.mult)
            I("dve", "scalar_tensor_tensor", [b_gq], [b_sq], out=sq[:, 1:2], in0=gq[:, 2:3], scalar=64.0 ** -0.5,
              in1=gq[:, 3:4], op0=ALU.mult, op1=ALU.mult)

            xt = Rot("xt", 2, [128, 1024], F32)
            xn = Rot("xn", 2, [128, 1024], BF16)
            xnT = Rot("xnT", 3, [128, 1024], BF16)
            junk = sb("junk", [128, 1024], F32)
            b_junk = Buf()
            st1 = Rot("st1", 9, [128, 1], F32)
            st8 = Rot("st8", 8, [128, 8], F32)
            pm = Rot("pm", 6, [128, 512], F32, psum=True)
            ps_t = Rot("ps_t", 2, [128, 1024], BF16, psum=True)
            cA = Rot("cA", 3, [128, 416], F32)
            cn = Rot("cn", 2, [128, 384], BF16)
            cnT = Rot("cnT", 2, [128, 384], BF16)
            qf = Rot("qf", 3, [128, 768], F32)
            kf = Rot("kf", 3, [128, 768], F32)
            sqt = Rot("sqt", 2, [128, 768], F32)
            rt = Rot("rt", 2, [128, 4 * 128], F32)
            krt = Rot("krt", 2, [128, 32], F32)
            nb = Rot("nb", 12, [128, 768], BF16)
            TT = Rot("TT", 4, [128, 1024], BF16)
            vmr = Rot("vm", 2, [128, 8 * 65], BF16)
            vbr = Rot("vb", 2, [128, 8 * 65], BF16)
            bf = Rot("bf", 6, [128, 512], F32)
            for r in (vmr, vbr):
                for tt, bb in zip(r.t, r.b):
                    I("pool", "memset", [], [bb], tt[:], 1.0)

            def rope(xv, bx, x1, x2, cosv, sinv, nh):
                r, br = rt.next()
                rv = r[:].rearrange("p (a h f) -> p a h f", a=4, h=8)
                t1, t2, t3, t4 = (rv[:, a, 0:nh, :] for a in range(4))
                I("pool", "tensor_tensor", [bx, b_cs], [br], out=t1, in0=x1, in1=cosv, op=ALU.mult)
                I("pool", "tensor_tensor", [bx, b_cs], [br], out=t2, in0=x2, in1=sinv, op=ALU.mult)
                I("pool", "tensor_tensor", [bx, b_cs], [br], out=t3, in0=x1, in1=sinv, op=ALU.mult)
                I("pool", "tensor_tensor", [bx, b_cs], [br], out=t4, in0=x2, in1=cosv, op=ALU.mult)
                I("pool", "tensor_tensor", [br], [bx], out=x1, in0=t1, in1=t2, op=ALU.subtract)
                I("pool", "tensor_tensor", [br], [bx], out=x2, in0=t3, in1=t4, op=ALU.add)

            def head_norm(xf, bxf, nh, hd, dstb, bdst):
                s2, bs2 = sqt.next()
                n = nh * hd
                I("pool", "tensor_tensor", [bxf], [bs2], out=s2[:, :n], in0=xf[:, :n], in1=xf[:, :n], op=ALU.mult)
                st, bst = st8.next()
                I("dve", "tensor_reduce", [bs2], [bst], out=st[:, 0:nh],
                  in_=s2[:, :n].rearrange("p (h d) -> p h d", h=nh), axis=AX.X, op=ALU.add)
                rstd_cols(st[:, 0:nh], bst, hd)
                I("dve", "tensor_tensor", [bxf, bst], [bdst], out=dstb[:, :n].rearrange("p (h d) -> p h d", h=nh),
                  in0=xf[:, :n].rearrange("p (h d) -> p h d", h=nh),
                  in1=st[:, 0:nh].unsqueeze(2).broadcast_to([128, nh, hd]), op=ALU.mult)

            T_ = {}

            def sA(t):
                    seq, kt = divmod(t, 32)
                    pos = slice(kt * 128, (kt + 1) * 128)
                    cosv = cosT[:, kt * 16:(kt + 1) * 16]
                    sinv = sinT[:, kt * 16:(kt + 1) * 16]
                    xti, bxt = xt.next()
                    DMA(xti[:], src[t * 128:(t + 1) * 128, :], [], [bxt])
                    st, bst = st1.next()
                    I("act", "activation", [bxt], [b_junk, bst], out=junk[:], in_=xti[:], func=AF.Square,
                      accum_out=st[:, 0:1])
                    rstd_cols(st[:, 0:1], bst, 1024)
                    xnq, bxn = xn.next()
                    I("dve", "tensor_scalar", [bxt, bst], [bxn], out=xnq[:], in0=xti[:], scalar1=st[:, 0:1], scalar2=None,
                      op0=ALU.mult)
                    pst, bpst = ps_t.next()
                    for k in range(8):
                        I("pe", "transpose", [bxn, b_ident], [bpst], out=pst[:, k * 128:(k + 1) * 128],
                          in_=xnq[:, k * 128:(k + 1) * 128], identity=ident[:])
                    xT, bxT = xnT.next()
                    I("act", "activation", [bpst], [bxT], out=xT[:], in_=pst[:], func=AF.Copy)
                    T_[t] = dict(xT=(xT, bxT))

            def sB(t):
                    seq, kt = divmod(t, 32)
                    pos = slice(kt * 128, (kt + 1) * 128)
                    cosv = cosT[:, kt * 16:(kt + 1) * 16]
                    sinv = sinT[:, kt * 16:(kt + 1) * 16]
                    d = T_[t]
                    xT, bxT = d['xT']
                    pg = []
                    for (c0, c1) in GROUPS_A:
                        p, bp = pm.next()
                        for k in range(8):
                            I("pe", "matmul", [bxT, b_w], [bp], p[:, 0:c1 - c0], lhsT=xT[:, k * 128:(k + 1) * 128],
                              rhs=win[:, k * IN_COLS + c0:k * IN_COLS + c1], start=(k == 0), stop=(k == 7))
                        pg.append((p, bp))
                    cAi, bcA = cA.next()
                    I("act", "activation", [pg[0][1]], [bcA], out=cAi[:], in_=pg[0][0][:, 0:416], func=AF.Copy)
                    bfs = []
                    for which in range(2):
                        p, bp = pg[1 + which]
                        bfi, bbf = bf.next()
                        I("act", "activation", [bp], [bbf], out=bfi[:], in_=p[:], func=AF.Copy)
                        bfs.append((bfi, bbf))
                    vbi, bvb = vbr.next()
                    vb3 = vbi[:].rearrange("p (h c) -> p h c", h=8)
                    I("act", "activation", [pg[3][1]], [bvb], out=vb3[:, :, 0:64],
                      in_=pg[3][0][:].rearrange("p (h d) -> p h d", h=8), func=AF.Copy)
                    sa, bsa = st1.next()
                    sb_, bsb = st1.next()
                    I("act", "activation", [bcA], [b_junk, bsa], out=junk[:, 0:256], in_=cAi[:, 0:256], func=AF.Square,
                      accum_out=sa[:, 0:1])
                    I("act", "activation", [bcA], [b_junk, bsb], out=junk[:, 0:128], in_=cAi[:, 256:384], func=AF.Square,
                      accum_out=sb_[:, 0:1])
                    rstd_cols(sa[:, 0:1], bsa, 256)
                    rstd_cols(sb_[:, 0:1], bsb, 128)
                    cni, bcn = cn.next()
                    I("dve", "tensor_scalar", [bcA, bsa], [bcn], out=cni[:, 0:256], in0=cAi[:, 0:256], scalar1=sa[:, 0:1],
                      scalar2=None, op0=ALU.mult)
                    I("dve", "tensor_scalar", [bcA, bsb], [bcn], out=cni[:, 256:384], in0=cAi[:, 256:384],
                      scalar1=sb_[:, 0:1], scalar2=None, op0=ALU.mult)
                    pst, bpst = ps_t.next()
                    for k in range(3):
                        I("pe", "transpose", [bcn, b_ident], [bpst], out=pst[:, k * 128:(k + 1) * 128],
                          in_=cni[:, k * 128:(k + 1) * 128], identity=ident[:])
                    cT, bcT = cnT.next()
                    I("act", "activation", [bpst], [bcT], out=cT[:], in_=pst[:, 0:384], func=AF.Copy)
                    pq0, bpq0 = pm.next()
                    pq1, bpq1 = pm.next()
                    for k in range(2):
                        I("pe", "matmul", [bcT, b_w], [bpq0], pq0[:], lhsT=cT[:, k * 128:(k + 1) * 128],
                          rhs=wuq[:, k * 768:k * 768 + 512], start=(k == 0), stop=(k == 1))
                    for k in range(2):
                        I("pe", "matmul", [bcT, b_w], [bpq1], pq1[:, 0:256], lhsT=cT[:, k * 128:(k + 1) * 128],
                          rhs=wuq[:, k * 768 + 512:k * 768 + 768], start=(k == 0), stop=(k == 1))
                    pk = []
                    for c in range(2):
                        p, bp = pm.next()
                        I("pe", "matmul", [bcT, b_w], [bp], p[:], lhsT=cT[:, 256:384], rhs=wukv[:, c * 512:(c + 1) * 512],
                          start=True, stop=True)
                        pk.append((p, bp))
                    qfi, bqf = qf.next()
                    I("act", "activation", [bpq0], [bqf], out=qfi[:, 0:512], in_=pq0[:], func=AF.Copy)
                    I("act", "activation", [bpq1], [bqf], out=qfi[:, 512:768], in_=pq1[:, 0:256], func=AF.Copy)
                    kfi, bkf = kf.next()
                    k3 = kfi[:].rearrange("p (h d) -> p h d", h=8)
                    for c in range(2):
                        I("act", "activation", [pk[c][1]], [bkf], out=k3[:, c * 4:(c + 1) * 4, 0:64],
                          in_=pk[c][0][:].rearrange("p (h d) -> p h d", h=4)[:, :, 0:64], func=AF.Copy)
                    vmi, bvm = vmr.next()
                    v3 = vmi[:].rearrange("p (h c) -> p h c", h=8)
                    for c in range(2):
                        I("act", "activation", [pk[c][1]], [bvm], out=v3[:, c * 4:(c + 1) * 4, 0:64],
                          in_=pk[c][0][:].rearrange("p (h d) -> p h d", h=4)[:, :, 64:128], func=AF.Copy)
                    DMA(Vm[seq].rearrange("h p (k c) -> p h k c", c=65)[:, :, kt, :], v3, [bvm], [], q="act")
                    DMA(Vb[seq].rearrange("h p (k c) -> p h k c", c=65)[:, :, kt, :], vb3, [bvb], [], q="act")
                    d.update(cA=(cAi, bcA), bfs=bfs, qf=(qfi, bqf), kf=(kfi, bkf))

            def sC(t):
                    seq, kt = divmod(t, 32)
                    pos = slice(kt * 128, (kt + 1) * 128)
                    cosv = cosT[:, kt * 16:(kt + 1) * 16]
                    sinv = sinT[:, kt * 16:(kt + 1) * 16]
                    d = T_[t]
                    cAi, bcA = d['cA']
                    qfi, bqf = d['qf']
                    kfi, bkf = d['kf']
                    k3 = kfi[:].rearrange('p (h d) -> p h d', h=8)
                    q3 = qfi[:].rearrange("p (h d) -> p h d", h=8)
                    rope(qfi, bqf, q3[:, :, 64:80], q3[:, :, 80:96], cosv.unsqueeze(1).broadcast_to([128, 8, 16]),
                         sinv.unsqueeze(1).broadcast_to([128, 8, 16]), 8)
                    qnb, bqnb = nb.next()
                    head_norm(qfi, bqf, 8, 96, qnb, bqnb)
                    kr, bkr = krt.next()
                    I("pool", "tensor_copy", [bcA], [bkr], out=kr[:], in_=cAi[:, 384:416])
                    rope(kr, bkr, kr[:, 0:16].unsqueeze(1), kr[:, 16:32].unsqueeze(1), cosv.unsqueeze(1), sinv.unsqueeze(1), 1)
                    I("pool", "tensor_copy", [bkr], [bkf], out=k3[:, :, 64:96],
                      in_=kr[:].unsqueeze(1).broadcast_to([128, 8, 32]))
                    knb, bknb = nb.next()
                    head_norm(kfi, bkf, 8, 96, knb, bknb)
                    nbs = []
                    for which in range(2):
                        bfi, bbf = d['bfs'][which]
                        nbi, bnb = nb.next()
                        head_norm(bfi, bbf, 8, 64, nbi, bnb)
                        nbs.append((nbi, bnb))
                    d.update(qnb=(qnb, bqnb), knb=(knb, bknb), nbs=nbs)

            def sD(t):
                    seq, kt = divmod(t, 32)
                    pos = slice(kt * 128, (kt + 1) * 128)
                    cosv = cosT[:, kt * 16:(kt + 1) * 16]
                    sinv = sinT[:, kt * 16:(kt + 1) * 16]
                    d = T_.pop(t)
                    qnb, bqnb = d['qnb']
                    knb, bknb = d['knb']
                    pst, bpst = ps_t.next()
                    for hh in range(8):
                        I("pe", "transpose", [bqnb, b_ident], [bpst], out=pst[0:96, hh * 128:(hh + 1) * 128],
                          in_=qnb[:, hh * 96:(hh + 1) * 96], identity=ident[:])
                    Tq, bTq = TT.next()
                    I("act", "activation", [bpst, b_sq], [bTq], out=Tq[0:96, :], in_=pst[0:96, :], func=AF.Copy,
                      scale=sq[0:96, 0:1])
                    DMA(QTm[seq].rearrange("h d s -> d h s")[:, :, pos], Tq[0:96, :].rearrange("p (h s) -> p h s", h=8),
                        [bTq], [], q="act")
                    pst, bpst = ps_t.next()
                    for hh in range(8):
                        I("pe", "transpose", [bknb, b_ident], [bpst], out=pst[0:96, hh * 128:(hh + 1) * 128],
                          in_=knb[:, hh * 96:(hh + 1) * 96], identity=ident[:])
                    Tk, bTk = TT.next()
                    I("act", "activation", [bpst], [bTk], out=Tk[0:96, :], in_=pst[0:96, :], func=AF.Copy)
                    DMA(KTm[seq].rearrange("h d s -> d h s")[:, :, pos], Tk[0:96, :].rearrange("p (h s) -> p h s", h=8),
                        [bTk], [], q="act")
                    for which in range(2):
                        nbi, bnb = d['nbs'][which]
                        pst, bpst = ps_t.next()
                        for pr in range(4):
                            I("pe", "transpose", [bnb, b_ident], [bpst], out=pst[:, pr * 128:(pr + 1) * 128],
                              in_=nbi[:, pr * 128:(pr + 1) * 128], identity=ident[:])
                        Tb, bTb = TT.next()
                        if which == 0:
                            I("act", "activation", [bpst, b_sq], [bTb], out=Tb[:, 0:512], in_=pst[:, 0:512], func=AF.Copy,
                              scale=sq[:, 1:2])
                        else:
                            I("act", "activation", [bpst], [bTb], out=Tb[:, 0:512], in_=pst[:, 0:512], func=AF.Copy)
                        dstT = QTb if which == 0 else KTb
                        DMA(dstT[seq].rearrange("r p s -> p r s")[:, :, pos], Tb[:, 0:512].rearrange("p (r s) -> p r s", r=4),
                            [bTb], [], q="act")

            skew([sA, sB, sC, sD], NT)
            phase_end()

        def attn_phase_b():
            phase_begin()
            QT = Rot("QT", 3, [128, SEQ], BF16)
            KT = Rot("KT", 3, [128, SEQ], BF16)
            VV = Rot("VV", 3, [128, 32 * 65], BF16)
            pss = Rot("pss", 3, [128, 512], F32, psum=True)
            pso = Rot("pso", 3, [128, 512], F32, psum=True)
            psg = Rot("psg", 1, [128, 64], F32, psum=True)
            pT = Rot("pT", 4, [128, 512], BF16)
            osb = Rot("osb", 2, [128, 260], F32)
            rec = Rot("rec", 2, [128, 4], F32)
            obf = Rot("obf", 2, [128, 256], BF16)
            km = Rot("km", 3, [128, 16], F32)
            kmb = Rot("kmb", 3, [128, 16], BF16)
            gs = Rot("gs", 2, [128, 64], F32)
            mx = Rot("mx", 2, [128, 32], F32)
            sel = Rot("sel", 2, [128, 64], F32)
            acc = Rot("acc", 2, [128, 260], F32)
            tmpc = Rot("tmpc", 3, [128, 260], F32)

            def finalize(src_ap, bsrc, seq, qb, col0):
                r, br = rec.next()
                s3 = src_ap.rearrange("p (s c) -> p s c", s=4)
                I("dve", "reciprocal", [bsrc], [br], out=r[:].unsqueeze(2), in_=s3[:, :, 64:65])
                o, bo = obf.next()
                I("dve", "tensor_tensor", [bsrc, br], [bo], out=o[:].rearrange("p (s c) -> p s c", s=4),
                  in0=s3[:, :, 0:64], in1=r[:].unsqueeze(2).broadcast_to([128, 4, 64]), op=ALU.mult)
                r0 = seq * SEQ + qb * 512
                DMA(Od[r0:r0 + 512, col0:col0 + 64].rearrange("(s p) c -> p s c", p=128),
                    o[:].rearrange("p (s c) -> p s c", s=4), [bo], [])

            def qk_exp(KTi, bKT, QTi, bQT, dk, kt, qb):
                p, bp = pss.next()
                I("pe", "matmul", [bKT, bQT], [bp], p[:], lhsT=KTi[0:dk, kt * 128:(kt + 1) * 128],
                  rhs=QTi[0:dk, qb * 512:(qb + 1) * 512], start=True, stop=True)
                e, be = pT.next()
                I("act", "activation", [bp], [be], out=e[:], in_=p[:], func=AF.Exp)
                if kt >= 4 * qb:
                    s = kt - 4 * qb
                    I("pool", "tensor_tensor", [be, b_tri], [be], out=e[:, s * 128:(s + 1) * 128],
                      in0=e[:, s * 128:(s + 1) * 128], in1=tri[:], op=ALU.mult)
                return e, be

            steps = []
            units = []
            for seq in range(nseq):
                for h in range(8):
                    units.append((seq, "mla", h))
                for h in range(8):
                    units.append((seq, "moba", h))
            loaded = {}

            def load_unit(u):
                seq, kind, h = units[u]
                QTi, bQT = QT.next()
                KTi, bKT = KT.next()
                Vi, bV = VV.next()
                if kind == "mla":
                    DMA(QTi[0:96, :], QTm[seq, h], [], [bQT])
                    DMA(KTi[0:96, :], KTm[seq, h], [], [bKT])
                    DMA(Vi[:], Vm[seq, h], [], [bV])
                else:
                    pr, half = divmod(h, 2)
                    DMA(QTi[0:64, :], QTb[seq, pr, half * 64:(half + 1) * 64, :], [], [bQT])
                    DMA(KTi[0:64, :], KTb[seq, pr, half * 64:(half + 1) * 64, :], [], [bKT])
                    DMA(Vi[:], Vb[seq, h], [], [bV])
                loaded[u] = (QTi, bQT, KTi, bKT, Vi, bV)

            def mk_mla(u, qb):
                seq, kind, h = units[u]
                state = {}

                def pre():
                    if qb == 0:
                        if u == 0:
                            load_unit(0)
                        if u + 1 < len(units):
                            load_unit(u + 1)
                    state["po"] = pso.next()

                def qk(kt):
                    def f():
                        if kt == 0:
                            pre()
                        QTi, bQT, KTi, bKT, Vi, bV = loaded[u]
                        state[kt] = qk_exp(KTi, bKT, QTi, bQT, 96, kt, qb)
                    return f

                def pv(kt):
                    def f():
                        QTi, bQT, KTi, bKT, Vi, bV = loaded[u]
                        po, bpo = state["po"]
                        e, be = state.pop(kt)
                        for s in range(4):
                            if kt <= 4 * qb + s:
                                I("pe", "matmul", [be, bV], [bpo], po[:, s * 65:(s + 1) * 65],
                                  lhsT=e[:, s * 128:(s + 1) * 128], rhs=Vi[:, kt * 65:(kt + 1) * 65],
                                  start=(kt == 0 and s == 0), stop=(kt == 4 * qb + 3 and s == 3),
                                  skip_group_check=True)
                        if kt == 4 * qb + 3:
                            o, bo = osb.next()
                            I("act", "activation", [bpo], [bo], out=o[:], in_=po[:, 0:260], func=AF.Copy)
                            finalize(o[:], bo, seq, qb, h * 64)
                    return f

                for kt in range(4 * qb + 4):
                    steps.append((qk(kt), pv(kt)))

            def mk_moba(u, qb):
                seq, kind, h = units[u]
                state = {}

                def pre():
                    if qb == 0:
                        if u == 0:
                            load_unit(0)
                        if u + 1 < len(units):
                            load_unit(u + 1)
                    QTi, bQT, KTi, bKT, Vi, bV = loaded[u]
                    if qb == 0:
                        kmi, bkm = km.next()
                        I("dve", "tensor_reduce", [bKT], [bkm], out=kmi[0:64, :],
                          in_=KTi[0:64, :].rearrange("p (n l) -> p n l", l=256), axis=AX.X, op=ALU.add)
                        kbi, bkb = kmb.next()
                        I("dve", "tensor_scalar", [bkm], [bkb], out=kbi[0:64, :], in0=kmi[0:64, :],
                          scalar1=1.0 / 256, scalar2=None, op0=ALU.mult)
                        loaded[("kmb", u)] = (kbi, bkb)
                    kbi, bkb = loaded[("kmb", u)]
                    pg_, bpg = psg.next()
                    for s in range(4):
                        I("pe", "matmul", [bQT, bkb], [bpg], pg_[:, s * 16:(s + 1) * 16],
                          lhsT=QTi[0:64, (4 * qb + s) * 128:(4 * qb + s + 1) * 128], rhs=kbi[0:64, :],
                          start=True, stop=True)
                    sl, bsl = sel.next()
                    curs = [2 * qb + s // 2 for s in range(4)]
                    if curs[3] <= 3:
                        I("pool", "memset", [], [bsl], sl[:], 1.0)
                    else:
                        g, bg = gs.next()
                        I("pool", "memset", [], [bg], g[:], NEG)
                        for s0 in (0, 2):
                            cur = curs[s0]
                            I("act", "activation", [bpg], [bg],
                              out=g[:].rearrange("p (s n) -> p s n", s=4)[:, s0:s0 + 2, 0:cur],
                              in_=pg_[:].rearrange("p (s n) -> p s n", s=4)[:, s0:s0 + 2, 0:cur], func=AF.Copy)
                        m8, bm8 = mx.next()
                        for s in range(4):
                            I("dve", "max", [bg], [bm8], out=m8[:, s * 8:(s + 1) * 8], in_=g[:, s * 16:(s + 1) * 16])
                        I("dve", "tensor_tensor", [bg, bm8], [bsl], out=sl[:].rearrange("p (s n) -> p s n", s=4),
                          in0=g[:].rearrange("p (s n) -> p s n", s=4),
                          in1=m8[:].rearrange("p (s e) -> p s e", s=4)[:, :, 2:3].broadcast_to([128, 4, 16]),
                          op=ALU.is_ge)
                        for s in range(4):
                            if curs[s] <= 3:
                                I("pool", "memset", [], [bsl], sl[:, s * 16:s * 16 + curs[s]], 1.0)
                            I("pool", "memset", [], [bsl], sl[:, s * 16 + curs[s]:s * 16 + curs[s] + 1], 1.0)
                    state["sl"] = (sl, bsl)
                    ac, bac = acc.next()
                    I("pool", "memset", [], [bac], ac[:], 0.0)
                    state["acc"] = (ac, bac)

                def qk(kt):
                    def f():
                        if kt == 0:
                            pre()
                        QTi, bQT, KTi, bKT, Vi, bV = loaded[u]
                        if kt % 2 == 0:
                            state[("po", kt // 2)] = pso.next()
                        state[kt] = qk_exp(KTi, bKT, QTi, bQT, 64, kt, qb)
                    return f

                def pv(kt):
                    def f():
                        QTi, bQT, KTi, bKT, Vi, bV = loaded[u]
                        n = kt // 2
                        po, bpo = state[("po", n)]
                        e, be = state.pop(kt)
                        for s in range(4):
                            cur = 2 * qb + s // 2
                            qt = 4 * qb + s
                            if n < cur or (n == cur and kt <= qt):
                                first = (kt == 2 * n and s == (0 if n <= 2 * qb else 2))
                                I("pe", "matmul", [be, bV], [bpo], po[:, s * 65:(s + 1) * 65],
                                  lhsT=e[:, s * 128:(s + 1) * 128], rhs=Vi[:, kt * 65:(kt + 1) * 65],
                                  start=first, stop=False, skip_group_check=True)
                        if kt == 2 * n + 1:
                            sl, bsl = state["sl"]
                            ac, bac = state["acc"]
                            s0 = 0 if n <= 2 * qb else 2
                            tm, btm = tmpc.next()
                            cs = slice(s0 * 65, 260)
                            ns = 4 - s0
                            I("dve", "tensor_tensor", [bpo, bsl], [btm],
                              out=tm[:, cs].rearrange("p (s c) -> p s c", s=ns),
                              in0=po[:, cs].rearrange("p (s c) -> p s c", s=ns),
                              in1=sl[:].rearrange("p (s n) -> p s n", s=4)[:, s0:4, n:n + 1].broadcast_to([128, ns, 65]),
                              op=ALU.mult)
                            I("pool", "tensor_tensor", [btm, bac], [bac], out=ac[:, cs], in0=ac[:, cs], in1=tm[:, cs],
                              op=ALU.add)
                            state.pop(("po", n))
                            if n == 2 * qb + 1:
                                finalize(ac[:], bac, seq, qb, 512 + h * 64)
                    return f

                for kt in range(4 * qb + 4):
                    steps.append((qk(kt), pv(kt)))

            for u, (seq, kind, h) in enumerate(units):
                for qb in range(8):
                    (mk_mla if kind == "mla" else mk_moba)(u, qb)
            LOOK = 2
            for idx in range(len(steps) + LOOK):
                if idx < len(steps):
                    steps[idx][0]()
                if idx - LOOK >= 0:
                    steps[idx - LOOK][1]()
            phase_end()

        def attn_phase_c(i, src, dst):
            phase_begin()
            wo = sb("wo", [128, 8 * 1024], BF16)
            b_wo = Buf()
            DMA(wo[:], woc[i], [], [b_wo])
            xt = Rot("xt", 4, [128, 1024], F32)
            ot = Rot("ot", 3, [128, 1024], BF16)
            oT = Rot("oT", 3, [128, 1024], BF16)
            ps_t = Rot("ps_t", 2, [128, 1024], BF16, psum=True)
            ps_y = Rot("ps_y", 4, [128, 512], F32, psum=True)
            T_ = {}

            def sA(t):
                rows = slice(t * 128, (t + 1) * 128)
                xti, bxt = xt.next()
                oti, bot = ot.next()
                DMA(xti[:], src[rows, :], [], [bxt])
                DMA(oti[:], Od[rows, :], [], [bot])
                pst, bpst = ps_t.next()
                for k in range(8):
                    I("pe", "transpose", [bot, b_ident], [bpst], out=pst[:, k * 128:(k + 1) * 128],
                      in_=oti[:, k * 128:(k + 1) * 128], identity=ident[:])
                oTi, boT = oT.next()
                I("act", "activation", [bpst], [boT], out=oTi[:], in_=pst[:], func=AF.Copy)
                T_[t] = (xti, bxt, oTi, boT)

            def sB(t):
                rows = slice(t * 128, (t + 1) * 128)
                xti, bxt, oTi, boT = T_.pop(t)
                for c in range(2):
                    py, bpy = ps_y.next()
                    for k in range(8):
                        I("pe", "matmul", [boT, b_wo], [bpy], py[:], lhsT=oTi[:, k * 128:(k + 1) * 128],
                          rhs=wo[:, k * 1024 + c * 512:k * 1024 + (c + 1) * 512], start=(k == 0), stop=(k == 7))
                    I("dve", "tensor_tensor", [bpy, bxt], [bxt], out=xti[:, c * 512:(c + 1) * 512], in0=py[:],
                      in1=xti[:, c * 512:(c + 1) * 512], op=ALU.add)
                DMA(dst[rows, :], xti[:], [bxt], [], q=cfg.get("stq", "pool"))

            skew([sA, sB], NT)
            phase_end()

        NLEV = 9
        PI = float(np.pi)

        def s5_prep(i, L):
            apw = sb("apw", [128, NLEV * 2 * 32], F32)
            CPr = sb("CPr", [128, 32 * 32], BF16)
            CPi = sb("CPi", [128, 32 * 32], BF16)
            negl = sb("negl", [128, NLEV * 32], F32)
            b_p = Buf()
            phase_begin()
            lr = sb("lr", [128, 32], F32)
            li = sb("li", [128, 32], F32)
            dt = sb("dt", [128, 32], F32)
            bt = Buf()
            for two in range(2):
                ps_ = slice(two * 64, (two + 1) * 64)
                DMA(lr[ps_, :], lam_re[i].rearrange("(gp two) p -> two p gp", two=2)[two], [], [bt],
                    allow_slow_non_contiguous=True)
                DMA(li[ps_, :], lam_im[i].rearrange("(gp two) p -> two p gp", two=2)[two], [], [bt],
                    allow_slow_non_contiguous=True)
                DMA(dt[ps_, :], log_dt[i].rearrange("(gp two) -> two gp", two=2)[two:two + 1, :].broadcast_to([64, 32]),
                    [], [bt], allow_slow_non_contiguous=True)
            tl = [sb("t%d" % k, [128, 32], F32) for k in range(12)]
            er, ang, kk, sn, cs, ca, m2, t0, t1, wr, wi, den = tl

            def V(name, out_, **kw):
                I("dve", name, [bt], [bt], out=out_, **kw)

            I("act", "activation", [bt], [bt], out=dt[:], in_=dt[:], func=AF.Exp)
            V("tensor_tensor", er[:], in0=lr[:], in1=dt[:], op=ALU.mult)
            I("act", "activation", [bt], [bt], out=er[:], in_=er[:], func=AF.Exp)
            V("tensor_tensor", ang[:], in0=li[:], in1=dt[:], op=ALU.mult)
            I("pool", "memset", [bt], [bt], kk[:], 0.0)
            for m in range(1, 5):
                V("scalar_tensor_tensor", kk[:], in0=ang[:], scalar=(2 * m - 1) * PI, in1=kk[:], op0=ALU.is_gt,
                  op1=ALU.add)
            V("scalar_tensor_tensor", ang[:], in0=kk[:], scalar=-2 * PI, in1=ang[:], op0=ALU.mult, op1=ALU.add)
            I("act", "activation", [bt], [bt], out=sn[:], in_=ang[:], func=AF.Sin)
            V("tensor_scalar", ca[:], in0=ang[:], scalar1=PI / 2, scalar2=None, op0=ALU.add)
            V("tensor_scalar", m2[:], in0=ca[:], scalar1=PI, scalar2=None, op0=ALU.is_gt)
            V("scalar_tensor_tensor", ca[:], in0=m2[:], scalar=-2 * PI, in1=ca[:], op0=ALU.mult, op1=ALU.add)
            I("act", "activation", [bt], [bt], out=cs[:], in_=ca[:], func=AF.Sin)

            def lev(k, c):
                return apw[:, (k * 2 + c) * 32:(k * 2 + c + 1) * 32]

            I("dve", "tensor_tensor", [bt], [b_p], out=lev(0, 0), in0=er[:], in1=cs[:], op=ALU.mult)
            I("dve", "tensor_tensor", [bt], [b_p], out=lev(0, 1), in0=er[:], in1=sn[:], op=ALU.mult)
            for k in range(1, NLEV):
                pr_, pi_ = lev(k - 1, 0), lev(k - 1, 1)
                I("dve", "tensor_tensor", [b_p], [bt], out=t0[:], in0=pr_, in1=pr_, op=ALU.mult)
                I("dve", "tensor_tensor", [b_p], [bt], out=t1[:], in0=pi_, in1=pi_, op=ALU.mult)
                I("dve", "tensor_tensor", [bt], [b_p], out=lev(k, 0), in0=t0[:], in1=t1[:], op=ALU.subtract)
                I("dve", "scalar_tensor_tensor", [b_p], [b_p], out=lev(k, 1), in0=pr_, scalar=2.0, in1=pi_,
                  op0=ALU.mult, op1=ALU.mult)
            I("dve", "tensor_scalar", [b_p], [b_p], out=negl[:].rearrange("p (k g) -> p k g", k=NLEV),
              in0=apw[:].rearrange("p (k c g) -> p k c g", k=NLEV, c=2)[:, :, 1, :], scalar1=-1.0, scalar2=None,
              op0=ALU.mult)
            ar, ai = lev(0, 0), lev(0, 1)
            I("dve", "tensor_tensor", [bt], [bt], out=t0[:], in0=lr[:], in1=lr[:], op=ALU.mult)
            I("dve", "tensor_tensor", [bt], [bt], out=t1[:], in0=li[:], in1=li[:], op=ALU.mult)
            I("dve", "tensor_tensor", [bt], [bt], out=den[:], in0=t0[:], in1=t1[:], op=ALU.add)
            I("dve", "reciprocal", [bt], [bt], out=den[:], in_=den[:])
            I("dve", "tensor_scalar", [b_p], [bt], out=t0[:], in0=ar, scalar1=-1.0, scalar2=None, op0=ALU.add)
            I("dve", "tensor_tensor", [bt], [bt], out=wr[:], in0=t0[:], in1=lr[:], op=ALU.mult)
            I("dve", "tensor_tensor", [bt, b_p], [bt], out=t1[:], in0=ai, in1=li[:], op=ALU.mult)
            I("dve", "tensor_tensor", [bt], [bt], out=wr[:], in0=wr[:], in1=t1[:], op=ALU.add)
            I("dve", "tensor_tensor", [bt], [bt], out=wr[:], in0=wr[:], in1=den[:], op=ALU.mult)
            I("dve", "tensor_tensor", [bt, b_p], [bt], out=wi[:], in0=ai, in1=lr[:], op=ALU.mult)
            I("dve", "tensor_tensor", [bt], [bt], out=t1[:], in0=t0[:], in1=li[:], op=ALU.mult)
            I("dve", "tensor_tensor", [bt], [bt], out=wi[:], in0=wi[:], in1=t1[:], op=ALU.subtract)
            I("dve", "tensor_tensor", [bt], [bt], out=wi[:], in0=wi[:], in1=den[:], op=ALU.mult)
            Br = sb("Br", [128, 512], F32)
            Bi = sb("Bi", [128, 512], F32)
            gm = sb("gm", [128, 512], F32)
            bb_ = Buf()
            for two in range(2):
                ps_ = slice(two * 64, (two + 1) * 64)
                for gp in range(32):
                    g = 2 * gp + two
                    DMA(Br[ps_, gp * 16:(gp + 1) * 16], b_re[i, g], [], [bb_])
                    DMA(Bi[ps_, gp * 16:(gp + 1) * 16], b_im[i, g], [], [bb_])
                DMA(gm[ps_, :].rearrange("p (gp i) -> p gp i", i=16),
                    mix_g[L].rearrange("(gp two i) -> two gp i", two=2, i=16)[two:two + 1].broadcast_to([64, 32, 16]),
                    [], [bb_], allow_slow_non_contiguous=True)
            Bbr = sb("Bbr", [128, 512], F32)
            Bbi = sb("Bbi", [128, 512], F32)
            tb = sb("tb", [128, 512], F32)
            Bb16 = sb("Bb16", [128, 2 * 512], BF16)
            wrb = wr[:].unsqueeze(2).broadcast_to([128, 32, 16])
            wib = wi[:].unsqueeze(2).broadcast_to([128, 32, 16])

            def v3(t_):
                return t_[:].rearrange("p (gp i) -> p gp i", i=16)

            I("dve", "tensor_tensor", [bt, bb_], [bb_], out=v3(Bbr), in0=v3(Br), in1=wrb, op=ALU.mult)
            I("dve", "tensor_tensor", [bt, bb_], [bb_], out=v3(tb), in0=v3(Bi), in1=wib, op=ALU.mult)
            I("dve", "tensor_tensor", [bb_], [bb_], out=Bbr[:], in0=Bbr[:], in1=tb[:], op=ALU.subtract)
            I("dve", "tensor_tensor", [bt, bb_], [bb_], out=v3(Bbi), in0=v3(Bi), in1=wrb, op=ALU.mult)
            I("dve", "tensor_tensor", [bt, bb_], [bb_], out=v3(tb), in0=v3(Br), in1=wib, op=ALU.mult)
            I("dve", "tensor_tensor", [bb_], [bb_], out=Bbi[:], in0=Bbi[:], in1=tb[:], op=ALU.add)
            I("dve", "tensor_tensor", [bb_], [bb_], out=Bb16[:, 0:512], in0=Bbr[:], in1=gm[:], op=ALU.mult)
            I("dve", "tensor_tensor", [bb_], [bb_], out=Bb16[:, 512:1024], in0=Bbi[:], in1=gm[:], op=ALU.mult)
            zt = sb("zt", [128, 128], BF16)
            bz = Buf()
            I("pool", "memset", [], [bz], zt[:], 0.0)
            bD = Buf()
            for gp in range(32):
                for c in range(2):
                    DMA(BTd[i, gp, c], zt[:], [bz], [bD])
            pT_ = Rot("pTp", 2, [128, 1024], BF16, psum=True)
            TB = Rot("TB", 2, [16, 256], BF16)
            for gp in range(32):
                p_, bp_ = pT_.next()
                for c in range(2):
                    I("pe", "transpose", [bb_, b_ident], [bp_], out=p_[0:16, c * 128:(c + 1) * 128],
                      in_=Bb16[:, c * 512 + gp * 16:c * 512 + (gp + 1) * 16], identity=ident[:])
                tbb, btb = TB.next()
                I("act", "activation", [bp_], [btb], out=tbb[:], in_=p_[0:16, 0:256], func=AF.Copy)
                for two in range(2):
                    g = 2 * gp + two
                    r0 = (g % 8) * 16
                    for c in range(2):
                        DMA(BTd[i, gp, c, r0:r0 + 16, two * 64:(two + 1) * 64],
                            tbb[:, c * 128 + two * 64:c * 128 + (two + 1) * 64], [btb, bD], [])
            Cr = sb("Cr", [128, 512], F32)
            Ci = sb("Ci", [128, 512], F32)
            bc_ = Buf()
            for two in range(2):
                ps_ = slice(two * 64, (two + 1) * 64)
                for gp in range(32):
                    g = 2 * gp + two
                    DMA(Cr[ps_, gp * 16:(gp + 1) * 16], c_re[i, g].rearrange("o p -> p o"), [], [bc_],
                        allow_slow_non_contiguous=True)
                    DMA(Ci[ps_, gp * 16:(gp + 1) * 16], c_im[i, g].rearrange("o p -> p o"), [], [bc_],
                        allow_slow_non_contiguous=True)
            I("pool", "memset", [], [b_p], CPr[:], 0.0)
            I("pool", "memset", [], [b_p], CPi[:], 0.0)
            for two in range(2):
                ps_ = slice(two * 64, (two + 1) * 64)
                I("dve", "tensor_copy", [bc_], [b_p],
                  out=CPr[ps_, :].rearrange("p (gp t o) -> p gp t o", t=2, o=16)[:, :, two, :],
                  in_=Cr[ps_, :].rearrange("p (gp o) -> p gp o", o=16))
                I("dve", "tensor_scalar", [bc_], [b_p],
                  out=CPi[ps_, :].rearrange("p (gp t o) -> p gp t o", t=2, o=16)[:, :, two, :],
                  in0=Ci[ps_, :].rearrange("p (gp o) -> p gp o", o=16), scalar1=-1.0, scalar2=None, op0=ALU.mult)
            phase_end()
            return apw, CPr, CPi, b_p, negl

        def s5_pre(src):
            phase_begin()
            xt = Rot("xt", 2, [128, 1024], F32)
            xn = Rot("xn", 2, [128, 1024], BF16)
            uT = Rot("uT", 2, [128, 8 * 512], BF16)
            junk = sb("junk", [128, 1024], F32)
            b_junk = Buf()
            st1 = Rot("st1", 4, [128, 1], F32)
            ps_t = Rot("ps_t", 2, [128, 1024], BF16, psum=True)
            for t5 in range(NT5):
                uTi, buT = uT.next()
                for s_ in range(4):
                    t = t5 * 4 + s_
                    xti, bxt = xt.next()
                    DMA(xti[:], src[t * 128:(t + 1) * 128, :], [], [bxt])
                    st, bst = st1.next()
                    I("act", "activation", [bxt], [b_junk, bst], out=junk[:], in_=xti[:], func=AF.Square,
                      accum_out=st[:, 0:1])
                    rstd_cols(st[:, 0:1], bst, 1024)
                    xnq, bxn = xn.next()
                    I("dve", "tensor_scalar", [bxt, bst], [bxn], out=xnq[:], in0=xti[:], scalar1=st[:, 0:1],
                      scalar2=None, op0=ALU.mult)
                    pst, bpst = ps_t.next()
                    for k in range(8):
                        I("pe", "transpose", [bxn, b_ident], [bpst], out=pst[:, k * 128:(k + 1) * 128],
                          in_=xnq[:, k * 128:(k + 1) * 128], identity=ident[:])
                    I("act", "activation", [bpst], [buT],
                      out=uTi[:].rearrange("p (k t) -> p k t", k=8)[:, :, s_ * 128:(s_ + 1) * 128],
                      in_=pst[:].rearrange("p (k t) -> p k t", k=8), func=AF.Copy)
                DMA(uTd[:, :, t5 * 512:(t5 + 1) * 512].rearrange("k p t -> p k t"),
                    uTi[:].rearrange("p (k t) -> p k t", k=8), [buT], [])
            phase_end()

        def s5_main(i, apw, CPr, CPi, b_p, sb_negl):
            phase_begin()
            PFX = 256
            W = PFX + 512
            uT = Rot("uTf", 2, [128, 512], BF16)
            BT = Rot("BTs", 2, [128, 4 * 2 * 128], BF16)
            bufs = [[sb("sc%d%d" % (a, c), [128, W], F32) for c in range(2)] for a in range(2)]
            bsc = [Buf(), Buf()]
            for a in range(2):
                for c in range(2):
                    I("pool", "memset", [], [bsc[a]], bufs[a][c][:], 0.0)
            carry = sb("carry", [128, 32 * nseq * 2], F32)
            b_car = Buf()
            I("pool", "memset", [], [b_car], carry[:], 0.0)
            HSb = Rot("HSb", 2, [128, 2 * 512], BF16)
            psb = Rot("psb", 4, [128, 512], F32, psum=True)
            psy = Rot("psy", 2, [128, 512], F32, psum=True)
            ysb = Rot("ysb", 2, [128, 512], F32)

            def lev(k, c, gp):
                return apw[:, (k * 2 + c) * 32 + gp:(k * 2 + c) * 32 + gp + 1]

            for fc in range(8):
                BTi, bBT = BT.next()
                DMA(BTi[:].rearrange("p (a c q) -> p a c q", a=4, c=2),
                    BTd[i, fc * 4:(fc + 1) * 4].rearrange("a c p q -> p a c q"), [], [bBT])
                for seq in range(nseq):
                    for blk in range(8):
                        t0_ = seq * SEQ + blk * 512
                        uTi, buT = uT.next()
                        DMA(uTi[:], uTd[fc, :, t0_:t0_ + 512], [], [buT])
                        py, bpy = psy.next()
                        firsty = True
                        for pl in range(4):
                            gp = fc * 4 + pl
                            pbr, bpbr = psb.next()
                            pbi, bpbi = psb.next()
                            for c, (pb_, bpb_) in enumerate(((pbr, bpbr), (pbi, bpbi))):
                                I("pe", "matmul", [bBT, buT], [bpb_], pb_[:],
                                  lhsT=BTi[:, (pl * 2 + c) * 128:(pl * 2 + c + 1) * 128], rhs=uTi[:],
                                  start=True, stop=True)
                            A, Bf = bufs[0], bufs[1]
                            I("act", "activation", [bpbr], [bsc[0]], out=A[0][:, PFX:W], in_=pbr[:], func=AF.Copy)
                            I("act", "activation", [bpbi], [bsc[0]], out=A[1][:, PFX:W], in_=pbi[:], func=AF.Copy)
                            cc = (gp * nseq + seq) * 2
                            cr, ci = carry[:, cc:cc + 1], carry[:, cc + 1:cc + 2]
                            ar, ai = lev(0, 0, gp), lev(0, 1, gp)
                            nai = sb_negl[:, gp:gp + 1]
                            f = slice(PFX, PFX + 1)
                            rd = [bsc[0], b_car, b_p]
                            I("dve", "scalar_tensor_tensor", rd, [bsc[0]], out=A[0][:, f], in0=cr, scalar=ar,
                              in1=A[0][:, f], op0=ALU.mult, op1=ALU.add)
                            I("dve", "scalar_tensor_tensor", rd, [bsc[0]], out=A[0][:, f], in0=ci, scalar=nai,
                              in1=A[0][:, f], op0=ALU.mult, op1=ALU.add)
                            I("dve", "scalar_tensor_tensor", rd, [bsc[0]], out=A[1][:, f], in0=ci, scalar=ar,
                              in1=A[1][:, f], op0=ALU.mult, op1=ALU.add)
                            I("dve", "scalar_tensor_tensor", rd, [bsc[0]], out=A[1][:, f], in0=cr, scalar=ai,
                              in1=A[1][:, f], op0=ALU.mult, op1=ALU.add)
                            src_i = 0
                            for k in range(NLEV):
                                sh = 1 << k
                                X, Y = bufs[src_i], bufs[1 - src_i]
                                bx, by = bsc[src_i], bsc[1 - src_i]
                                kr_, ki_ = lev(k, 0, gp), lev(k, 1, gp)
                                nki = sb_negl[:, k * 32 + gp:k * 32 + gp + 1]
                                cur_ = slice(PFX, W)
                                shf = slice(PFX - sh, W - sh)
                                I("dve", "scalar_tensor_tensor", [bx, b_p], [by], out=Y[0][:, cur_], in0=X[0][:, shf],
                                  scalar=kr_, in1=X[0][:, cur_], op0=ALU.mult, op1=ALU.add)
                                I("dve", "scalar_tensor_tensor", [bx, b_p, by], [by], out=Y[0][:, cur_],
                                  in0=X[1][:, shf], scalar=nki, in1=Y[0][:, cur_], op0=ALU.mult, op1=ALU.add)
                                I("dve", "scalar_tensor_tensor", [bx, b_p], [by], out=Y[1][:, cur_], in0=X[1][:, shf],
                                  scalar=kr_, in1=X[1][:, cur_], op0=ALU.mult, op1=ALU.add)
                                I("dve", "scalar_tensor_tensor", [bx, b_p, by], [by], out=Y[1][:, cur_],
                                  in0=X[0][:, shf], scalar=ki_, in1=Y[1][:, cur_], op0=ALU.mult, op1=ALU.add)
                                src_i = 1 - src_i
                            R_, bR = bufs[src_i], bsc[src_i]
                            I("pool", "tensor_copy", [bR], [b_car], out=carry[:, cc:cc + 1], in_=R_[0][:, W - 1:W])
                            I("pool", "tensor_copy", [bR], [b_car], out=carry[:, cc + 1:cc + 2], in_=R_[1][:, W - 1:W])
                            hs, bhs = HSb.next()
                            I("act", "activation", [bR], [bhs], out=hs[:, 0:512], in_=R_[0][:, PFX:W], func=AF.Copy)
                            I("pool", "tensor_copy", [bR], [bhs], out=hs[:, 512:1024], in_=R_[1][:, PFX:W])
                            for s_ in range(4):
                                for c, CP in enumerate((CPr, CPi)):
                                    I("pe", "matmul", [bhs, b_p], [bpy],
                                      py[:, s_ * 128 + pl * 32:s_ * 128 + (pl + 1) * 32],
                                      lhsT=hs[:, c * 512 + s_ * 128:c * 512 + (s_ + 1) * 128],
                                      rhs=CP[:, gp * 32:(gp + 1) * 32], start=firsty, stop=False,
                                      skip_group_check=True)
                                    firsty = False
                        yo, byo = ysb.next()
                        I("act", "activation", [bpy], [byo], out=yo[:], in_=py[:], func=AF.Copy)
                        DMA(yd[t0_:t0_ + 512, fc * 128:(fc + 1) * 128].rearrange("(s p) c -> p s c", p=128),
                            yo[:].rearrange("p (s c) -> p s c", s=4), [byo], [])
            phase_end()

        def s5_post(i, L, src, dst):
            phase_begin()
            wg = sb("wg", [128, 8 * 2048], BF16)
            b_wg = Buf()
            for k in range(4):
                DMA(wg[:, k * 4096:(k + 1) * 4096], wgluc[i][:, k * 4096:(k + 1) * 4096], [], [b_wg])
            dp = sb("dp", [128, 1024], F32)
            gmb = sb("gmb", [128, 1024], F32)
            b_dp = Buf()
            DMA(dp[:], d_skip[i:i + 1, :].broadcast_to([128, 1024]), [], [b_dp], allow_slow_non_contiguous=True)
            DMA(gmb[:], mix_g[L:L + 1, :].broadcast_to([128, 1024]), [], [b_dp], allow_slow_non_contiguous=True)
            I("dve", "tensor_tensor", [b_dp], [b_dp], out=dp[:], in0=dp[:], in1=gmb[:], op=ALU.mult)
            xt = Rot("xt", 6, [128, 1024], F32)
            yt = Rot("yt", 4, [128, 1024], F32)
            u_ = Rot("u", 3, [128, 1024], F32)
            w_ = Rot("w", 3, [128, 1024], F32)
            gb = Rot("gb", 3, [128, 1024], BF16)
            gT = Rot("gT", 3, [128, 1024], BF16)
            sg = Rot("sg", 2, [128, 1024], F32)
            junk = sb("junk", [128, 1024], F32)
            b_junk = Buf()
            st1 = Rot("st1", 6, [128, 1], F32)
            ps_t = Rot("ps_t", 2, [128, 1024], BF16, psum=True)
            pm = Rot("pm", 6, [128, 512], F32, psum=True)
            T_ = {}

            def sA(t):
                rows = slice(t * 128, (t + 1) * 128)
                xti, bxt = xt.next()
                yti, byt = yt.next()
                DMA(xti[:], src[rows, :], [], [bxt])
                DMA(yti[:], yd[rows, :], [], [byt])
                st, bst = st1.next()
                I("act", "activation", [bxt], [b_junk, bst], out=junk[:], in_=xti[:], func=AF.Square,
                  accum_out=st[:, 0:1])
                rstd_cols(st[:, 0:1], bst, 1024)
                ui, bu = u_.next()
                I("dve", "scalar_tensor_tensor", [bxt, bst, b_dp], [bu], out=ui[:], in0=xti[:], scalar=st[:, 0:1],
                  in1=dp[:], op0=ALU.mult, op1=ALU.mult)
                T_[t] = dict(x=(xti, bxt), y=(yti, byt), u=(ui, bu))

            def sB(t):
                d = T_[t]
                yti, byt = d["y"]
                ui, bu = d["u"]
                I("pool", "tensor_tensor", [bu, byt], [byt], out=yti[:], in0=yti[:], in1=ui[:], op=ALU.add)
                wi_, bw = w_.next()
                I("act", "activation", [byt], [bw], out=wi_[:], in_=yti[:], func=AF.Square)
                I("dve", "tensor_scalar", [bw], [bw], out=wi_[:], in0=wi_[:], scalar1=0.044715, scalar2=1.0,
                  op0=ALU.mult, op1=ALU.add)
                I("pool", "tensor_tensor", [bw, byt], [bw], out=wi_[:], in0=wi_[:], in1=yti[:], op=ALU.mult)
                I("act", "activation", [bw], [bw], out=wi_[:], in_=wi_[:], func=AF.Sigmoid, scale=1.5957691216057308)
                gbi, bgb = gb.next()
                I("dve", "tensor_tensor", [bw, byt], [bgb], out=gbi[:], in0=wi_[:], in1=yti[:], op=ALU.mult)
                d["g"] = (gbi, bgb)

            def sC(t):
                d = T_[t]
                gbi, bgb = d["g"]
                pst, bpst = ps_t.next()
                for k in range(8):
                    I("pe", "transpose", [bgb, b_ident], [bpst], out=pst[:, k * 128:(k + 1) * 128],
                      in_=gbi[:, k * 128:(k + 1) * 128], identity=ident[:])
                gTi, bgT = gT.next()
                I("act", "activation", [bpst], [bgT], out=gTi[:], in_=pst[:], func=AF.Copy)
                pvd = {}
                for c in (2, 3, 0, 1):
                    p_, bp_ = pm.next()
                    for k in range(8):
                        I("pe", "matmul", [bgT, b_wg], [bp_], p_[:], lhsT=gTi[:, k * 128:(k + 1) * 128],
                          rhs=wg[:, k * 2048 + c * 512:k * 2048 + (c + 1) * 512], start=(k == 0), stop=(k == 7))
                    pvd[c] = (p_, bp_)
                d["pv"] = pvd

            def sD(t):
                rows = slice(t * 128, (t + 1) * 128)
                d = T_.pop(t)
                pvd = d["pv"]
                xti, bxt = d["x"]
                sgi, bsg = sg.next()
                for c in range(2):
                    cs_ = slice(c * 512, (c + 1) * 512)
                    I("act", "activation", [pvd[2 + c][1]], [bsg], out=sgi[:, cs_], in_=pvd[2 + c][0][:], func=AF.Sigmoid)
                    I("dve", "tensor_tensor", [pvd[c][1], bsg], [bsg], out=sgi[:, cs_], in0=pvd[c][0][:],
                      in1=sgi[:, cs_], op=ALU.mult)
                    I("pool", "tensor_tensor", [bsg, bxt], [bxt], out=xti[:, cs_], in0=sgi[:, cs_], in1=xti[:, cs_],
                      op=ALU.add)
                DMA(dst[rows, :], xti[:], [bxt], [], q="pool")

            skew([sA, sB, sC, sD], NT)
            phase_end()

        def s5_prep2(i, L):
            apw = sb("apw", [128, NLEV * 2 * 32], F32)
            negl = sb("negl", [128, NLEV * 32], F32)
            CPLr = sb("CPLr", [128, 8 * 1024], BF16)
            CPLi = sb("CPLi", [128, 8 * 1024], BF16)
            b_p = Buf()
            phase_begin()
            lr = sb("lr", [128, 32], F32)
            li = sb("li", [128, 32], F32)
            dt = sb("dt", [128, 32], F32)
            bt = Buf()
            for two in range(2):
                ps_ = slice(two * 64, (two + 1) * 64)
                DMA(lr[ps_, :], lam_re[i].rearrange("(gp two) p -> two p gp", two=2)[two], [], [bt],
                    allow_slow_non_contiguous=True)
                DMA(li[ps_, :], lam_im[i].rearrange("(gp two) p -> two p gp", two=2)[two], [], [bt],
                    allow_slow_non_contiguous=True)
                DMA(dt[ps_, :], log_dt[i].rearrange("(gp two) -> two gp", two=2)[two:two + 1, :].broadcast_to([64, 32]),
                    [], [bt], allow_slow_non_contiguous=True)
            tl = [sb("t%d" % k, [128, 32], F32) for k in range(12)]
            er, ang, kk, sn, cs, ca, m2, t0, t1, wr, wi, den = tl
            APD = sb("APD", [128, 9 * 2 * 32], F32)

            def apd(d, c):
                return APD[:, (d * 2 + c) * 32:(d * 2 + c + 1) * 32]

            def V(name, out_, **kw):
                I("dve", name, [bt], [bt], out=out_, **kw)

            I("act", "activation", [bt], [bt], out=dt[:], in_=dt[:], func=AF.Exp)
            V("tensor_tensor", er[:], in0=lr[:], in1=dt[:], op=ALU.mult)
            I("act", "activation", [bt], [bt], out=er[:], in_=er[:], func=AF.Exp)
            V("tensor_tensor", ang[:], in0=li[:], in1=dt[:], op=ALU.mult)
            I("pool", "memset", [bt], [bt], kk[:], 0.0)
            for m in range(1, 5):
                V("scalar_tensor_tensor", kk[:], in0=ang[:], scalar=(2 * m - 1) * PI, in1=kk[:], op0=ALU.is_gt,
                  op1=ALU.add)
            V("scalar_tensor_tensor", ang[:], in0=kk[:], scalar=-2 * PI, in1=ang[:], op0=ALU.mult, op1=ALU.add)
            I("act", "activation", [bt], [bt], out=sn[:], in_=ang[:], func=AF.Sin)
            V("tensor_scalar", ca[:], in0=ang[:], scalar1=PI / 2, scalar2=None, op0=ALU.add)
            V("tensor_scalar", m2[:], in0=ca[:], scalar1=PI, scalar2=None, op0=ALU.is_gt)
            V("scalar_tensor_tensor", ca[:], in0=m2[:], scalar=-2 * PI, in1=ca[:], op0=ALU.mult, op1=ALU.add)
            I("act", "activation", [bt], [bt], out=cs[:], in_=ca[:], func=AF.Sin)
            I("pool", "memset", [bt], [bt], apd(0, 0), 1.0)
            I("pool", "memset", [bt], [bt], apd(0, 1), 0.0)
            V("tensor_tensor", apd(1, 0), in0=er[:], in1=cs[:], op=ALU.mult)
            V("tensor_tensor", apd(1, 1), in0=er[:], in1=sn[:], op=ALU.mult)
            ar, ai = apd(1, 0), apd(1, 1)
            for d in range(2, 9):
                V("tensor_tensor", t0[:], in0=apd(d - 1, 0), in1=ar, op=ALU.mult)
                V("tensor_tensor", t1[:], in0=apd(d - 1, 1), in1=ai, op=ALU.mult)
                V("tensor_tensor", apd(d, 0), in0=t0[:], in1=t1[:], op=ALU.subtract)
                V("tensor_tensor", t0[:], in0=apd(d - 1, 0), in1=ai, op=ALU.mult)
                V("tensor_tensor", t1[:], in0=apd(d - 1, 1), in1=ar, op=ALU.mult)
                V("tensor_tensor", apd(d, 1), in0=t0[:], in1=t1[:], op=ALU.add)

            def lev(k, c):
                return apw[:, (k * 2 + c) * 32:(k * 2 + c + 1) * 32]

            I("dve", "tensor_copy", [bt], [b_p], out=lev(0, 0), in_=apd(8, 0))
            I("dve", "tensor_copy", [bt], [b_p], out=lev(0, 1), in_=apd(8, 1))
            for k in range(1, NLEV):
                pr_, pi_ = lev(k - 1, 0), lev(k - 1, 1)
                I("dve", "tensor_tensor", [b_p], [bt], out=t0[:], in0=pr_, in1=pr_, op=ALU.mult)
                I("dve", "tensor_tensor", [b_p], [bt], out=t1[:], in0=pi_, in1=pi_, op=ALU.mult)
                I("dve", "tensor_tensor", [bt], [b_p], out=lev(k, 0), in0=t0[:], in1=t1[:], op=ALU.subtract)
                I("dve", "scalar_tensor_tensor", [b_p], [b_p], out=lev(k, 1), in0=pr_, scalar=2.0, in1=pi_,
                  op0=ALU.mult, op1=ALU.mult)
            I("dve", "tensor_scalar", [b_p], [b_p], out=negl[:].rearrange("p (k g) -> p k g", k=NLEV),
              in0=apw[:].rearrange("p (k c g) -> p k c g", k=NLEV, c=2)[:, :, 1, :], scalar1=-1.0, scalar2=None,
              op0=ALU.mult)
            V("tensor_tensor", t0[:], in0=lr[:], in1=lr[:], op=ALU.mult)
            V("tensor_tensor", t1[:], in0=li[:], in1=li[:], op=ALU.mult)
            V("tensor_tensor", den[:], in0=t0[:], in1=t1[:], op=ALU.add)
            V("reciprocal", den[:], in_=den[:])
            V("tensor_scalar", t0[:], in0=ar, scalar1=-1.0, scalar2=None, op0=ALU.add)
            V("tensor_tensor", wr[:], in0=t0[:], in1=lr[:], op=ALU.mult)
            V("tensor_tensor", t1[:], in0=ai, in1=li[:], op=ALU.mult)
            V("tensor_tensor", wr[:], in0=wr[:], in1=t1[:], op=ALU.add)
            V("tensor_tensor", wr[:], in0=wr[:], in1=den[:], op=ALU.mult)
            V("tensor_tensor", wi[:], in0=ai, in1=lr[:], op=ALU.mult)
            V("tensor_tensor", t1[:], in0=t0[:], in1=li[:], op=ALU.mult)
            V("tensor_tensor", wi[:], in0=wi[:], in1=t1[:], op=ALU.subtract)
            V("tensor_tensor", wi[:], in0=wi[:], in1=den[:], op=ALU.mult)
            Br = sb("Br", [128, 512], F32)
            Bi = sb("Bi", [128, 512], F32)
            gm = sb("gm", [128, 512], F32)
            bb_ = Buf()
            for two in range(2):
                ps_ = slice(two * 64, (two + 1) * 64)
                for gp in range(32):
                    g = 2 * gp + two
                    DMA(Br[ps_, gp * 16:(gp + 1) * 16], b_re[i, g], [], [bb_])
                    DMA(Bi[ps_, gp * 16:(gp + 1) * 16], b_im[i, g], [], [bb_])
                DMA(gm[ps_, :].rearrange("p (gp i) -> p gp i", i=16),
                    mix_g[L].rearrange("(gp two i) -> two gp i", two=2, i=16)[two:two + 1].broadcast_to([64, 32, 16]),
                    [], [bb_], allow_slow_non_contiguous=True)
            Bd = [[sb("Bd%d%d" % (a_, c), [128, 512], F32) for c in range(2)] for a_ in range(2)]
            tb = sb("tb", [128, 512], F32)
            tb2 = sb("tb2", [128, 512], F32)
            BbA = sb("BbA", [128, 8 * 2 * 1024], BF16)
            bA = Buf()
            I("pool", "memset", [], [bA], BbA[:], 0.0)

            def v3(t_):
                return t_[:].rearrange("p (gp i) -> p gp i", i=16)

            def bc(ap32):
                return ap32.unsqueeze(2).broadcast_to([128, 32, 16])

            rd = [bt, bb_]
            I("dve", "tensor_tensor", rd, [bb_], out=v3(tb), in0=v3(Br), in1=bc(wr[:]), op=ALU.mult)
            I("dve", "tensor_tensor", rd, [bb_], out=v3(tb2), in0=v3(Bi), in1=bc(wi[:]), op=ALU.mult)
            I("dve", "tensor_tensor", [bb_], [bb_], out=tb[:], in0=tb[:], in1=tb2[:], op=ALU.subtract)
            I("dve", "tensor_tensor", [bb_], [bb_], out=Bd[0][0][:], in0=tb[:], in1=gm[:], op=ALU.mult)
            I("dve", "tensor_tensor", rd, [bb_], out=v3(tb), in0=v3(Bi), in1=bc(wr[:]), op=ALU.mult)
            I("dve", "tensor_tensor", rd, [bb_], out=v3(tb2), in0=v3(Br), in1=bc(wi[:]), op=ALU.mult)
            I("dve", "tensor_tensor", [bb_], [bb_], out=tb[:], in0=tb[:], in1=tb2[:], op=ALU.add)
            I("dve", "tensor_tensor", [bb_], [bb_], out=Bd[0][1][:], in0=tb[:], in1=gm[:], op=ALU.mult)
            BA5 = BbA[:].rearrange("p (d c gp t i) -> p d c gp t i", d=8, c=2, gp=32, t=2)
            for d in range(8):
                X = Bd[d % 2]
                Y = Bd[1 - d % 2]
                for c in range(2):
                    for two in range(2):
                        ps_ = slice(two * 64, (two + 1) * 64)
                        I("pool" if c else "act", "tensor_copy" if c else "copy", [bb_], [bA],
                          out=BbA[ps_, :].rearrange("p (d c gp t i) -> p d c gp t i", d=8, c=2, gp=32, t=2)[:, d, c, :, two, :],
                          in_=X[c][ps_, :].rearrange("p (gp i) -> p gp i", i=16))
                if d < 7:
                    I("dve", "tensor_tensor", rd, [bb_], out=v3(tb), in0=v3(X[0]), in1=bc(ar), op=ALU.mult)
                    I("dve", "tensor_tensor", rd, [bb_], out=v3(tb2), in0=v3(X[1]), in1=bc(ai), op=ALU.mult)
                    I("dve", "tensor_tensor", [bb_], [bb_], out=Y[0][:], in0=tb[:], in1=tb2[:], op=ALU.subtract)
                    I("dve", "tensor_tensor", rd, [bb_], out=v3(tb), in0=v3(X[0]), in1=bc(ai), op=ALU.mult)
                    I("dve", "tensor_tensor", rd, [bb_], out=v3(tb2), in0=v3(X[1]), in1=bc(ar), op=ALU.mult)
                    I("dve", "tensor_tensor", [bb_], [bb_], out=Y[1][:], in0=tb[:], in1=tb2[:], op=ALU.add)
            pT_ = Rot("pTp", 2, [128, 1024], BF16, psum=True)
            TB = Rot("TB", 3, [32, 1024], BF16)
            for gp in range(32):
                for c in range(2):
                    p_, bp_ = pT_.next()
                    for d in range(8):
                        off = ((d * 2 + c) * 32 + gp) * 32
                        I("pe", "transpose", [bA, b_ident], [bp_], out=p_[0:32, d * 128:(d + 1) * 128],
                          in_=BbA[:, off:off + 32], identity=ident[:])
                    tbb, btb = TB.next()
                    I("act", "activation", [bp_], [btb], out=tbb[:], in_=p_[0:32, :], func=AF.Copy)
                    DMA(BTd2[i, gp, :, c].rearrange("d r q -> r d q"), tbb[:].rearrange("r (d q) -> r d q", d=8),
                        [btb], [])
            Cr = sb("Cr", [128, 512], F32)
            Ci = sb("Ci", [128, 512], F32)
            bc_ = Buf()
            for two in range(2):
                ps_ = slice(two * 64, (two + 1) * 64)
                for gp in range(32):
                    g = 2 * gp + two
                    DMA(Cr[ps_, gp * 16:(gp + 1) * 16], c_re[i, g].rearrange("o p -> p o"), [], [bc_],
                        allow_slow_non_contiguous=True)
                    DMA(Ci[ps_, gp * 16:(gp + 1) * 16], c_im[i, g].rearrange("o p -> p o"), [], [bc_],
                        allow_slow_non_contiguous=True)
            CP0r = sb("CP0r", [128, 1024], BF16)
            CP0i = sb("CP0i", [128, 1024], BF16)
            b0 = Buf()
            I("pool", "memset", [], [b0], CP0r[:], 0.0)
            I("pool", "memset", [], [b0], CP0i[:], 0.0)
            I("pool", "memset", [], [b_p], CPLr[:], 0.0)
            I("pool", "memset", [], [b_p], CPLi[:], 0.0)
            for two in range(2):
                ps_ = slice(two * 64, (two + 1) * 64)
                I("dve", "tensor_copy", [bc_], [b0],
                  out=CP0r[ps_, :].rearrange("p (gp t o) -> p gp t o", t=2, o=16)[:, :, two, :],
                  in_=Cr[ps_, :].rearrange("p (gp o) -> p gp o", o=16))
                I("dve", "tensor_scalar", [bc_], [b0],
                  out=CP0i[ps_, :].rearrange("p (gp t o) -> p gp t o", t=2, o=16)[:, :, two, :],
                  in0=Ci[ps_, :].rearrange("p (gp o) -> p gp o", o=16), scalar1=-1.0, scalar2=None, op0=ALU.mult)
            rdc = [bt, bc_, bb_]
            for l in range(8):
                Ar, Ai = apd(l + 1, 0), apd(l + 1, 1)
                I("dve", "tensor_tensor", rdc, [bb_], out=v3(tb), in0=v3(Cr), in1=bc(Ar), op=ALU.mult)
                I("dve", "tensor_tensor", rdc, [bb_], out=v3(tb2), in0=v3(Ci), in1=bc(Ai), op=ALU.mult)
                I("dve", "tensor_tensor", [bb_], [bb_], out=tb[:], in0=tb[:], in1=tb2[:], op=ALU.subtract)
                for two in range(2):
                    ps_ = slice(two * 64, (two + 1) * 64)
                    I("pool", "tensor_copy", [bb_], [b_p],
                      out=CPLr[ps_, :].rearrange("p (l gp t o) -> p l gp t o", l=8, t=2, o=16)[:, l, :, two, :],
                      in_=tb[ps_, :].rearrange("p (gp o) -> p gp o", o=16))
                I("dve", "tensor_tensor", rdc, [bb_], out=v3(tb), in0=v3(Cr), in1=bc(Ai), op=ALU.mult)
                I("dve", "tensor_tensor", rdc, [bb_], out=v3(tb2), in0=v3(Ci), in1=bc(Ar), op=ALU.mult)
                I("dve", "tensor_tensor", [bb_], [bb_], out=tb[:], in0=tb[:], in1=tb2[:], op=ALU.add)
                for two in range(2):
                    ps_ = slice(two * 64, (two + 1) * 64)
                    I("pool", "tensor_scalar", [bb_], [b_p],
                      out=CPLi[ps_, :].rearrange("p (l gp t o) -> p l gp t o", l=8, t=2, o=16)[:, l, :, two, :],
                      in0=tb[ps_, :].rearrange("p (gp o) -> p gp o", o=16), scalar1=-1.0, scalar2=None,
                      op0=ALU.mult)
            pMa = Rot("pMa", 2, [128, 512], F32, psum=True)
            pMb = Rot("pMb", 2, [128, 512], F32, psum=True)
            Ml = Rot("Mlp", 2, [128, 1024], BF16)
            for fc in range(8):
                pa, bpa = pMa.next()
                pb, bpb = pMb.next()
                I("dve", "memset", [], [bpa], pa[:], 0.0)
                I("dve", "memset", [], [bpb], pb[:], 0.0)
                for pl in range(4):
                    gp = fc * 4 + pl
                    for d in range(8):
                        pk, bpk = (pa, bpa) if d < 4 else (pb, bpb)
                        for c in range(2):
                            off = ((d * 2 + c) * 32 + gp) * 32
                            CP0 = CP0r if c == 0 else CP0i
                            I("pe", "matmul", [bA, b0], [bpk],
                              pk[pl * 32:(pl + 1) * 32, (d % 4) * 128 + pl * 32:(d % 4) * 128 + (pl + 1) * 32],
                              lhsT=BbA[:, off:off + 32], rhs=CP0[:, gp * 32:(gp + 1) * 32],
                              start=(d % 4 == 0 and c == 0), stop=False, skip_group_check=True,
                              tile_position=(0, 32 * pl))
                mli, bml = Ml.next()
                I("act", "activation", [bpa], [bml], out=mli[:, 0:512], in_=pa[:], func=AF.Copy)
                I("act", "activation", [bpb], [bml], out=mli[:, 512:1024], in_=pb[:], func=AF.Copy)
                DMA(Mlagd[i, fc], mli[:], [bml], [])
            phase_end()
            return apw, negl, CPLr, CPLi, b_p

        def s5_pre2(src):
            phase_begin()
            xt = Rot("xt", 3, [128, 1024], F32)
            xn = Rot("xn", 3, [128, 1024], BF16)
            uT = Rot("uT", 3, [128, 8 * 8 * 64], BF16)
            junk = sb("junk", [128, 1024], F32)
            b_junk = Buf()
            st1 = Rot("st1", 6, [128, 1], F32)
            ps_t = Rot("ps_t", 3, [128, 1024], BF16, psum=True)
            T_ = {}
            U_ = {}

            def sA(t):
                xti, bxt = xt.next()
                DMA(xti[:], src[t * 128:(t + 1) * 128, :], [], [bxt])
                st, bst = st1.next()
                I("act", "activation", [bxt], [b_junk, bst], out=junk[:], in_=xti[:], func=AF.Square,
                  accum_out=st[:, 0:1])
                rstd_cols(st[:, 0:1], bst, 1024)
                xnq, bxn = xn.next()
                I("dve", "tensor_scalar", [bxt, bst], [bxn], out=xnq[:], in0=xti[:], scalar1=st[:, 0:1],
                  scalar2=None, op0=ALU.mult)
                T_[t] = (xnq, bxn)

            def sB(t):
                t5, s_ = divmod(t, 4)
                seq, blk = divmod(t5, 8)
                if s_ == 0:
                    U_[t5] = uT.next()
                uTi, buT = U_[t5]
                u4 = uTi[:].rearrange("p (k j n) -> p k j n", k=8, j=8)
                xnq, bxn = T_.pop(t)
                pst, bpst = ps_t.next()
                for k in range(8):
                    I("pe", "transpose", [bxn, b_ident], [bpst], out=pst[:, k * 128:(k + 1) * 128],
                      in_=xnq[:, k * 128:(k + 1) * 128], identity=ident[:])
                for j in range(8):
                    eng = "act" if j % 2 == 0 else "dve"
                    if eng == "act":
                        I("act", "activation", [bpst], [buT], out=u4[:, :, j, s_ * 16:(s_ + 1) * 16],
                          in_=pst[:].rearrange("p (k n j) -> p k n j", k=8, j=8)[:, :, :, j], func=AF.Copy)
                    else:
                        I("dve", "tensor_copy", [bpst], [buT], out=u4[:, :, j, s_ * 16:(s_ + 1) * 16],
                          in_=pst[:].rearrange("p (k n j) -> p k n j", k=8, j=8)[:, :, :, j])
                if s_ == 3:
                    for k in range(8):
                        DMA(uTd2[k, :, seq, :, blk * 64:(blk + 1) * 64], u4[:, k], [buT], [], q="pool")
                    U_.pop(t5)

            skew([sA, sB], NT)
            phase_end()

        def s5_main2(i, apw, sb_negl, CPLr, CPLi, b_p):
            phase_begin()
            PFX = 256
            W = PFX + 512
            uT = Rot("uTf", 2, [128, 8 * 512], BF16)
            BT = Rot("BTs", 2, [128, 8 * 2 * 128], BF16)
            Ml = Rot("Mls", 2, [128, 1024], BF16)
            bufs = [[sb("sc%d%d" % (a, c), [128, W], F32) for c in range(2)] for a in range(2)]
            bsc = [Buf(), Buf()]
            for a in range(2):
                for c in range(2):
                    I("pool", "memset", [], [bsc[a]], bufs[a][c][:], 0.0)
            Zb = Rot("Zb", 2, [128, 4 * 2 * 512], BF16)
            psb = Rot("psb", 4, [128, 512], F32, psum=True)
            psya = Rot("psya", 2, [128, 512], F32, psum=True)
            psyb = Rot("psyb", 2, [128, 512], F32, psum=True)
            ysb = Rot("ysb", 2, [128, 1024], F32)

            def lev(k, c, gp):
                return apw[:, (k * 2 + c) * 32 + gp:(k * 2 + c) * 32 + gp + 1]

            for fc in range(8):
                BTi, bBT = BT.next()
                for pl in range(4):
                    DMA(BTi[pl * 32:(pl + 1) * 32, :].rearrange("p (d c q) -> p d c q", d=8, c=2),
                        BTd2[i, fc * 4 + pl].rearrange("d c r q -> r d c q"), [], [bBT])
                Mli, bMl = Ml.next()
                DMA(Mli[:], Mlagd[i, fc], [], [bMl])
                for seq in range(nseq):
                    uTi, buT = uT.next()
                    DMA(uTi[:], uTd2[fc, :, seq].rearrange("p j n -> p (j n)"), [], [buT])
                    Zi, bZ = Zb.next()
                    for pl in range(4):
                        gp = fc * 4 + pl
                        rows = slice(pl * 32, (pl + 1) * 32)
                        pcs = []
                        for c in range(2):
                            pb_, bpb_ = psb.next()
                            for j in range(8):
                                d = 7 - j
                                I("pe", "matmul", [bBT, buT], [bpb_], pb_[:],
                                  lhsT=BTi[rows, (d * 2 + c) * 128:(d * 2 + c + 1) * 128],
                                  rhs=uTi[rows, j * 512:(j + 1) * 512], start=(j == 0), stop=(j == 7),
                                  tile_position=(32 * pl, 0))
                            pcs.append((pb_, bpb_))
                        A = bufs[0]
                        I("act", "activation", [pcs[0][1]], [bsc[0]], out=A[0][:, PFX:W], in_=pcs[0][0][:], func=AF.Copy)
                        I("act", "activation", [pcs[1][1]], [bsc[0]], out=A[1][:, PFX:W], in_=pcs[1][0][:], func=AF.Copy)
                        src_i = 0
                        for k in range(NLEV):
                            sh = 1 << k
                            X, Y = bufs[src_i], bufs[1 - src_i]
                            bx, by = bsc[src_i], bsc[1 - src_i]
                            kr_, ki_ = lev(k, 0, gp), lev(k, 1, gp)
                            nki = sb_negl[:, k * 32 + gp:k * 32 + gp + 1]
                            cur_ = slice(PFX, W)
                            shf = slice(PFX - sh, W - sh)
                            I("dve", "scalar_tensor_tensor", [bx, b_p], [by], out=Y[0][:, cur_], in0=X[0][:, shf],
                              scalar=kr_, in1=X[0][:, cur_], op0=ALU.mult, op1=ALU.add)
                            I("dve", "scalar_tensor_tensor", [bx, b_p, by], [by], out=Y[0][:, cur_],
                              in0=X[1][:, shf], scalar=nki, in1=Y[0][:, cur_], op0=ALU.mult, op1=ALU.add)
                            I("dve", "scalar_tensor_tensor", [bx, b_p], [by], out=Y[1][:, cur_], in0=X[1][:, shf],
                              scalar=kr_, in1=X[1][:, cur_], op0=ALU.mult, op1=ALU.add)
                            I("dve", "scalar_tensor_tensor", [bx, b_p, by], [by], out=Y[1][:, cur_],
                              in0=X[0][:, shf], scalar=ki_, in1=Y[1][:, cur_], op0=ALU.mult, op1=ALU.add)
                            src_i = 1 - src_i
                        R_, bR = bufs[src_i], bsc[src_i]
                        I("act", "activation", [bR], [bZ], out=Zi[:, (pl * 2) * 512:(pl * 2 + 1) * 512],
                          in_=R_[0][:, PFX - 1:W - 1], func=AF.Copy)
                        I("pool", "tensor_copy", [bR], [bZ], out=Zi[:, (pl * 2 + 1) * 512:(pl * 2 + 2) * 512],
                          in_=R_[1][:, PFX - 1:W - 1])
                    for nb in range(4):
                        pa, bpa = psya.next()
                        pb, bpb = psyb.next()
                        ncol = slice(nb * 128, (nb + 1) * 128)
                        for j in range(8):
                            lhs = uTi[:, j * 512 + nb * 128:j * 512 + (nb + 1) * 128]
                            if j <= 3:
                                I("pe", "matmul", [buT, bMl], [bpa], pa[:, j * 128:512], lhsT=lhs,
                                  rhs=Mli[:, 0:(4 - j) * 128], start=(j == 0), stop=False, skip_group_check=True)
                                I("pe", "matmul", [buT, bMl], [bpb], pb[:, 0:512], lhsT=lhs,
                                  rhs=Mli[:, (4 - j) * 128:(8 - j) * 128], start=(j == 0), stop=False,
                                  skip_group_check=True)
                            else:
                                I("pe", "matmul", [buT, bMl], [bpb], pb[:, (j - 4) * 128:512], lhsT=lhs,
                                  rhs=Mli[:, 0:(8 - j) * 128], start=False, stop=False, skip_group_check=True)
                        for pl in range(4):
                            gp = fc * 4 + pl
                            for l in range(8):
                                pk, bpk = (pa, bpa) if l < 4 else (pb, bpb)
                                for c, CPL in enumerate((CPLr, CPLi)):
                                    I("pe", "matmul", [bZ, b_p], [bpk],
                                      pk[:, (l % 4) * 128 + pl * 32:(l % 4) * 128 + (pl + 1) * 32],
                                      lhsT=Zi[:, (pl * 2 + c) * 512 + nb * 128:(pl * 2 + c) * 512 + (nb + 1) * 128],
                                      rhs=CPL[:, (l * 32 + gp) * 32:(l * 32 + gp + 1) * 32], start=False, stop=False,
                                      skip_group_check=True)
                        yo, byo = ysb.next()
                        I("act", "activation", [bpa], [byo], out=yo[:, 0:512], in_=pa[:], func=AF.Copy)
                        I("act", "activation", [bpb], [byo], out=yo[:, 512:1024], in_=pb[:], func=AF.Copy)
                        r0 = seq * SEQ + nb * 1024
                        DMA(yd[r0:r0 + 1024, fc * 128:(fc + 1) * 128].rearrange("(n l) c -> n l c", l=8),
                            yo[:].rearrange("p (l c) -> p l c", l=8), [byo], [], q="act")
            phase_end()

        prep_phase()
        cur = x_in
        for li, L in enumerate(layers):
            last = (li == len(layers) - 1)
            if do_mix:
                mdst = xm if do_ffn or not last else out
                if L % 2 == 0:
                    ph = cfg.get("ph", "abc")
                    if "a" in ph:
                        attn_phase_a(L // 2, L, cur)
                    if "b" in ph:
                        attn_phase_b()
                    if "c" in ph:
                        attn_phase_c(L // 2, cur, mdst)
                    cur = mdst
                else:
                    i = L // 2
                    if cfg.get("s5v", 2) == 1:
                        apw, CPr, CPi, b_p, negl = s5_prep(i, L)
                        s5_pre(cur)
                        s5_main(i, apw, CPr, CPi, b_p, negl)
                    else:
                        sph = cfg.get("s5ph", "prmo")
                        phase_begin()
                        apw, negl, CPLr, CPLi, b_p = s5_prep2(i, L)
                        if "r" in sph:
                            s5_pre2(cur)
                        if "m" in sph:
                            s5_main2(i, apw, negl, CPLr, CPLi, b_p)
                        phase_end()
                    if "o" in cfg.get("s5ph", "prmo"):
                        s5_post(i, L, cur, mdst)
                    cur = mdst
            if do_ffn:
                dst = out if last else xa
                ffn_half(L, 0, cur, xh)
                ffn_half(L, 1, xh, dst)
                cur = dst
        with nc.Block() as block:
            S.emit(nc, block)
    return nc


def _consts():
    ident = np.eye(128, dtype=np.float32).astype(ml_dtypes.bfloat16)
    tri = (np.arange(128)[:, None] <= np.arange(128)[None, :]).astype(np.float32).astype(ml_dtypes.bfloat16)
    half = 16
    inv = (np.float32(10000.0) ** (-np.arange(half, dtype=np.float32) / np.float32(half))).astype(np.float32)
    ang = (np.arange(SEQ, dtype=np.float32)[:, None] * inv[None, :]).astype(np.float32)
    return {"ident": ident, "trimask": tri, "rope_cos": np.cos(ang).astype(np.float32),
            "rope_sin": np.sin(ang).astype(np.float32)}


PARAM_KEYS = ["mix_norm_g", "ffn_norm_g", "w_in", "g_cq", "w_uq", "g_ckv", "w_ukv", "g_qn_mla", "g_kn_mla",
              "g_qn_moba", "g_kn_moba", "w_o", "lam_re", "lam_im", "log_dt", "b_re", "b_im", "c_re", "c_im",
              "d_skip", "w_glu", "w_ff1", "w_ff2"]


def kernel(**inputs):
    nc = build({})
    consts = _consts()
    x = np.ascontiguousarray(inputs["x"], dtype=np.float32).reshape(NCORES, TOK, D)
    params = {k: np.ascontiguousarray(inputs[k], dtype=np.float32) for k in PARAM_KEYS}
    in_maps = []
    for c in range(NCORES):
        m = {"x": x[c]}
        m.update(params)
        m.update(consts)
        in_maps.append(m)
    res = run_bass_kernel_spmd(nc, in_maps, core_ids=list(range(NCORES)))
    outs = [np.asarray(r["out"]) for r in res.results]
    return np.stack(outs, 0).reshape(16, SEQ, D).astype(np.float32)
```

```python
import numpy as np
import ml_dtypes
import concourse.bass as bass
import concourse.mybir as mybir
from concourse.bass_utils import run_bass_kernel_spmd

F32 = mybir.dt.float32
BF16 = mybir.dt.bfloat16
ALU = mybir.AluOpType
AF = mybir.ActivationFunctionType
AX = mybir.AxisListType

NCORES = 8
D = 1024
SEQ = 4096
NSEQ = 2
TOK = NSEQ * SEQ
DFF = 4096
DEPTH = 4
EPS = 1e-6
IN_COLS = 1952

SAME_ENGINE_SYNC = True
N_DSEM = 40


class Buf:
    __slots__ = ("name", "lastw", "lastr", "dmar")

    def __init__(self, name=""):
        self.name = name
        self.lastw = {}
        self.lastr = {}
        self.dmar = []


class Op:
    __slots__ = ("eng", "fn", "deps", "is_dma", "signal", "sigval", "dsem", "dval", "idx")

    def __init__(self, eng, fn, is_dma):
        self.eng = eng
        self.fn = fn
        self.is_dma = is_dma
        self.deps = []
        self.signal = False
        self.sigval = None
        self.dsem = None
        self.dval = None


ENGS = ["pe", "act", "dve", "pool", "sp"]


class Sched:
    def __init__(self):
        self.ops = {e: [] for e in ENGS}
        self.n_dma = 0
        self.pending = {}
        self.dma_since = []

    def barrier(self):
        deps = [self.ops[e][-1] for e in ENGS if self.ops[e] and not self.ops[e][-1].is_dma]
        for e in ENGS:
            for o in reversed(self.ops[e]):
                if not o.is_dma:
                    if o not in deps:
                        deps.append(o)
                    break
        deps = deps + list(self.dma_since)
        self.dma_since = []
        for e in ENGS:
            self.pending[e] = list(self.pending.get(e, [])) + deps

    def op(self, eng, fn, reads=(), writes=(), dma=False):
        o = Op(eng, fn, dma)
        deps = {}
        for b in reads:
            for w in b.lastw.values():
                deps[id(w)] = w
        for b in writes:
            for w in b.lastw.values():
                deps[id(w)] = w
            for r in b.lastr.values():
                deps[id(r)] = r
            for r in b.dmar:
                deps[id(r)] = r
        if eng in self.pending:
            for d in self.pending.pop(eng):
                deps[id(d)] = d
        if dma:
            self.dma_since.append(o)
        o.deps = list(deps.values())
        for b in reads:
            if dma:
                b.dmar.append(o)
            else:
                b.lastr[eng] = o
        for b in writes:
            b.lastw = {("dma%d" % id(o)) if dma else eng: o}
            b.lastr = {}
            b.dmar = []
        self.ops[eng].append(o)
        return o

    def emit(self, nc, block):
        for e in ENGS:
            for o in self.ops[e]:
                for d in o.deps:
                    if d.is_dma:
                        continue
                    if d.eng == o.eng and not o.is_dma:
                        if d.eng == "pe" or not SAME_ENGINE_SYNC:
                            continue
                    d.signal = True
        dcount = [0] * N_DSEM
        k = 0
        for e in ENGS:
            c = 0
            for o in self.ops[e]:
                if o.is_dma:
                    pass
                elif o.signal:
                    c += 1
                    o.sigval = c
        sems = {e: nc.alloc_semaphore(name="sem_" + e) for e in ["pe", "act", "dve", "pool"]}
        dsems = [nc.alloc_semaphore(name="dsem%d" % i) for i in range(N_DSEM)]
        pools = {"sp": (0, 24), "act": (24, 8), "pool": (32, 8), "dve": (32, 8), "pe": (32, 8)}
        for e in ENGS:
            base, n = pools[e]
            k = 0
            for o in self.ops[e]:
                if o.is_dma:
                    o.dsem = base + k % n
                    dcount[o.dsem] += 16
                    o.dval = dcount[o.dsem]
                    k += 1
        self.final_dma = [(i, dcount[i]) for i in range(N_DSEM) if dcount[i] > 0]

        def run_engine(ename, eng):
            waited = {}

            def wait(key, sem, val):
                if waited.get(key, 0) >= val:
                    return
                eng.wait_ge(sem, val)
                waited[key] = val

            for o in self.ops[ename]:
                for d in o.deps:
                    if d.is_dma:
                        wait(("d", d.dsem), dsems[d.dsem], d.dval)
                    else:
                        if d.eng == ename and not o.is_dma:
                            if ename == "pe" or not SAME_ENGINE_SYNC:
                                continue
                        wait(d.eng, sems[d.eng], d.sigval)
                if o.is_dma:
                    if o.dval > 16:
                        wait(("d", o.dsem), dsems[o.dsem], o.dval - 16)
                    ins = o.fn(eng)
                    ins.then_inc(dsems[o.dsem], 16)
                else:
                    ins = o.fn(eng)
                    if o.signal:
                        ins.then_inc(sems[ename], 1)
            if ename == "sp":
                for i, v in self.final_dma:
                    eng.wait_ge(dsems[i], v)

        @block.tensor
        def _(eng):
            run_engine("pe", eng)

        @block.scalar
        def _(eng):
            run_engine("act", eng)

        @block.vector
        def _(eng):
            run_engine("dve", eng)

        @block.gpsimd
        def _(eng):
            run_engine("pool", eng)

        @block.sync
        def _(eng):
            run_engine("sp", eng)


import contextlib

NEG = -1.0e30
GROUPS_A = [(0, 416), (416, 928), (928, 1440), (1440, 1952)]


def build(cfg):
    nc = bass.Bass("TRN2", target_bir_lowering=False)
    S = Sched()
    nseq = cfg.get("nseq", NSEQ)
    tok = nseq * SEQ
    layers = cfg.get("layers", list(range(DEPTH)))
    do_ffn = cfg.get("ffn", True)
    do_mix = cfg.get("mix", True)
    dbg = cfg.get("dbg", ())
    NT = tok // 128
    NT5 = tok // 512

    def din(name, shape, dt=F32):
        return nc.dram_tensor(name, list(shape), dt, kind="ExternalInput").ap()

    def dscr(name, shape, dt):
        kind = "ExternalOutput" if name in dbg else "Internal"
        return nc.dram_tensor(name, list(shape), dt, kind=kind).ap()

    x_in = din("x", [tok, D])
    mix_g = din("mix_norm_g", [DEPTH, D])
    ffn_g = din("ffn_norm_g", [DEPTH, D])
    w_in = din("w_in", [2, D, IN_COLS])
    g_cq = din("g_cq", [2, 256])
    w_uq = din("w_uq", [2, 256, 768])
    g_ckv = din("g_ckv", [2, 128])
    w_ukv = din("w_ukv", [2, 128, 1024])
    g_qn_mla = din("g_qn_mla", [2, 96])
    g_kn_mla = din("g_kn_mla", [2, 96])
    g_qn_moba = din("g_qn_moba", [2, 64])
    g_kn_moba = din("g_kn_moba", [2, 64])
    w_o = din("w_o", [2, D, D])
    lam_re = din("lam_re", [2, 64, 64])
    lam_im = din("lam_im", [2, 64, 64])
    log_dt = din("log_dt", [2, 64])
    b_re = din("b_re", [2, 64, 64, 16])
    b_im = din("b_im", [2, 64, 64, 16])
    c_re = din("c_re", [2, 64, 16, 64])
    c_im = din("c_im", [2, 64, 16, 64])
    d_skip = din("d_skip", [2, D])
    w_glu = din("w_glu", [2, D, 2 * D])
    w_ff1 = din("w_ff1", [DEPTH, D, DFF])
    w_ff2 = din("w_ff2", [DEPTH, DFF, D])
    ident_in = din("ident", [128, 128], BF16)
    tri_in = din("trimask", [128, 128], BF16)
    cos_in = din("rope_cos", [SEQ, 16])
    sin_in = din("rope_sin", [SEQ, 16])
    out = nc.dram_tensor("out", [tok, D], F32, kind="ExternalOutput").ap()

    xa = dscr("xa", [tok, D], F32)
    xm = dscr("xm", [tok, D], F32)
    xh = dscr("xh", [tok, D], F32)
    hTd = dscr("hTd", [NT5, 128, 8 * 512], BF16)
    w1c = dscr("w1c", [DEPTH, 2, 128, 8 * 2048], BF16)
    w2c = dscr("w2c", [DEPTH, 2, 128, 16 * 1024], BF16)
    winc = dscr("winc", [2, 128, 8 * IN_COLS], BF16)
    wuqc = dscr("wuqc", [2, 128, 2 * 768], BF16)
    wukvc = dscr("wukvc", [2, 128, 1024], BF16)
    woc = dscr("woc", [2, 128, 8 * 1024], BF16)
    QTm = dscr("QTm", [nseq, 8, 96, SEQ], BF16)
    KTm = dscr("KTm", [nseq, 8, 96, SEQ], BF16)
    Vm = dscr("Vm", [nseq, 8, 128, 32 * 65], BF16)
    QTb = dscr("QTb", [nseq, 4, 128, SEQ], BF16)
    KTb = dscr("KTb", [nseq, 4, 128, SEQ], BF16)
    Vb = dscr("Vb", [nseq, 8, 128, 32 * 65], BF16)
    Od = dscr("Od", [tok, D], BF16)
    uTd = dscr("uTd", [8, 128, tok], BF16)
    yd = dscr("yd", [tok, D], F32)
    BTd = dscr("BTd", [2, 32, 2, 128, 128], BF16)
    wgluc = dscr("wgluc", [2, 128, 8 * 2048], BF16)
    BTd2 = dscr("BTd2", [2, 32, 8, 2, 32, 128], BF16)
    Mlagd = dscr("Mlagd", [2, 8, 128, 1024], BF16)
    uTd2 = dscr("uTd2", [8, 128, nseq, 8, 512], BF16)

    es = contextlib.ExitStack()
    stk = [es]
    uid = [0]

    def sb(name, shape, dt):
        uid[0] += 1
        return stk[-1].enter_context(nc.sbuf_tensor("s%d_%s" % (uid[0], name), list(shape), dt))

    def ps(name, shape, dt):
        uid[0] += 1
        return stk[-1].enter_context(nc.psum_tensor("p%d_%s" % (uid[0], name), list(shape), dt))

    def phase_begin():
        stk.append(contextlib.ExitStack())

    def phase_end():
        S.barrier()
        stk.pop().close()

    def I(eng, name, reads, writes, *args, **kw):
        return S.op(eng, lambda e: getattr(e, name)(*args, **kw), reads, writes)

    def DMA(out_, in_, reads, writes, q="sp", **kw):
        return S.op(q, lambda e: e.dma_start(out=out_, in_=in_, **kw), reads, writes, dma=True)

    class Rot:
        def __init__(self, name, n, shape, dt, psum=False):
            mk = ps if psum else sb
            self.t = [mk("%s%d" % (name, i), shape, dt) for i in range(n)]
            self.b = [Buf("%s%d" % (name, i)) for i in range(n)]
            self.i = 0

        def next(self):
            k = self.i % len(self.t)
            self.i += 1
            return self.t[k], self.b[k]

    def rstd_cols(st_ap, st_buf, div):
        I("dve", "tensor_scalar", [st_buf], [st_buf], out=st_ap, in0=st_ap, scalar1=1.0 / div, scalar2=EPS,
          op0=ALU.mult, op1=ALU.add)
        I("act", "activation", [st_buf], [st_buf], out=st_ap, in_=st_ap, func=AF.Ln)
        I("act", "activation", [st_buf], [st_buf], out=st_ap, in_=st_ap, func=AF.Exp, scale=-0.5)

    with es:
        ident = sb("ident", [128, 128], BF16)
        b_ident = Buf("ident")
        DMA(ident[:], ident_in, [], [b_ident])
        tri = sb("tri", [128, 128], BF16)
        b_tri = Buf("tri")
        DMA(tri[:], tri_in, [], [b_tri])

        def prep_phase():
            phase_begin()
            stage = Rot("stage", 4, [128, 4096], F32)
            stageo = Rot("stageo", 4, [128, 4096], BF16)
            gsb = sb("gsb", [128, 80], F32)
            b_gsb = Buf("gsb")
            DMA(gsb[:, 0:32].rearrange("p (l k) -> p l k", l=DEPTH), ffn_g.rearrange("l (k p) -> p l k", p=128),
                [], [b_gsb], allow_slow_non_contiguous=True)
            DMA(gsb[:, 32:64].rearrange("p (l k) -> p l k", l=DEPTH), mix_g.rearrange("l (k p) -> p l k", p=128),
                [], [b_gsb], allow_slow_non_contiguous=True)
            DMA(gsb[:, 64:68].rearrange("p (l k) -> p l k", l=2), g_cq.rearrange("l (k p) -> p l k", p=128),
                [], [b_gsb], allow_slow_non_contiguous=True)
            DMA(gsb[:, 68:70].rearrange("p (l k) -> p l k", l=2), g_ckv.rearrange("l (k p) -> p l k", p=128),
                [], [b_gsb], allow_slow_non_contiguous=True)
            cnt = [0]

            def prep(src_ap, src_view, ncols, gcol, dsts):
                st, bst = stage.next()
                so, bso = stageo.next()
                eng = ["dve", "act"][cnt[0] % 2]
                cnt[0] += 1
                DMA(src_view(st), src_ap, [], [bst])
                if eng == "act":
                    if gcol is None:
                        I("act", "activation", [bst], [bso], out=so[:, :ncols], in_=st[:, :ncols], func=AF.Copy)
                    else:
                        I("act", "activation", [bst, b_gsb], [bso], out=so[:, :ncols], in_=st[:, :ncols],
                          func=AF.Copy, scale=gcol)
                elif gcol is None:
                    I(eng, "tensor_copy", [bst], [bso], out=so[:, :ncols], in_=st[:, :ncols])
                else:
                    I(eng, "tensor_scalar", [bst, b_gsb], [bso], out=so[:, :ncols], in0=st[:, :ncols], scalar1=gcol,
                      scalar2=None, op0=ALU.mult)
                for dap, sl in dsts:
                    DMA(dap, so[:, sl], [bso], [], q="act")

            for L in layers:
                if do_ffn:
                    for k in range(8):
                        dsts = [(w1c[L, h].rearrange("p (k c) -> p k c", k=8)[:, k, :],
                                 slice(h * 2048, (h + 1) * 2048)) for h in range(2)]
                        prep(w_ff1[L, k * 128:(k + 1) * 128, :], lambda st: st[:, :4096], 4096,
                             gsb[:, L * 8 + k:L * 8 + k + 1], dsts)
                    for jj in range(8):
                        h, j0 = divmod(jj * 4, 16)
                        prep(w_ff2[L, jj * 512:(jj + 1) * 512, :].rearrange("(j p) c -> p j c", p=128),
                             lambda st: st[:].rearrange("p (j c) -> p j c", j=4), 4096, None,
                             [(w2c[L, h][:, j0 * 1024:(j0 + 4) * 1024], slice(0, 4096))])
                if do_mix and L % 2 == 0:
                    i = L // 2
                    for k in range(8):
                        prep(w_in[i, k * 128:(k + 1) * 128, :], lambda st: st[:, :IN_COLS], IN_COLS,
                             gsb[:, 32 + L * 8 + k:32 + L * 8 + k + 1],
                             [(winc[i].rearrange("p (k c) -> p k c", k=8)[:, k, :], slice(0, IN_COLS))])
                    for k in range(2):
                        prep(w_uq[i, k * 128:(k + 1) * 128, :], lambda st: st[:, :768], 768,
                             gsb[:, 64 + i * 2 + k:64 + i * 2 + k + 1],
                             [(wuqc[i][:, k * 768:(k + 1) * 768], slice(0, 768))])
                    prep(w_ukv[i], lambda st: st[:, :1024], 1024, gsb[:, 68 + i:69 + i],
                         [(wukvc[i], slice(0, 1024))])
                    for kk in range(2):
                        prep(w_o[i, kk * 512:(kk + 1) * 512, :].rearrange("(j p) c -> p j c", p=128),
                             lambda st: st[:].rearrange("p (j c) -> p j c", j=4), 4096, None,
                             [(woc[i][:, kk * 4096:(kk + 1) * 4096], slice(0, 4096))])
                if do_mix and L % 2 == 1:
                    i = L // 2
                    for k in range(8):
                        prep(w_glu[i, k * 128:(k + 1) * 128, :], lambda st: st[:, :2048], 2048, None,
                             [(wgluc[i].rearrange("p (k c) -> p k c", k=8)[:, k, :], slice(0, 2048))])
            phase_end()

        def ffn_half(L, h, src, dst):
            phase_begin()
            w1s = sb("w1s", [128, 8 * 2048], BF16)
            w2s = sb("w2s", [128, 16 * 1024], BF16)
            b_w1s, b_w2s = Buf(), Buf()
            for k in range(8):
                DMA(w1s[:, k * 2048:(k + 1) * 2048], w1c[L, h][:, k * 2048:(k + 1) * 2048], [], [b_w1s])
            for k in range(4):
                DMA(w2s[:, k * 4096:(k + 1) * 4096], w2c[L, h][:, k * 4096:(k + 1) * 4096], [], [b_w2s])
            xt = Rot("xt", 3, [128, 4 * 1024], F32)
            hT = Rot("hT", 2, [128, 8 * 512], BF16)
            aT = sb("aT", [128, 16 * 512], BF16)
            b_aT = [Buf() for _ in range(16)]
            xn = Rot("xn", 2, [128, 1024], BF16)
            junk = sb("junk", [128, 1024], F32)
            b_junk = Buf()
            rl = Rot("rl", 3, [128, 512], F32)
            stat = Rot("stat", 4, [128, 1], F32)
            ps_h = Rot("ps_h", 3, [128, 512], F32, psum=True)
            ps_y = Rot("ps_y", 2, [128, 512], F32, psum=True)
            ps_t = Rot("ps_t", 2, [128, 1024], BF16, psum=True)
            tiles = {}

            def stage_a(t):
                xti, bxt = xt.next()
                hTi, bhT = hT.next()
                tiles[t] = (xti, bxt, hTi, bhT)
                rows = slice(t * 512, (t + 1) * 512)
                DMA(xti[:].rearrange("p (s c) -> p s c", s=4), src[rows, :].rearrange("(s p) c -> p s c", p=128),
                    [], [bxt])
                if h == 0:
                    for s in range(4):
                        xs = xti[:, s * 1024:(s + 1) * 1024]
                        st, bst = stat.next()
                        I("act", "activation", [bxt], [b_junk, bst], out=junk[:], in_=xs, func=AF.Square,
                          accum_out=st[:, 0:1])
                        rstd_cols(st[:, 0:1], bst, 1024)
                        xnq, bxn = xn.next()
                        I("dve", "tensor_scalar", [bxt, bst], [bxn], out=xnq[:], in0=xs, scalar1=st[:, 0:1],
                          scalar2=None, op0=ALU.mult)
                        pst, bpst = ps_t.next()
                        for k in range(8):
                            I("pe", "transpose", [bxn, b_ident], [bpst], out=pst[:, k * 128:(k + 1) * 128],
                              in_=xnq[:, k * 128:(k + 1) * 128], identity=ident[:])
                        I("act", "activation", [bpst], [bhT],
                          out=hTi[:].rearrange("p (k t) -> p k t", k=8)[:, :, s * 128:(s + 1) * 128],
                          in_=pst[:].rearrange("p (k t) -> p k t", k=8), func=AF.Copy)
                    DMA(hTd[t], hTi[:], [bhT], [], q="act")
                else:
                    DMA(hTi[:], hTd[t], [], [bhT])

            def stage_b(t):
                xti, bxt, hTi, bhT = tiles[t]
                for j in range(16):
                    psh, bpsh = ps_h.next()
                    for k in range(8):
                        I("pe", "matmul", [b_w1s, bhT], [bpsh], psh[:],
                          lhsT=w1s[:, k * 2048 + j * 128:k * 2048 + (j + 1) * 128],
                          rhs=hTi[:, k * 512:(k + 1) * 512], start=(k == 0), stop=(k == 7))
                    rlq, brl = rl.next()
                    I("act", "activation", [bpsh], [brl], out=rlq[:], in_=psh[:], func=AF.Relu)
                    I("pool" if j % 4 == 3 else "dve", "tensor_tensor", [brl], [b_aT[j]],
                      out=aT[:, j * 512:(j + 1) * 512], in0=rlq[:], in1=rlq[:], op=ALU.mult)

            def stage_c(t):
                xti, bxt, hTi, bhT = tiles.pop(t)
                rows = slice(t * 512, (t + 1) * 512)
                for s in range(4):
                    for c in range(2):
                        psy, bpsy = ps_y.next()
                        for j in range(16):
                            I("pe", "matmul", [b_aT[j], b_w2s], [bpsy], psy[:],
                              lhsT=aT[:, j * 512 + s * 128:j * 512 + (s + 1) * 128],
                              rhs=w2s[:, j * 1024 + c * 512:j * 1024 + (c + 1) * 512],
                              start=(j == 0), stop=(j == 15))
                        cs = slice(s * 1024 + c * 512, s * 1024 + (c + 1) * 512)
                        I("dve", "tensor_tensor", [bpsy, bxt], [bxt], out=xti[:, cs], in0=psy[:], in1=xti[:, cs],
                          op=ALU.add)
                DMA(dst[rows, :].rearrange("(s p) c -> p s c", p=128), xti[:].rearrange("p (s c) -> p s c", s=4),
                    [bxt], [], q="pool")

            stage_a(0)
            for t in range(NT5):
                if t + 1 < NT5:
                    stage_a(t + 1)
                stage_b(t)
                stage_c(t)
            phase_end()

        def skew(stages, n):
            ns = len(stages)
            for k in range(n + ns - 1):
                for si in reversed(range(ns)):
                    t = k - si
                    if 0 <= t < n:
                        stages[si](t)

        def attn_phase_a(i, L, src):
            phase_begin()
            win = sb("win", [128, 8 * IN_COLS], BF16)
            wuq = sb("wuq", [128, 2 * 768], BF16)
            wukv = sb("wukv", [128, 1024], BF16)
            b_w = Buf()
            DMA(win[:], winc[i], [], [b_w])
            DMA(wuq[:], wuqc[i], [], [b_w])
            DMA(wukv[:], wukvc[i], [], [b_w])
            cosT = sb("cosT", [128, 32 * 16], F32)
            sinT = sb("sinT", [128, 32 * 16], F32)
            b_cs = Buf()
            DMA(cosT[:].rearrange("p (t f) -> p t f", f=16), cos_in.rearrange("(t p) f -> p t f", p=128), [], [b_cs])
            DMA(sinT[:].rearrange("p (t f) -> p t f", f=16), sin_in.rearrange("(t p) f -> p t f", p=128), [], [b_cs])
            gq = sb("gq", [128, 4], F32)
            b_gq = Buf()
            DMA(gq[0:96, 0:1], g_qn_mla[i].rearrange("(d o) -> d o", o=1), [], [b_gq], allow_slow_non_contiguous=True)
            DMA(gq[0:96, 1:2], g_kn_mla[i].rearrange("(d o) -> d o", o=1), [], [b_gq], allow_slow_non_contiguous=True)
            for hh in range(2):
                DMA(gq[hh * 64:(hh + 1) * 64, 2:3], g_qn_moba[i].rearrange("(d o) -> d o", o=1), [], [b_gq],
                    allow_slow_non_contiguous=True)
                DMA(gq[hh * 64:(hh + 1) * 64, 3:4], g_kn_moba[i].rearrange("(d o) -> d o", o=1), [], [b_gq],
                    allow_slow_non_contiguous=True)
            sq = sb("sq", [128, 2], F32)
            b_sq = Buf()
            I("dve", "scalar_tensor_tensor", [b_gq], [b_sq], out=sq[0:96, 0:1], in0=gq[0:96, 0:1], scalar=96.0 ** -0.5,
              in1=gq[0:96, 1:2], op0=ALU.mult, op1=ALU.mult)
            I("dve", "scalar_tensor_tensor", [b_gq], [b_sq], out=sq[:, 1:2], in0=gq[:, 2:3], scalar=64.0 ** -0.5,
              in1=gq[:, 3:4], op0=ALU.mult, op1=ALU.mult)

            xt = Rot("xt", 2, [128, 1024], F32)
            xn = Rot("xn", 2, [128, 1024], BF16)
            xnT = Rot("xnT", 3, [128, 1024], BF16)
            junk = sb("junk", [128, 1024], F32)
            b_junk = Buf()
            st1 = Rot("st1", 9, [128, 1], F32)
            st8 = Rot("st8", 8, [128, 8], F32)
            pm = Rot("pm", 6, [128, 512], F32, psum=True)
            ps_t = Rot("ps_t", 2, [128, 1024], BF16, psum=True)
            cA = Rot("cA", 3, [128, 416], F32)
            cn = Rot("cn", 2, [128, 384], BF16)
            cnT = Rot("cnT", 2, [128, 384], BF16)
            qf = Rot("qf", 3, [128, 768], F32)
            kf = Rot("kf", 3, [128, 768], F32)
            sqt = Rot("sqt", 2, [128, 768], F32)
            rt = Rot("rt", 2, [128, 4 * 128], F32)
            krt = Rot("krt", 2, [128, 32], F32)
            nb = Rot("nb", 12, [128, 768], BF16)
            TT = Rot("TT", 4, [128, 1024], BF16)
            vmr = Rot("vm", 2, [128, 8 * 65], BF16)
            vbr = Rot("vb", 2, [128, 8 * 65], BF16)
            bf = Rot("bf", 6, [128, 512], F32)
            for r in (vmr, vbr):
                for tt, bb in zip(r.t, r.b):
                    I("pool", "memset", [], [bb], tt[:], 1.0)

            def rope(xv, bx, x1, x2, cosv, sinv, nh):
                r, br = rt.next()
                rv = r[:].rearrange("p (a h f) -> p a h f", a=4, h=8)
                t1, t2, t3, t4 = (rv[:, a, 0:nh, :] for a in range(4))
                I("pool", "tensor_tensor", [bx, b_cs], [br], out=t1, in0=x1, in1=cosv, op=ALU.mult)
                I("pool", "tensor_tensor", [bx, b_cs], [br], out=t2, in0=x2, in1=sinv, op=ALU.mult)
                I("pool", "tensor_tensor", [bx, b_cs], [br], out=t3, in0=x1, in1=sinv, op=ALU.mult)
                I("pool", "tensor_tensor", [bx, b_cs], [br], out=t4, in0=x2, in1=cosv, op=ALU.mult)
                I("pool", "tensor_tensor", [br], [bx], out=x1, in0=t1, in1=t2, op=ALU.subtract)
                I("pool", "tensor_tensor", [br], [bx], out=x2, in0=t3, in1=t4, op=ALU.add)

            def head_norm(xf, bxf, nh, hd, dstb, bdst):
                s2, bs2 = sqt.next()
                n = nh * hd
                I("pool", "tensor_tensor", [bxf], [bs2], out=s2[:, :n], in0=xf[:, :n], in1=xf[:, :n], op=ALU.mult)
                st, bst = st8.next()
                I("dve", "tensor_reduce", [bs2], [bst], out=st[:, 0:nh],
                  in_=s2[:, :n].rearrange("p (h d) -> p h d", h=nh), axis=AX.X, op=ALU.add)
                rstd_cols(st[:, 0:nh], bst, hd)
                I("dve", "tensor_tensor", [bxf, bst], [bdst], out=dstb[:, :n].rearrange("p (h d) -> p h d", h=nh),
                  in0=xf[:, :n].rearrange("p (h d) -> p h d", h=nh),
                  in1=st[:, 0:nh].unsqueeze(2).broadcast_to([128, nh, hd]), op=ALU.mult)

            T_ = {}

            def sA(t):
                    seq, kt = divmod(t, 32)
                    pos = slice(kt * 128, (kt + 1) * 128)
                    cosv = cosT[:, kt * 16:(kt + 1) * 16]
                    sinv = sinT[:, kt * 16:(kt + 1) * 16]
                    xti, bxt = xt.next()
                    DMA(xti[:], src[t * 128:(t + 1) * 128, :], [], [bxt])
                    st, bst = st1.next()
                    I("act", "activation", [bxt], [b_junk, bst], out=junk[:], in_=xti[:], func=AF.Square,
                      accum_out=st[:, 0:1])
                    rstd_cols(st[:, 0:1], bst, 1024)
                    xnq, bxn = xn.next()
                    I("dve", "tensor_scalar", [bxt, bst], [bxn], out=xnq[:], in0=xti[:], scalar1=st[:, 0:1], scalar2=None,
                      op0=ALU.mult)
                    pst, bpst = ps_t.next()
                    for k in range(8):
                        I("pe", "transpose", [bxn, b_ident], [bpst], out=pst[:, k * 128:(k + 1) * 128],
                          in_=xnq[:, k * 128:(k + 1) * 128], identity=ident[:])
                    xT, bxT = xnT.next()
                    I("act", "activation", [bpst], [bxT], out=xT[:], in_=pst[:], func=AF.Copy)
                    T_[t] = dict(xT=(xT, bxT))

            def sB(t):
                    seq, kt = divmod(t, 32)
                    pos = slice(kt * 128, (kt + 1) * 128)
                    cosv = cosT[:, kt * 16:(kt + 1) * 16]
                    sinv = sinT[:, kt * 16:(kt + 1) * 16]
                    d = T_[t]
                    xT, bxT = d['xT']
                    pg = []
                    for (c0, c1) in GROUPS_A:
                        p, bp = pm.next()
                        for k in range(8):
                            I("pe", "matmul", [bxT, b_w], [bp], p[:, 0:c1 - c0], lhsT=xT[:, k * 128:(k + 1) * 128],
                              rhs=win[:, k * IN_COLS + c0:k * IN_COLS + c1], start=(k == 0), stop=(k == 7))
                        pg.append((p, bp))
                    cAi, bcA = cA.next()
                    I("act", "activation", [pg[0][1]], [bcA], out=cAi[:], in_=pg[0][0][:, 0:416], func=AF.Copy)
                    bfs = []
                    for which in range(2):
                        p, bp = pg[1 + which]
                        bfi, bbf = bf.next()
                        I("act", "activation", [bp], [bbf], out=bfi[:], in_=p[:], func=AF.Copy)
                        bfs.append((bfi, bbf))
                    vbi, bvb = vbr.next()
                    vb3 = vbi[:].rearrange("p (h c) -> p h c", h=8)
                    I("act", "activation", [pg[3][1]], [bvb], out=vb3[:, :, 0:64],
                      in_=pg[3][0][:].rearrange("p (h d) -> p h d", h=8), func=AF.Copy)
                    sa, bsa = st1.next()
                    sb_, bsb = st1.next()
                    I("act", "activation", [bcA], [b_junk, bsa], out=junk[:, 0:256], in_=cAi[:, 0:256], func=AF.Square,
                      accum_out=sa[:, 0:1])
                    I("act", "activation", [bcA], [b_junk, bsb], out=junk[:, 0:128], in_=cAi[:, 256:384], func=AF.Square,
                      accum_out=sb_[:, 0:1])
                    rstd_cols(sa[:, 0:1], bsa, 256)
                    rstd_cols(sb_[:, 0:1], bsb, 128)
                    cni, bcn = cn.next()
                    I("dve", "tensor_scalar", [bcA, bsa], [bcn], out=cni[:, 0:256], in0=cAi[:, 0:256], scalar1=sa[:, 0:1],
                      scalar2=None, op0=ALU.mult)
                    I("dve", "tensor_scalar", [bcA, bsb], [bcn], out=cni[:, 256:384], in0=cAi[:, 256:384],
                      scalar1=sb_[:, 0:1], scalar2=None, op0=ALU.mult)
                    pst, bpst = ps_t.next()
                    for k in range(3):
                        I("pe", "transpose", [bcn, b_ident], [bpst], out=pst[:, k * 128:(k + 1) * 128],
                          in_=cni[:, k * 128:(k + 1) * 128], identity=ident[:])
                    cT, bcT = cnT.next()
                    I("act", "activation", [bpst], [bcT], out=cT[:], in_=pst[:, 0:384], func=AF.Copy)
                    pq0, bpq0 = pm.next()
                    pq1, bpq1 = pm.next()
                    for k in range(2):
                        I("pe", "matmul", [bcT, b_w], [bpq0], pq0[:], lhsT=cT[:, k * 128:(k + 1) * 128],
                          rhs=wuq[:, k * 768:k * 768 + 512], start=(k == 0), stop=(k == 1))
                    for k in range(2):
                        I("pe", "matmul", [bcT, b_w], [bpq1], pq1[:, 0:256], lhsT=cT[:, k * 128:(k + 1) * 128],
                          rhs=wuq[:, k * 768 + 512:k * 768 + 768], start=(k == 0), stop=(k == 1))
                    pk = []
                    for c in range(2):
                        p, bp = pm.next()
                        I("pe", "matmul", [bcT, b_w], [bp], p[:], lhsT=cT[:, 256:384], rhs=wukv[:, c * 512:(c + 1) * 512],
                          start=True, stop=True)
                        pk.append((p, bp))
                    qfi, bqf = qf.next()
                    I("act", "activation", [bpq0], [bqf], out=qfi[:, 0:512], in_=pq0[:], func=AF.Copy)
                    I("act", "activation", [bpq1], [bqf], out=qfi[:, 512:768], in_=pq1[:, 0:256], func=AF.Copy)
                    kfi, bkf = kf.next()
                    k3 = kfi[:].rearrange("p (h d) -> p h d", h=8)
                    for c in range(2):
                        I("act", "activation", [pk[c][1]], [bkf], out=k3[:, c * 4:(c + 1) * 4, 0:64],
                          in_=pk[c][0][:].rearrange("p (h d) -> p h d", h=4)[:, :, 0:64], func=AF.Copy)
                    vmi, bvm = vmr.next()
                    v3 = vmi[:].rearrange("p (h c) -> p h c", h=8)
                    for c in range(2):
                        I("act", "activation", [pk[c][1]], [bvm], out=v3[:, c * 4:(c + 1) * 4, 0:64],
                          in_=pk[c][0][:].rearrange("p (h d) -> p h d", h=4)[:, :, 64:128], func=AF.Copy)
                    DMA(Vm[seq].rearrange("h p (k c) -> p h k c", c=65)[:, :, kt, :], v3, [bvm], [], q="act")
                    DMA(Vb[seq].rearrange("h p (k c) -> p h k c", c=65)[:, :, kt, :], vb3, [bvb], [], q="act")
                    d.update(cA=(cAi, bcA), bfs=bfs, qf=(qfi, bqf), kf=(kfi, bkf))

            def sC(t):
                    seq, kt = divmod(t, 32)
                    pos = slice(kt * 128, (kt + 1) * 128)
                    cosv = cosT[:, kt * 16:(kt + 1) * 16]
                    sinv = sinT[:, kt * 16:(kt + 1) * 16]
                    d = T_[t]
                    cAi, bcA = d['cA']
                    qfi, bqf = d['qf']
                    kfi, bkf = d['kf']
                    k3 = kfi[:].rearrange('p (h d) -> p h d', h=8)
                    q3 = qfi[:].rearrange("p (h d) -> p h d", h=8)
                    rope(qfi, bqf, q3[:, :, 64:80], q3[:, :, 80:96], cosv.unsqueeze(1).broadcast_to([128, 8, 16]),
                         sinv.unsqueeze(1).broadcast_to([128, 8, 16]), 8)
                    qnb, bqnb = nb.next()
                    head_norm(qfi, bqf, 8, 96, qnb, bqnb)
                    kr, bkr = krt.next()
                    I("pool", "tensor_copy", [bcA], [bkr], out=kr[:], in_=cAi[:, 384:416])
                    rope(kr, bkr, kr[:, 0:16].unsqueeze(1), kr[:, 16:32].unsqueeze(1), cosv.unsqueeze(1), sinv.unsqueeze(1), 1)
                    I("pool", "tensor_copy", [bkr], [bkf], out=k3[:, :, 64:96],
                      in_=kr[:].unsqueeze(1).broadcast_to([128, 8, 32]))
                    knb, bknb = nb.next()
                    head_norm(kfi, bkf, 8, 96, knb, bknb)
                    nbs = []
                    for which in range(2):
                        bfi, bbf = d['bfs'][which]
                        nbi, bnb = nb.next()
                        head_norm(bfi, bbf, 8, 64, nbi, bnb)
                        nbs.append((nbi, bnb))
                    d.update(qnb=(qnb, bqnb), knb=(knb, bknb), nbs=nbs)

            def sD(t):
                    seq, kt = divmod(t, 32)
                    pos = slice(kt * 128, (kt + 1) * 128)
                    cosv = cosT[:, kt * 16:(kt + 1) * 16]
                    sinv = sinT[:, kt * 16:(kt + 1) * 16]
                    d = T_.pop(t)
                    qnb, bqnb = d['qnb']
                    knb, bknb = d['knb']
                    pst, bpst = ps_t.next()
                    for hh in range(8):
                        I("pe", "transpose", [bqnb, b_ident], [bpst], out=pst[0:96, hh * 128:(hh + 1) * 128],
                          in_=qnb[:, hh * 96:(hh + 1) * 96], identity=ident[:])
                    Tq, bTq = TT.next()
                    I("act", "activation", [bpst, b_sq], [bTq], out=Tq[0:96, :], in_=pst[0:96, :], func=AF.Copy,
                      scale=sq[0:96, 0:1])
                    DMA(QTm[seq].rearrange("h d s -> d h s")[:, :, pos], Tq[0:96, :].rearrange("p (h s) -> p h s", h=8),
                        [bTq], [], q="act")
                    pst, bpst = ps_t.next()
                    for hh in range(8):
                        I("pe", "transpose", [bknb, b_ident], [bpst], out=pst[0:96, hh * 128:(hh + 1) * 128],
                          in_=knb[:, hh * 96:(hh + 1) * 96], identity=ident[:])
                    Tk, bTk = TT.next()
                    I("act", "activation", [bpst], [bTk], out=Tk[0:96, :], in_=pst[0:96, :], func=AF.Copy)
                    DMA(KTm[seq].rearrange("h d s -> d h s")[:, :, pos], Tk[0:96, :].rearrange("p (h s) -> p h s", h=8),
                        [bTk], [], q="act")
                    for which in range(2):
                        nbi, bnb = d['nbs'][which]
                        pst, bpst = ps_t.next()
                        for pr in range(4):
                            I("pe", "transpose", [bnb, b_ident], [bpst], out=pst[:, pr * 128:(pr + 1) * 128],
                              in_=nbi[:, pr * 128:(pr + 1) * 128], identity=ident[:])
                        Tb, bTb = TT.next()
                        if which == 0:
                            I("act", "activation", [bpst, b_sq], [bTb], out=Tb[:, 0:512], in_=pst[:, 0:512], func=AF.Copy,
                              scale=sq[:, 1:2])
                        else:
                            I("act", "activation", [bpst], [bTb], out=Tb[:, 0:512], in_=pst[:, 0:512], func=AF.Copy)
                        dstT = QTb if which == 0 else KTb
                        DMA(dstT[seq].rearrange("r p s -> p r s")[:, :, pos], Tb[:, 0:512].rearrange("p (r s) -> p r s", r=4),
                            [bTb], [], q="act")

            skew([sA, sB, sC, sD], NT)
            phase_end()

        def attn_phase_b():
            phase_begin()
            QT = Rot("QT", 3, [128, SEQ], BF16)
            KT = Rot("KT", 3, [128, SEQ], BF16)
            VV = Rot("VV", 3, [128, 32 * 65], BF16)
            pss = Rot("pss", 4, [128, 512], F32, psum=True)
            pso = Rot("pso", 3, [128, 512], F32, psum=True)
            psg = Rot("psg", 1, [128, 64], F32, psum=True)
            pT = Rot("pT", 5, [128, 512], BF16)
            osb = Rot("osb", 2, [128, 260], F32)
            rec = Rot("rec", 2, [128, 4], F32)
            obf = Rot("obf", 2, [128, 256], BF16)
            km = Rot("km", 3, [128, 16], F32)
            kmb = Rot("kmb", 3, [128, 16], BF16)
            gs = Rot("gs", 2, [128, 64], F32)
            mx = Rot("mx", 2, [128, 32], F32)
            sel = Rot("sel", 2, [128, 64], F32)
            acc = Rot("acc", 2, [128, 260], F32)
            tmpc = Rot("tmpc", 3, [128, 260], F32)

            def finalize(src_ap, bsrc, seq, qb, col0):
                r, br = rec.next()
                s3 = src_ap.rearrange("p (s c) -> p s c", s=4)
                I("dve", "reciprocal", [bsrc], [br], out=r[:].unsqueeze(2), in_=s3[:, :, 64:65])
                o, bo = obf.next()
                I("dve", "tensor_tensor", [bsrc, br], [bo], out=o[:].rearrange("p (s c) -> p s c", s=4),
                  in0=s3[:, :, 0:64], in1=r[:].unsqueeze(2).broadcast_to([128, 4, 64]), op=ALU.mult)
                r0 = seq * SEQ + qb * 512
                DMA(Od[r0:r0 + 512, col0:col0 + 64].rearrange("(s p) c -> p s c", p=128),
                    o[:].rearrange("p (s c) -> p s c", s=4), [bo], [], q="pool")

            def qk_exp(KTi, bKT, QTi, bQT, dk, kt, qb):
                p, bp = pss.next()
                I("pe", "matmul", [bKT, bQT], [bp], p[:], lhsT=KTi[0:dk, kt * 128:(kt + 1) * 128],
                  rhs=QTi[0:dk, qb * 512:(qb + 1) * 512], start=True, stop=True)
                e, be = pT.next()
                I("act", "activation", [bp], [be], out=e[:], in_=p[:], func=AF.Exp)
                if kt >= 4 * qb:
                    s = kt - 4 * qb
                    I("pool", "tensor_tensor", [be, b_tri], [be], out=e[:, s * 128:(s + 1) * 128],
                      in0=e[:, s * 128:(s + 1) * 128], in1=tri[:], op=ALU.mult)
                return e, be

            steps = []
            units = []
            for seq in range(nseq):
                for h in range(8):
                    units.append((seq, "mla", h))
                for h in range(8):
                    units.append((seq, "moba", h))
            loaded = {}

            def load_unit(u):
                seq, kind, h = units[u]
                QTi, bQT = QT.next()
                KTi, bKT = KT.next()
                Vi, bV = VV.next()
                if kind == "mla":
                    DMA(QTi[0:96, :], QTm[seq, h], [], [bQT])
                    DMA(KTi[0:96, :], KTm[seq, h], [], [bKT])
                    DMA(Vi[:], Vm[seq, h], [], [bV])
                else:
                    pr, half = divmod(h, 2)
                    DMA(QTi[0:64, :], QTb[seq, pr, half * 64:(half + 1) * 64, :], [], [bQT])
                    DMA(KTi[0:64, :], KTb[seq, pr, half * 64:(half + 1) * 64, :], [], [bKT])
                    DMA(Vi[:], Vb[seq, h], [], [bV])
                loaded[u] = (QTi, bQT, KTi, bKT, Vi, bV)

            def mk_mla(u, qb):
                seq, kind, h = units[u]
                state = {}

                def pre():
                    if qb == 0:
                        if u == 0:
                            load_unit(0)
                        if u + 1 < len(units):
                            load_unit(u + 1)
                    state["po"] = pso.next()

                def qk(kt):
                    def f():
                        if kt == 0:
                            pre()
                        QTi, bQT, KTi, bKT, Vi, bV = loaded[u]
                        state[kt] = qk_exp(KTi, bKT, QTi, bQT, 96, kt, qb)
                    return f

                def pv(kt):
                    def f():
                        QTi, bQT, KTi, bKT, Vi, bV = loaded[u]
                        po, bpo = state["po"]
                        e, be = state.pop(kt)
                        for s in range(4):
                            if kt <= 4 * qb + s:
                                I("pe", "matmul", [be, bV], [bpo], po[:, s * 65:(s + 1) * 65],
                                  lhsT=e[:, s * 128:(s + 1) * 128], rhs=Vi[:, kt * 65:(kt + 1) * 65],
                                  start=(kt == 0 and s == 0), stop=(kt == 4 * qb + 3 and s == 3),
                                  skip_group_check=True)
                        if kt == 4 * qb + 3:
                            o, bo = osb.next()
                            I("act", "activation", [bpo], [bo], out=o[:], in_=po[:, 0:260], func=AF.Copy)
                            finalize(o[:], bo, seq, qb, h * 64)
                    return f

                for kt in range(4 * qb + 4):
                    steps.append((qk(kt), pv(kt)))

            def mk_moba(u, qb):
                seq, kind, h = units[u]
                state = {}

                def pre():
                    if qb == 0:
                        if u == 0:
                            load_unit(0)
                        if u + 1 < len(units):
                            load_unit(u + 1)
                    QTi, bQT, KTi, bKT, Vi, bV = loaded[u]
                    if qb == 0:
                        kmi, bkm = km.next()
                        I("dve", "tensor_reduce", [bKT], [bkm], out=kmi[0:64, :],
                          in_=KTi[0:64, :].rearrange("p (n l) -> p n l", l=256), axis=AX.X, op=ALU.add)
                        kbi, bkb = kmb.next()
                        I("dve", "tensor_scalar", [bkm], [bkb], out=kbi[0:64, :], in0=kmi[0:64, :],
                          scalar1=1.0 / 256, scalar2=None, op0=ALU.mult)
                        loaded[("kmb", u)] = (kbi, bkb)
                    kbi, bkb = loaded[("kmb", u)]
                    pg_, bpg = psg.next()
                    for s in range(4):
                        I("pe", "matmul", [bQT, bkb], [bpg], pg_[:, s * 16:(s + 1) * 16],
                          lhsT=QTi[0:64, (4 * qb + s) * 128:(4 * qb + s + 1) * 128], rhs=kbi[0:64, :],
                          start=True, stop=True)
                    sl, bsl = sel.next()
                    curs = [2 * qb + s // 2 for s in range(4)]
                    if curs[3] <= 3:
                        I("pool", "memset", [], [bsl], sl[:], 1.0)
                    else:
                        g, bg = gs.next()
                        I("pool", "memset", [], [bg], g[:], NEG)
                        for s0 in (0, 2):
                            cur = curs[s0]
                            I("act", "activation", [bpg], [bg],
                              out=g[:].rearrange("p (s n) -> p s n", s=4)[:, s0:s0 + 2, 0:cur],
                              in_=pg_[:].rearrange("p (s n) -> p s n", s=4)[:, s0:s0 + 2, 0:cur], func=AF.Copy)
                        m8, bm8 = mx.next()
                        for s in range(4):
                            I("dve", "max", [bg], [bm8], out=m8[:, s * 8:(s + 1) * 8], in_=g[:, s * 16:(s + 1) * 16])
                        I("dve", "tensor_tensor", [bg, bm8], [bsl], out=sl[:].rearrange("p (s n) -> p s n", s=4),
                          in0=g[:].rearrange("p (s n) -> p s n", s=4),
                          in1=m8[:].rearrange("p (s e) -> p s e", s=4)[:, :, 2:3].broadcast_to([128, 4, 16]),
                          op=ALU.is_ge)
                        for s in range(4):
                            if curs[s] <= 3:
                                I("pool", "memset", [], [bsl], sl[:, s * 16:s * 16 + curs[s]], 1.0)
                            I("pool", "memset", [], [bsl], sl[:, s * 16 + curs[s]:s * 16 + curs[s] + 1], 1.0)
                    state["sl"] = (sl, bsl)
                    ac, bac = acc.next()
                    I("pool", "memset", [], [bac], ac[:], 0.0)
                    state["acc"] = (ac, bac)

                def qk(kt):
                    def f():
                        if kt == 0:
                            pre()
                        QTi, bQT, KTi, bKT, Vi, bV = loaded[u]
                        if kt % 2 == 0:
                            state[("po", kt // 2)] = pso.next()
                        state[kt] = qk_exp(KTi, bKT, QTi, bQT, 64, kt, qb)
                    return f

                def pv(kt):
                    def f():
                        QTi, bQT, KTi, bKT, Vi, bV = loaded[u]
                        n = kt // 2
                        po, bpo = state[("po", n)]
                        e, be = state.pop(kt)
                        for s in range(4):
                            cur = 2 * qb + s // 2
                            qt = 4 * qb + s
                            if n < cur or (n == cur and kt <= qt):
                                first = (kt == 2 * n and s == (0 if n <= 2 * qb else 2))
                                I("pe", "matmul", [be, bV], [bpo], po[:, s * 65:(s + 1) * 65],
                                  lhsT=e[:, s * 128:(s + 1) * 128], rhs=Vi[:, kt * 65:(kt + 1) * 65],
                                  start=first, stop=False, skip_group_check=True)
                        if kt == 2 * n + 1:
                            sl, bsl = state["sl"]
                            ac, bac = state["acc"]
                            s0 = 0 if n <= 2 * qb else 2
                            tm, btm = tmpc.next()
                            cs = slice(s0 * 65, 260)
                            ns = 4 - s0
                            I("dve", "tensor_tensor", [bpo, bsl], [btm],
                              out=tm[:, cs].rearrange("p (s c) -> p s c", s=ns),
                              in0=po[:, cs].rearrange("p (s c) -> p s c", s=ns),
                              in1=sl[:].rearrange("p (s n) -> p s n", s=4)[:, s0:4, n:n + 1].broadcast_to([128, ns, 65]),
                              op=ALU.mult)
                            I("pool", "tensor_tensor", [btm, bac], [bac], out=ac[:, cs], in0=ac[:, cs], in1=tm[:, cs],
                              op=ALU.add)
                            state.pop(("po", n))
                            if n == 2 * qb + 1:
                                finalize(ac[:], bac, seq, qb, 512 + h * 64)
                    return f

                for kt in range(4 * qb + 4):
                    steps.append((qk(kt), pv(kt)))

            for u, (seq, kind, h) in enumerate(units):
                for qb in range(8):
                    (mk_mla if kind == "mla" else mk_moba)(u, qb)
            LOOK = 3
            for idx in range(len(steps) + LOOK):
                if idx < len(steps):
                    steps[idx][0]()
                if idx - LOOK >= 0:
                    steps[idx - LOOK][1]()
            phase_end()

        def attn_phase_c(i, src, dst):
            phase_begin()
            wo = sb("wo", [128, 8 * 1024], BF16)
            b_wo = Buf()
            DMA(wo[:], woc[i], [], [b_wo])
            xt = Rot("xt", 4, [128, 1024], F32)
            ot = Rot("ot", 3, [128, 1024], BF16)
            oT = Rot("oT", 3, [128, 1024], BF16)
            ps_t = Rot("ps_t", 2, [128, 1024], BF16, psum=True)
            ps_y = Rot("ps_y", 4, [128, 512], F32, psum=True)
            T_ = {}

            def sA(t):
                rows = slice(t * 128, (t + 1) * 128)
                xti, bxt = xt.next()
                oti, bot = ot.next()
                DMA(xti[:], src[rows, :], [], [bxt])
                DMA(oti[:], Od[rows, :], [], [bot])
                pst, bpst = ps_t.next()
                for k in range(8):
                    I("pe", "transpose", [bot, b_ident], [bpst], out=pst[:, k * 128:(k + 1) * 128],
                      in_=oti[:, k * 128:(k + 1) * 128], identity=ident[:])
                oTi, boT = oT.next()
                I("act", "activation", [bpst], [boT], out=oTi[:], in_=pst[:], func=AF.Copy)
                T_[t] = (xti, bxt, oTi, boT)

            def sB(t):
                rows = slice(t * 128, (t + 1) * 128)
                xti, bxt, oTi, boT = T_.pop(t)
                for c in range(2):
                    py, bpy = ps_y.next()
                    for k in range(8):
                        I("pe", "matmul", [boT, b_wo], [bpy], py[:], lhsT=oTi[:, k * 128:(k + 1) * 128],
                          rhs=wo[:, k * 1024 + c * 512:k * 1024 + (c + 1) * 512], start=(k == 0), stop=(k == 7))
                    I("dve", "tensor_tensor", [bpy, bxt], [bxt], out=xti[:, c * 512:(c + 1) * 512], in0=py[:],
                      in1=xti[:, c * 512:(c + 1) * 512], op=ALU.add)
                DMA(dst[rows, :], xti[:], [bxt], [], q=cfg.get("stq", "pool"))

            skew([sA, sB], NT)
            phase_end()

        NLEV = 9
        PI = float(np.pi)

        def s5_prep(i, L):
            apw = sb("apw", [128, NLEV * 2 * 32], F32)
            CPr = sb("CPr", [128, 32 * 32], BF16)
            CPi = sb("CPi", [128, 32 * 32], BF16)
            negl = sb("negl", [128, NLEV * 32], F32)
            b_p = Buf()
            phase_begin()
            lr = sb("lr", [128, 32], F32)
            li = sb("li", [128, 32], F32)
            dt = sb("dt", [128, 32], F32)
            bt = Buf()
            for two in range(2):
                ps_ = slice(two * 64, (two + 1) * 64)
                DMA(lr[ps_, :], lam_re[i].rearrange("(gp two) p -> two p gp", two=2)[two], [], [bt],
                    allow_slow_non_contiguous=True)
                DMA(li[ps_, :], lam_im[i].rearrange("(gp two) p -> two p gp", two=2)[two], [], [bt],
                    allow_slow_non_contiguous=True)
                DMA(dt[ps_, :], log_dt[i].rearrange("(gp two) -> two gp", two=2)[two:two + 1, :].broadcast_to([64, 32]),
                    [], [bt], allow_slow_non_contiguous=True)
            tl = [sb("t%d" % k, [128, 32], F32) for k in range(12)]
            er, ang, kk, sn, cs, ca, m2, t0, t1, wr, wi, den = tl

            def V(name, out_, **kw):
                I("dve", name, [bt], [bt], out=out_, **kw)

            I("act", "activation", [bt], [bt], out=dt[:], in_=dt[:], func=AF.Exp)
            V("tensor_tensor", er[:], in0=lr[:], in1=dt[:], op=ALU.mult)
            I("act", "activation", [bt], [bt], out=er[:], in_=er[:], func=AF.Exp)
            V("tensor_tensor", ang[:], in0=li[:], in1=dt[:], op=ALU.mult)
            I("pool", "memset", [bt], [bt], kk[:], 0.0)
            for m in range(1, 5):
                V("scalar_tensor_tensor", kk[:], in0=ang[:], scalar=(2 * m - 1) * PI, in1=kk[:], op0=ALU.is_gt,
                  op1=ALU.add)
            V("scalar_tensor_tensor", ang[:], in0=kk[:], scalar=-2 * PI, in1=ang[:], op0=ALU.mult, op1=ALU.add)
            I("act", "activation", [bt], [bt], out=sn[:], in_=ang[:], func=AF.Sin)
            V("tensor_scalar", ca[:], in0=ang[:], scalar1=PI / 2, scalar2=None, op0=ALU.add)
            V("tensor_scalar", m2[:], in0=ca[:], scalar1=PI, scalar2=None, op0=ALU.is_gt)
            V("scalar_tensor_tensor", ca[:], in0=m2[:], scalar=-2 * PI, in1=ca[:], op0=ALU.mult, op1=ALU.add)
            I("act", "activation", [bt], [bt], out=cs[:], in_=ca[:], func=AF.Sin)

            def lev(k, c):
                return apw[:, (k * 2 + c) * 32:(k * 2 + c + 1) * 32]

            I("dve", "tensor_tensor", [bt], [b_p], out=lev(0, 0), in0=er[:], in1=cs[:], op=ALU.mult)
            I("dve", "tensor_tensor", [bt], [b_p], out=lev(0, 1), in0=er[:], in1=sn[:], op=ALU.mult)
            for k in range(1, NLEV):
                pr_, pi_ = lev(k - 1, 0), lev(k - 1, 1)
                I("dve", "tensor_tensor", [b_p], [bt], out=t0[:], in0=pr_, in1=pr_, op=ALU.mult)
                I("dve", "tensor_tensor", [b_p], [bt], out=t1[:], in0=pi_, in1=pi_, op=ALU.mult)
                I("dve", "tensor_tensor", [bt], [b_p], out=lev(k, 0), in0=t0[:], in1=t1[:], op=ALU.subtract)
                I("dve", "scalar_tensor_tensor", [b_p], [b_p], out=lev(k, 1), in0=pr_, scalar=2.0, in1=pi_,
                  op0=ALU.mult, op1=ALU.mult)
            I("dve", "tensor_scalar", [b_p], [b_p], out=negl[:].rearrange("p (k g) -> p k g", k=NLEV),
              in0=apw[:].rearrange("p (k c g) -> p k c g", k=NLEV, c=2)[:, :, 1, :], scalar1=-1.0, scalar2=None,
              op0=ALU.mult)
            ar, ai = lev(0, 0), lev(0, 1)
            I("dve", "tensor_tensor", [bt], [bt], out=t0[:], in0=lr[:], in1=lr[:], op=ALU.mult)
            I("dve", "tensor_tensor", [bt], [bt], out=t1[:], in0=li[:], in1=li[:], op=ALU.mult)
            I("dve", "tensor_tensor", [bt], [bt], out=den[:], in0=t0[:], in1=t1[:], op=ALU.add)
            I("dve", "reciprocal", [bt], [bt], out=den[:], in_=den[:])
            I("dve", "tensor_scalar", [b_p], [bt], out=t0[:], in0=ar, scalar1=-1.0, scalar2=None, op0=ALU.add)
            I("dve", "tensor_tensor", [bt], [bt], out=wr[:], in0=t0[:], in1=lr[:], op=ALU.mult)
            I("dve", "tensor_tensor", [bt, b_p], [bt], out=t1[:], in0=ai, in1=li[:], op=ALU.mult)
            I("dve", "tensor_tensor", [bt], [bt], out=wr[:], in0=wr[:], in1=t1[:], op=ALU.add)
            I("dve", "tensor_tensor", [bt], [bt], out=wr[:], in0=wr[:], in1=den[:], op=ALU.mult)
            I("dve", "tensor_tensor", [bt, b_p], [bt], out=wi[:], in0=ai, in1=lr[:], op=ALU.mult)
            I("dve", "tensor_tensor", [bt], [bt], out=t1[:], in0=t0[:], in1=li[:], op=ALU.mult)
            I("dve", "tensor_tensor", [bt], [bt], out=wi[:], in0=wi[:], in1=t1[:], op=ALU.subtract)
            I("dve", "tensor_tensor", [bt], [bt], out=wi[:], in0=wi[:], in1=den[:], op=ALU.mult)
            Br = sb("Br", [128, 512], F32)
            Bi = sb("Bi", [128, 512], F32)
            gm = sb("gm", [128, 512], F32)
            bb_ = Buf()
            for two in range(2):
                ps_ = slice(two * 64, (two + 1) * 64)
                for gp in range(32):
                    g = 2 * gp + two
                    DMA(Br[ps_, gp * 16:(gp + 1) * 16], b_re[i, g], [], [bb_])
                    DMA(Bi[ps_, gp * 16:(gp + 1) * 16], b_im[i, g], [], [bb_])
                DMA(gm[ps_, :].rearrange("p (gp i) -> p gp i", i=16),
                    mix_g[L].rearrange("(gp two i) -> two gp i", two=2, i=16)[two:two + 1].broadcast_to([64, 32, 16]),
                    [], [bb_], allow_slow_non_contiguous=True)
            Bbr = sb("Bbr", [128, 512], F32)
            Bbi = sb("Bbi", [128, 512], F32)
            tb = sb("tb", [128, 512], F32)
            Bb16 = sb("Bb16", [128, 2 * 512], BF16)
            wrb = wr[:].unsqueeze(2).broadcast_to([128, 32, 16])
            wib = wi[:].unsqueeze(2).broadcast_to([128, 32, 16])

            def v3(t_):
                return t_[:].rearrange("p (gp i) -> p gp i", i=16)

            I("dve", "tensor_tensor", [bt, bb_], [bb_], out=v3(Bbr), in0=v3(Br), in1=wrb, op=ALU.mult)
            I("dve", "tensor_tensor", [bt, bb_], [bb_], out=v3(tb), in0=v3(Bi), in1=wib, op=ALU.mult)
            I("dve", "tensor_tensor", [bb_], [bb_], out=Bbr[:], in0=Bbr[:], in1=tb[:], op=ALU.subtract)
            I("dve", "tensor_tensor", [bt, bb_], [bb_], out=v3(Bbi), in0=v3(Bi), in1=wrb, op=ALU.mult)
            I("dve", "tensor_tensor", [bt, bb_], [bb_], out=v3(tb), in0=v3(Br), in1=wib, op=ALU.mult)
            I("dve", "tensor_tensor", [bb_], [bb_], out=Bbi[:], in0=Bbi[:], in1=tb[:], op=ALU.add)
            I("dve", "tensor_tensor", [bb_], [bb_], out=Bb16[:, 0:512], in0=Bbr[:], in1=gm[:], op=ALU.mult)
            I("dve", "tensor_tensor", [bb_], [bb_], out=Bb16[:, 512:1024], in0=Bbi[:], in1=gm[:], op=ALU.mult)
            zt = sb("zt", [128, 128], BF16)
            bz = Buf()
            I("pool", "memset", [], [bz], zt[:], 0.0)
            bD = Buf()
            for gp in range(32):
                for c in range(2):
                    DMA(BTd[i, gp, c], zt[:], [bz], [bD])
            pT_ = Rot("pTp", 2, [128, 1024], BF16, psum=True)
            TB = Rot("TB", 2, [16, 256], BF16)
            for gp in range(32):
                p_, bp_ = pT_.next()
                for c in range(2):
                    I("pe", "transpose", [bb_, b_ident], [bp_], out=p_[0:16, c * 128:(c + 1) * 128],
                      in_=Bb16[:, c * 512 + gp * 16:c * 512 + (gp + 1) * 16], identity=ident[:])
                tbb, btb = TB.next()
                I("act", "activation", [bp_], [btb], out=tbb[:], in_=p_[0:16, 0:256], func=AF.Copy)
                for two in range(2):
                    g = 2 * gp + two
                    r0 = (g % 8) * 16
                    for c in range(2):
                        DMA(BTd[i, gp, c, r0:r0 + 16, two * 64:(two + 1) * 64],
                            tbb[:, c * 128 + two * 64:c * 128 + (two + 1) * 64], [btb, bD], [])
            Cr = sb("Cr", [128, 512], F32)
            Ci = sb("Ci", [128, 512], F32)
            bc_ = Buf()
            for two in range(2):
                ps_ = slice(two * 64, (two + 1) * 64)
                for gp in range(32):
                    g = 2 * gp + two
                    DMA(Cr[ps_, gp * 16:(gp + 1) * 16], c_re[i, g].rearrange("o p -> p o"), [], [bc_],
                        allow_slow_non_contiguous=True)
                    DMA(Ci[ps_, gp * 16:(gp + 1) * 16], c_im[i, g].rearrange("o p -> p o"), [], [bc_],
                        allow_slow_non_contiguous=True)
            I("pool", "memset", [], [b_p], CPr[:], 0.0)
            I("pool", "memset", [], [b_p], CPi[:], 0.0)
            for two in range(2):
                ps_ = slice(two * 64, (two + 1) * 64)
                I("dve", "tensor_copy", [bc_], [b_p],
                  out=CPr[ps_, :].rearrange("p (gp t o) -> p gp t o", t=2, o=16)[:, :, two, :],
                  in_=Cr[ps_, :].rearrange("p (gp o) -> p gp o", o=16))
                I("dve", "tensor_scalar", [bc_], [b_p],
                  out=CPi[ps_, :].rearrange("p (gp t o) -> p gp t o", t=2, o=16)[:, :, two, :],
                  in0=Ci[ps_, :].rearrange("p (gp o) -> p gp o", o=16), scalar1=-1.0, scalar2=None, op0=ALU.mult)
            phase_end()
            return apw, CPr, CPi, b_p, negl

        def s5_pre(src):
            phase_begin()
            xt = Rot("xt", 2, [128, 1024], F32)
            xn = Rot("xn", 2, [128, 1024], BF16)
            uT = Rot("uT", 2, [128, 8 * 512], BF16)
            junk = sb("junk", [128, 1024], F32)
            b_junk = Buf()
            st1 = Rot("st1", 4, [128, 1], F32)
            ps_t = Rot("ps_t", 2, [128, 1024], BF16, psum=True)
            for t5 in range(NT5):
                uTi, buT = uT.next()
                for s_ in range(4):
                    t = t5 * 4 + s_
                    xti, bxt = xt.next()
                    DMA(xti[:], src[t * 128:(t + 1) * 128, :], [], [bxt])
                    st, bst = st1.next()
                    I("act", "activation", [bxt], [b_junk, bst], out=junk[:], in_=xti[:], func=AF.Square,
                      accum_out=st[:, 0:1])
                    rstd_cols(st[:, 0:1], bst, 1024)
                    xnq, bxn = xn.next()
                    I("dve", "tensor_scalar", [bxt, bst], [bxn], out=xnq[:], in0=xti[:], scalar1=st[:, 0:1],
                      scalar2=None, op0=ALU.mult)
                    pst, bpst = ps_t.next()
                    for k in range(8):
                        I("pe", "transpose", [bxn, b_ident], [bpst], out=pst[:, k * 128:(k + 1) * 128],
                          in_=xnq[:, k * 128:(k + 1) * 128], identity=ident[:])
                    I("act", "activation", [bpst], [buT],
                      out=uTi[:].rearrange("p (k t) -> p k t", k=8)[:, :, s_ * 128:(s_ + 1) * 128],
                      in_=pst[:].rearrange("p (k t) -> p k t", k=8), func=AF.Copy)
                DMA(uTd[:, :, t5 * 512:(t5 + 1) * 512].rearrange("k p t -> p k t"),
                    uTi[:].rearrange("p (k t) -> p k t", k=8), [buT], [])
            phase_end()

        def s5_main(i, apw, CPr, CPi, b_p, sb_negl):
            phase_begin()
            PFX = 256
            W = PFX + 512
            uT = Rot("uTf", 2, [128, 512], BF16)
            BT = Rot("BTs", 2, [128, 4 * 2 * 128], BF16)
            bufs = [[sb("sc%d%d" % (a, c), [128, W], F32) for c in range(2)] for a in range(2)]
            bsc = [Buf(), Buf()]
            for a in range(2):
                for c in range(2):
                    I("pool", "memset", [], [bsc[a]], bufs[a][c][:], 0.0)
            carry = sb("carry", [128, 32 * nseq * 2], F32)
            b_car = Buf()
            I("pool", "memset", [], [b_car], carry[:], 0.0)
            HSb = Rot("HSb", 2, [128, 2 * 512], BF16)
            psb = Rot("psb", 4, [128, 512], F32, psum=True)
            psy = Rot("psy", 2, [128, 512], F32, psum=True)
            ysb = Rot("ysb", 2, [128, 512], F32)

            def lev(k, c, gp):
                return apw[:, (k * 2 + c) * 32 + gp:(k * 2 + c) * 32 + gp + 1]

            for fc in range(8):
                BTi, bBT = BT.next()
                DMA(BTi[:].rearrange("p (a c q) -> p a c q", a=4, c=2),
                    BTd[i, fc * 4:(fc + 1) * 4].rearrange("a c p q -> p a c q"), [], [bBT])
                for seq in range(nseq):
                    for blk in range(8):
                        t0_ = seq * SEQ + blk * 512
                        uTi, buT = uT.next()
                        DMA(uTi[:], uTd[fc, :, t0_:t0_ + 512], [], [buT])
                        py, bpy = psy.next()
                        firsty = True
                        for pl in range(4):
                            gp = fc * 4 + pl
                            pbr, bpbr = psb.next()
                            pbi, bpbi = psb.next()
                            for c, (pb_, bpb_) in enumerate(((pbr, bpbr), (pbi, bpbi))):
                                I("pe", "matmul", [bBT, buT], [bpb_], pb_[:],
                                  lhsT=BTi[:, (pl * 2 + c) * 128:(pl * 2 + c + 1) * 128], rhs=uTi[:],
                                  start=True, stop=True)
                            A, Bf = bufs[0], bufs[1]
                            I("act", "activation", [bpbr], [bsc[0]], out=A[0][:, PFX:W], in_=pbr[:], func=AF.Copy)
                            I("act", "activation", [bpbi], [bsc[0]], out=A[1][:, PFX:W], in_=pbi[:], func=AF.Copy)
                            cc = (gp * nseq + seq) * 2
                            cr, ci = carry[:, cc:cc + 1], carry[:, cc + 1:cc + 2]
                            ar, ai = lev(0, 0, gp), lev(0, 1, gp)
                            nai = sb_negl[:, gp:gp + 1]
                            f = slice(PFX, PFX + 1)
                            rd = [bsc[0], b_car, b_p]
                            I("dve", "scalar_tensor_tensor", rd, [bsc[0]], out=A[0][:, f], in0=cr, scalar=ar,
                              in1=A[0][:, f], op0=ALU.mult, op1=ALU.add)
                            I("dve", "scalar_tensor_tensor", rd, [bsc[0]], out=A[0][:, f], in0=ci, scalar=nai,
                              in1=A[0][:, f], op0=ALU.mult, op1=ALU.add)
                            I("dve", "scalar_tensor_tensor", rd, [bsc[0]], out=A[1][:, f], in0=ci, scalar=ar,
                              in1=A[1][:, f], op0=ALU.mult, op1=ALU.add)
                            I("dve", "scalar_tensor_tensor", rd, [bsc[0]], out=A[1][:, f], in0=cr, scalar=ai,
                              in1=A[1][:, f], op0=ALU.mult, op1=ALU.add)
                            src_i = 0
                            for k in range(NLEV):
                                sh = 1 << k
                                X, Y = bufs[src_i], bufs[1 - src_i]
                                bx, by = bsc[src_i], bsc[1 - src_i]
                                kr_, ki_ = lev(k, 0, gp), lev(k, 1, gp)
                                nki = sb_negl[:, k * 32 + gp:k * 32 + gp + 1]
                                cur_ = slice(PFX, W)
                                shf = slice(PFX - sh, W - sh)
                                I("dve", "scalar_tensor_tensor", [bx, b_p], [by], out=Y[0][:, cur_], in0=X[0][:, shf],
                                  scalar=kr_, in1=X[0][:, cur_], op0=ALU.mult, op1=ALU.add)
                                I("dve", "scalar_tensor_tensor", [bx, b_p, by], [by], out=Y[0][:, cur_],
                                  in0=X[1][:, shf], scalar=nki, in1=Y[0][:, cur_], op0=ALU.mult, op1=ALU.add)
                                I("dve", "scalar_tensor_tensor", [bx, b_p], [by], out=Y[1][:, cur_], in0=X[1][:, shf],
                                  scalar=kr_, in1=X[1][:, cur_], op0=ALU.mult, op1=ALU.add)
                                I("dve", "scalar_tensor_tensor", [bx, b_p, by], [by], out=Y[1][:, cur_],
                                  in0=X[0][:, shf], scalar=ki_, in1=Y[1][:, cur_], op0=ALU.mult, op1=ALU.add)
                                src_i = 1 - src_i
                            R_, bR = bufs[src_i], bsc[src_i]
                            I("pool", "tensor_copy", [bR], [b_car], out=carry[:, cc:cc + 1], in_=R_[0][:, W - 1:W])
                            I("pool", "tensor_copy", [bR], [b_car], out=carry[:, cc + 1:cc + 2], in_=R_[1][:, W - 1:W])
                            hs, bhs = HSb.next()
                            I("act", "activation", [bR], [bhs], out=hs[:, 0:512], in_=R_[0][:, PFX:W], func=AF.Copy)
                            I("pool", "tensor_copy", [bR], [bhs], out=hs[:, 512:1024], in_=R_[1][:, PFX:W])
                            for s_ in range(4):
                                for c, CP in enumerate((CPr, CPi)):
                                    I("pe", "matmul", [bhs, b_p], [bpy],
                                      py[:, s_ * 128 + pl * 32:s_ * 128 + (pl + 1) * 32],
                                      lhsT=hs[:, c * 512 + s_ * 128:c * 512 + (s_ + 1) * 128],
                                      rhs=CP[:, gp * 32:(gp + 1) * 32], start=firsty, stop=False,
                                      skip_group_check=True)
                                    firsty = False
                        yo, byo = ysb.next()
                        I("act", "activation", [bpy], [byo], out=yo[:], in_=py[:], func=AF.Copy)
                        DMA(yd[t0_:t0_ + 512, fc * 128:(fc + 1) * 128].rearrange("(s p) c -> p s c", p=128),
                            yo[:].rearrange("p (s c) -> p s c", s=4), [byo], [])
            phase_end()

        def s5_post(i, L, src, dst):
            phase_begin()
            wg = sb("wg", [128, 8 * 2048], BF16)
            b_wg = Buf()
            for k in range(4):
                DMA(wg[:, k * 4096:(k + 1) * 4096], wgluc[i][:, k * 4096:(k + 1) * 4096], [], [b_wg])
            dp = sb("dp", [128, 1024], F32)
            gmb = sb("gmb", [128, 1024], F32)
            b_dp = Buf()
            DMA(dp[:], d_skip[i:i + 1, :].broadcast_to([128, 1024]), [], [b_dp], allow_slow_non_contiguous=True)
            DMA(gmb[:], mix_g[L:L + 1, :].broadcast_to([128, 1024]), [], [b_dp], allow_slow_non_contiguous=True)
            I("dve", "tensor_tensor", [b_dp], [b_dp], out=dp[:], in0=dp[:], in1=gmb[:], op=ALU.mult)
            xt = Rot("xt", 6, [128, 1024], F32)
            yt = Rot("yt", 4, [128, 1024], F32)
            u_ = Rot("u", 3, [128, 1024], F32)
            w_ = Rot("w", 3, [128, 1024], F32)
            gb = Rot("gb", 3, [128, 1024], BF16)
            gT = Rot("gT", 3, [128, 1024], BF16)
            sg = Rot("sg", 2, [128, 1024], F32)
            junk = sb("junk", [128, 1024], F32)
            b_junk = Buf()
            st1 = Rot("st1", 6, [128, 1], F32)
            ps_t = Rot("ps_t", 2, [128, 1024], BF16, psum=True)
            pm = Rot("pm", 6, [128, 512], F32, psum=True)
            T_ = {}

            def sA(t):
                rows = slice(t * 128, (t + 1) * 128)
                xti, bxt = xt.next()
                yti, byt = yt.next()
                DMA(xti[:], src[rows, :], [], [bxt])
                DMA(yti[:], yd[rows, :], [], [byt])
                st, bst = st1.next()
                I("act", "activation", [bxt], [b_junk, bst], out=junk[:], in_=xti[:], func=AF.Square,
                  accum_out=st[:, 0:1])
                rstd_cols(st[:, 0:1], bst, 1024)
                ui, bu = u_.next()
                I("dve", "scalar_tensor_tensor", [bxt, bst, b_dp], [bu], out=ui[:], in0=xti[:], scalar=st[:, 0:1],
                  in1=dp[:], op0=ALU.mult, op1=ALU.mult)
                T_[t] = dict(x=(xti, bxt), y=(yti, byt), u=(ui, bu))

            def sB(t):
                d = T_[t]
                yti, byt = d["y"]
                ui, bu = d["u"]
                I("pool", "tensor_tensor", [bu, byt], [byt], out=yti[:], in0=yti[:], in1=ui[:], op=ALU.add)
                wi_, bw = w_.next()
                I("act", "activation", [byt], [bw], out=wi_[:], in_=yti[:], func=AF.Square)
                I("dve", "tensor_scalar", [bw], [bw], out=wi_[:], in0=wi_[:], scalar1=0.044715, scalar2=1.0,
                  op0=ALU.mult, op1=ALU.add)
                I("pool", "tensor_tensor", [bw, byt], [bw], out=wi_[:], in0=wi_[:], in1=yti[:], op=ALU.mult)
                I("act", "activation", [bw], [bw], out=wi_[:], in_=wi_[:], func=AF.Sigmoid, scale=1.5957691216057308)
                gbi, bgb = gb.next()
                I("dve", "tensor_tensor", [bw, byt], [bgb], out=gbi[:], in0=wi_[:], in1=yti[:], op=ALU.mult)
                d["g"] = (gbi, bgb)

            def sC(t):
                d = T_[t]
                gbi, bgb = d["g"]
                pst, bpst = ps_t.next()
                for k in range(8):
                    I("pe", "transpose", [bgb, b_ident], [bpst], out=pst[:, k * 128:(k + 1) * 128],
                      in_=gbi[:, k * 128:(k + 1) * 128], identity=ident[:])
                gTi, bgT = gT.next()
                I("act", "activation", [bpst], [bgT], out=gTi[:], in_=pst[:], func=AF.Copy)
                pvd = {}
                for c in (2, 3, 0, 1):
                    p_, bp_ = pm.next()
                    for k in range(8):
                        I("pe", "matmul", [bgT, b_wg], [bp_], p_[:], lhsT=gTi[:, k * 128:(k + 1) * 128],
                          rhs=wg[:, k * 2048 + c * 512:k * 2048 + (c + 1) * 512], start=(k == 0), stop=(k == 7))
                    pvd[c] = (p_, bp_)
                d["pv"] = pvd

            def sD(t):
                rows = slice(t * 128, (t + 1) * 128)
                d = T_.pop(t)
                pvd = d["pv"]
                xti, bxt = d["x"]
                sgi, bsg = sg.next()
                for c in range(2):
                    cs_ = slice(c * 512, (c + 1) * 512)
                    I("act", "activation", [pvd[2 + c][1]], [bsg], out=sgi[:, cs_], in_=pvd[2 + c][0][:], func=AF.Sigmoid)
                    I("dve", "tensor_tensor", [pvd[c][1], bsg], [bsg], out=sgi[:, cs_], in0=pvd[c][0][:],
                      in1=sgi[:, cs_], op=ALU.mult)
                    I("pool", "tensor_tensor", [bsg, bxt], [bxt], out=xti[:, cs_], in0=sgi[:, cs_], in1=xti[:, cs_],
                      op=ALU.add)
                DMA(dst[rows, :], xti[:], [bxt], [], q="pool")

            skew([sA, sB, sC, sD], NT)
            phase_end()

        def s5_prep2(i, L):
            apw = sb("apw", [128, NLEV * 2 * 32], F32)
            negl = sb("negl", [128, NLEV * 32], F32)
            CPLr = sb("CPLr", [128, 8 * 1024], BF16)
            CPLi = sb("CPLi", [128, 8 * 1024], BF16)
            b_p = Buf()
            phase_begin()
            lr = sb("lr", [128, 32], F32)
            li = sb("li", [128, 32], F32)
            dt = sb("dt", [128, 32], F32)
            bt = Buf()
            for two in range(2):
                ps_ = slice(two * 64, (two + 1) * 64)
                DMA(lr[ps_, :], lam_re[i].rearrange("(gp two) p -> two p gp", two=2)[two], [], [bt],
                    allow_slow_non_contiguous=True)
                DMA(li[ps_, :], lam_im[i].rearrange("(gp two) p -> two p gp", two=2)[two], [], [bt],
                    allow_slow_non_contiguous=True)
                DMA(dt[ps_, :], log_dt[i].rearrange("(gp two) -> two gp", two=2)[two:two + 1, :].broadcast_to([64, 32]),
                    [], [bt], allow_slow_non_contiguous=True)
            tl = [sb("t%d" % k, [128, 32], F32) for k in range(12)]
            er, ang, kk, sn, cs, ca, m2, t0, t1, wr, wi, den = tl
            APD = sb("APD", [128, 9 * 2 * 32], F32)

            def apd(d, c):
                return APD[:, (d * 2 + c) * 32:(d * 2 + c + 1) * 32]

            def V(name, out_, **kw):
                I("dve", name, [bt], [bt], out=out_, **kw)

            I("act", "activation", [bt], [bt], out=dt[:], in_=dt[:], func=AF.Exp)
            V("tensor_tensor", er[:], in0=lr[:], in1=dt[:], op=ALU.mult)
            I("act", "activation", [bt], [bt], out=er[:], in_=er[:], func=AF.Exp)
            V("tensor_tensor", ang[:], in0=li[:], in1=dt[:], op=ALU.mult)
            I("pool", "memset", [bt], [bt], kk[:], 0.0)
            for m in range(1, 5):
                V("scalar_tensor_tensor", kk[:], in0=ang[:], scalar=(2 * m - 1) * PI, in1=kk[:], op0=ALU.is_gt,
                  op1=ALU.add)
            V("scalar_tensor_tensor", ang[:], in0=kk[:], scalar=-2 * PI, in1=ang[:], op0=ALU.mult, op1=ALU.add)
            I("act", "activation", [bt], [bt], out=sn[:], in_=ang[:], func=AF.Sin)
            V("tensor_scalar", ca[:], in0=ang[:], scalar1=PI / 2, scalar2=None, op0=ALU.add)
            V("tensor_scalar", m2[:], in0=ca[:], scalar1=PI, scalar2=None, op0=ALU.is_gt)
            V("scalar_tensor_tensor", ca[:], in0=m2[:], scalar=-2 * PI, in1=ca[:], op0=ALU.mult, op1=ALU.add)
            I("act", "activation", [bt], [bt], out=cs[:], in_=ca[:], func=AF.Sin)
            I("pool", "memset", [bt], [bt], apd(0, 0), 1.0)
            I("pool", "memset", [bt], [bt], apd(0, 1), 0.0)
            V("tensor_tensor", apd(1, 0), in0=er[:], in1=cs[:], op=ALU.mult)
            V("tensor_tensor", apd(1, 1), in0=er[:], in1=sn[:], op=ALU.mult)
            ar, ai = apd(1, 0), apd(1, 1)
            for d in range(2, 9):
                V("tensor_tensor", t0[:], in0=apd(d - 1, 0), in1=ar, op=ALU.mult)
                V("tensor_tensor", t1[:], in0=apd(d - 1, 1), in1=ai, op=ALU.mult)
                V("tensor_tensor", apd(d, 0), in0=t0[:], in1=t1[:], op=ALU.subtract)
                V("tensor_tensor", t0[:], in0=apd(d - 1, 0), in1=ai, op=ALU.mult)
                V("tensor_tensor", t1[:], in0=apd(d - 1, 1), in1=ar, op=ALU.mult)
                V("tensor_tensor", apd(d, 1), in0=t0[:], in1=t1[:], op=ALU.add)

            def lev(k, c):
                return apw[:, (k * 2 + c) * 32:(k * 2 + c + 1) * 32]

            I("dve", "tensor_copy", [bt], [b_p], out=lev(0, 0), in_=apd(8, 0))
            I("dve", "tensor_copy", [bt], [b_p], out=lev(0, 1), in_=apd(8, 1))
            for k in range(1, NLEV):
                pr_, pi_ = lev(k - 1, 0), lev(k - 1, 1)
                I("dve", "tensor_tensor", [b_p], [bt], out=t0[:], in0=pr_, in1=pr_, op=ALU.mult)
                I("dve", "tensor_tensor", [b_p], [bt], out=t1[:], in0=pi_, in1=pi_, op=ALU.mult)
                I("dve", "tensor_tensor", [bt], [b_p], out=lev(k, 0), in0=t0[:], in1=t1[:], op=ALU.subtract)
                I("dve", "scalar_tensor_tensor", [b_p], [b_p], out=lev(k, 1), in0=pr_, scalar=2.0, in1=pi_,
                  op0=ALU.mult, op1=ALU.mult)
            I("dve", "tensor_scalar", [b_p], [b_p], out=negl[:].rearrange("p (k g) -> p k g", k=NLEV),
              in0=apw[:].rearrange("p (k c g) -> p k c g", k=NLEV, c=2)[:, :, 1, :], scalar1=-1.0, scalar2=None,
              op0=ALU.mult)
            V("tensor_tensor", t0[:], in0=lr[:], in1=lr[:], op=ALU.mult)
            V("tensor_tensor", t1[:], in0=li[:], in1=li[:], op=ALU.mult)
            V("tensor_tensor", den[:], in0=t0[:], in1=t1[:], op=ALU.add)
            V("reciprocal", den[:], in_=den[:])
            V("tensor_scalar", t0[:], in0=ar, scalar1=-1.0, scalar2=None, op0=ALU.add)
            V("tensor_tensor", wr[:], in0=t0[:], in1=lr[:], op=ALU.mult)
            V("tensor_tensor", t1[:], in0=ai, in1=li[:], op=ALU.mult)
            V("tensor_tensor", wr[:], in0=wr[:], in1=t1[:], op=ALU.add)
            V("tensor_tensor", wr[:], in0=wr[:], in1=den[:], op=ALU.mult)
            V("tensor_tensor", wi[:], in0=ai, in1=lr[:], op=ALU.mult)
            V("tensor_tensor", t1[:], in0=t0[:], in1=li[:], op=ALU.mult)
            V("tensor_tensor", wi[:], in0=wi[:], in1=t1[:], op=ALU.subtract)
            V("tensor_tensor", wi[:], in0=wi[:], in1=den[:], op=ALU.mult)
            Br = sb("Br", [128, 512], F32)
            Bi = sb("Bi", [128, 512], F32)
            gm = sb("gm", [128, 512], F32)
            bb_ = Buf()
            for two in range(2):
                ps_ = slice(two * 64, (two + 1) * 64)
                for gp in range(32):
                    g = 2 * gp + two
                    DMA(Br[ps_, gp * 16:(gp + 1) * 16], b_re[i, g], [], [bb_])
                    DMA(Bi[ps_, gp * 16:(gp + 1) * 16], b_im[i, g], [], [bb_])
                DMA(gm[ps_, :].rearrange("p (gp i) -> p gp i", i=16),
                    mix_g[L].rearrange("(gp two i) -> two gp i", two=2, i=16)[two:two + 1].broadcast_to([64, 32, 16]),
                    [], [bb_], allow_slow_non_contiguous=True)
            Bd = [[sb("Bd%d%d" % (a_, c), [128, 512], F32) for c in range(2)] for a_ in range(2)]
            tb = sb("tb", [128, 512], F32)
            tb2 = sb("tb2", [128, 512], F32)
            BbA = sb("BbA", [128, 8 * 2 * 1024], BF16)
            bA = Buf()
            I("pool", "memset", [], [bA], BbA[:], 0.0)

            def v3(t_):
                return t_[:].rearrange("p (gp i) -> p gp i", i=16)

            def bc(ap32):
                return ap32.unsqueeze(2).broadcast_to([128, 32, 16])

            rd = [bt, bb_]
            I("dve", "tensor_tensor", rd, [bb_], out=v3(tb), in0=v3(Br), in1=bc(wr[:]), op=ALU.mult)
            I("dve", "tensor_tensor", rd, [bb_], out=v3(tb2), in0=v3(Bi), in1=bc(wi[:]), op=ALU.mult)
            I("dve", "tensor_tensor", [bb_], [bb_], out=tb[:], in0=tb[:], in1=tb2[:], op=ALU.subtract)
            I("dve", "tensor_tensor", [bb_], [bb_], out=Bd[0][0][:], in0=tb[:], in1=gm[:], op=ALU.mult)
            I("dve", "tensor_tensor", rd, [bb_], out=v3(tb), in0=v3(Bi), in1=bc(wr[:]), op=ALU.mult)
            I("dve", "tensor_tensor", rd, [bb_], out=v3(tb2), in0=v3(Br), in1=bc(wi[:]), op=ALU.mult)
            I("dve", "tensor_tensor", [bb_], [bb_], out=tb[:], in0=tb[:], in1=tb2[:], op=ALU.add)
            I("dve", "tensor_tensor", [bb_], [bb_], out=Bd[0][1][:], in0=tb[:], in1=gm[:], op=ALU.mult)
            BA5 = BbA[:].rearrange("p (d c gp t i) -> p d c gp t i", d=8, c=2, gp=32, t=2)
            for d in range(8):
                X = Bd[d % 2]
                Y = Bd[1 - d % 2]
                for c in range(2):
                    for two in range(2):
                        ps_ = slice(two * 64, (two + 1) * 64)
                        I("pool" if c else "act", "tensor_copy" if c else "copy", [bb_], [bA],
                          out=BbA[ps_, :].rearrange("p (d c gp t i) -> p d c gp t i", d=8, c=2, gp=32, t=2)[:, d, c, :, two, :],
                          in_=X[c][ps_, :].rearrange("p (gp i) -> p gp i", i=16))
                if d < 7:
                    I("dve", "tensor_tensor", rd, [bb_], out=v3(tb), in0=v3(X[0]), in1=bc(ar), op=ALU.mult)
                    I("dve", "tensor_tensor", rd, [bb_], out=v3(tb2), in0=v3(X[1]), in1=bc(ai), op=ALU.mult)
                    I("dve", "tensor_tensor", [bb_], [bb_], out=Y[0][:], in0=tb[:], in1=tb2[:], op=ALU.subtract)
                    I("dve", "tensor_tensor", rd, [bb_], out=v3(tb), in0=v3(X[0]), in1=bc(ai), op=ALU.mult)
                    I("dve", "tensor_tensor", rd, [bb_], out=v3(tb2), in0=v3(X[1]), in1=bc(ar), op=ALU.mult)
                    I("dve", "tensor_tensor", [bb_], [bb_], out=Y[1][:], in0=tb[:], in1=tb2[:], op=ALU.add)
            pT_ = Rot("pTp", 2, [128, 1024], BF16, psum=True)
            TB = Rot("TB", 3, [32, 1024], BF16)
            for gp in range(32):
                for c in range(2):
                    p_, bp_ = pT_.next()
                    for d in range(8):
                        off = ((d * 2 + c) * 32 + gp) * 32
                        I("pe", "transpose", [bA, b_ident], [bp_], out=p_[0:32, d * 128:(d + 1) * 128],
                          in_=BbA[:, off:off + 32], identity=ident[:])
                    tbb, btb = TB.next()
                    I("act", "activation", [bp_], [btb], out=tbb[:], in_=p_[0:32, :], func=AF.Copy)
                    DMA(BTd2[i, gp, :, c].rearrange("d r q -> r d q"), tbb[:].rearrange("r (d q) -> r d q", d=8),
                        [btb], [])
            Cr = sb("Cr", [128, 512], F32)
            Ci = sb("Ci", [128, 512], F32)
            bc_ = Buf()
            for two in range(2):
                ps_ = slice(two * 64, (two + 1) * 64)
                for gp in range(32):
                    g = 2 * gp + two
                    DMA(Cr[ps_, gp * 16:(gp + 1) * 16], c_re[i, g].rearrange("o p -> p o"), [], [bc_],
                        allow_slow_non_contiguous=True)
                    DMA(Ci[ps_, gp * 16:(gp + 1) * 16], c_im[i, g].rearrange("o p -> p o"), [], [bc_],
                        allow_slow_non_contiguous=True)
            CP0r = sb("CP0r", [128, 1024], BF16)
            CP0i = sb("CP0i", [128, 1024], BF16)
            b0 = Buf()
            I("pool", "memset", [], [b0], CP0r[:], 0.0)
            I("pool", "memset", [], [b0], CP0i[:], 0.0)
            I("pool", "memset", [], [b_p], CPLr[:], 0.0)
            I("pool", "memset", [], [b_p], CPLi[:], 0.0)
            for two in range(2):
                ps_ = slice(two * 64, (two + 1) * 64)
                I("dve", "tensor_copy", [bc_], [b0],
                  out=CP0r[ps_, :].rearrange("p (gp t o) -> p gp t o", t=2, o=16)[:, :, two, :],
                  in_=Cr[ps_, :].rearrange("p (gp o) -> p gp o", o=16))
                I("dve", "tensor_scalar", [bc_], [b0],
                  out=CP0i[ps_, :].rearrange("p (gp t o) -> p gp t o", t=2, o=16)[:, :, two, :],
                  in0=Ci[ps_, :].rearrange("p (gp o) -> p gp o", o=16), scalar1=-1.0, scalar2=None, op0=ALU.mult)
            rdc = [bt, bc_, bb_]
            for l in range(8):
                Ar, Ai = apd(l + 1, 0), apd(l + 1, 1)
                I("dve", "tensor_tensor", rdc, [bb_], out=v3(tb), in0=v3(Cr), in1=bc(Ar), op=ALU.mult)
                I("dve", "tensor_tensor", rdc, [bb_], out=v3(tb2), in0=v3(Ci), in1=bc(Ai), op=ALU.mult)
                I("dve", "tensor_tensor", [bb_], [bb_], out=tb[:], in0=tb[:], in1=tb2[:], op=ALU.subtract)
                for two in range(2):
                    ps_ = slice(two * 64, (two + 1) * 64)
                    I("pool", "tensor_copy", [bb_], [b_p],
                      out=CPLr[ps_, :].rearrange("p (l gp t o) -> p l gp t o", l=8, t=2, o=16)[:, l, :, two, :],
                      in_=tb[ps_, :].rearrange("p (gp o) -> p gp o", o=16))
                I("dve", "tensor_tensor", rdc, [bb_], out=v3(tb), in0=v3(Cr), in1=bc(Ai), op=ALU.mult)
                I("dve", "tensor_tensor", rdc, [bb_], out=v3(tb2), in0=v3(Ci), in1=bc(Ar), op=ALU.mult)
                I("dve", "tensor_tensor", [bb_], [bb_], out=tb[:], in0=tb[:], in1=tb2[:], op=ALU.add)
                for two in range(2):
                    ps_ = slice(two * 64, (two + 1) * 64)
                    I("pool", "tensor_scalar", [bb_], [b_p],
                      out=CPLi[ps_, :].rearrange("p (l gp t o) -> p l gp t o", l=8, t=2, o=16)[:, l, :, two, :],
                      in0=tb[ps_, :].rearrange("p (gp o) -> p gp o", o=16), scalar1=-1.0, scalar2=None,
                      op0=ALU.mult)
            pMa = Rot("pMa", 2, [128, 512], F32, psum=True)
            pMb = Rot("pMb", 2, [128, 512], F32, psum=True)
            Ml = Rot("Mlp", 2, [128, 1024], BF16)
            for fc in range(8):
                pa, bpa = pMa.next()
                pb, bpb = pMb.next()
                I("dve", "memset", [], [bpa], pa[:], 0.0)
                I("dve", "memset", [], [bpb], pb[:], 0.0)
                for pl in range(4):
                    gp = fc * 4 + pl
                    for d in range(8):
                        pk, bpk = (pa, bpa) if d < 4 else (pb, bpb)
                        for c in range(2):
                            off = ((d * 2 + c) * 32 + gp) * 32
                            CP0 = CP0r if c == 0 else CP0i
                            I("pe", "matmul", [bA, b0], [bpk],
                              pk[pl * 32:(pl + 1) * 32, (d % 4) * 128 + pl * 32:(d % 4) * 128 + (pl + 1) * 32],
                              lhsT=BbA[:, off:off + 32], rhs=CP0[:, gp * 32:(gp + 1) * 32],
                              start=(d % 4 == 0 and c == 0), stop=False, skip_group_check=True,
                              tile_position=(0, 32 * pl))
                mli, bml = Ml.next()
                I("act", "activation", [bpa], [bml], out=mli[:, 0:512], in_=pa[:], func=AF.Copy)
                I("act", "activation", [bpb], [bml], out=mli[:, 512:1024], in_=pb[:], func=AF.Copy)
                DMA(Mlagd[i, fc], mli[:], [bml], [])
            phase_end()
            return apw, negl, CPLr, CPLi, b_p

        def s5_pre2(src):
            phase_begin()
            xt = Rot("xt", 3, [128, 1024], F32)
            xn = Rot("xn", 3, [128, 1024], BF16)
            uT = Rot("uT", 3, [128, 8 * 8 * 64], BF16)
            junk = sb("junk", [128, 1024], F32)
            b_junk = Buf()
            st1 = Rot("st1", 6, [128, 1], F32)
            ps_t = Rot("ps_t", 3, [128, 1024], BF16, psum=True)
            T_ = {}
            U_ = {}

            def sA(t):
                xti, bxt = xt.next()
                DMA(xti[:], src[t * 128:(t + 1) * 128, :], [], [bxt])
                st, bst = st1.next()
                I("act", "activation", [bxt], [b_junk, bst], out=junk[:], in_=xti[:], func=AF.Square,
                  accum_out=st[:, 0:1])
                rstd_cols(st[:, 0:1], bst, 1024)
                xnq, bxn = xn.next()
                I("dve", "tensor_scalar", [bxt, bst], [bxn], out=xnq[:], in0=xti[:], scalar1=st[:, 0:1],
                  scalar2=None, op0=ALU.mult)
                T_[t] = (xnq, bxn)

            def sB(t):
                t5, s_ = divmod(t, 4)
                seq, blk = divmod(t5, 8)
                if s_ == 0:
                    U_[t5] = uT.next()
                uTi, buT = U_[t5]
                u4 = uTi[:].rearrange("p (k j n) -> p k j n", k=8, j=8)
                xnq, bxn = T_.pop(t)
                pst, bpst = ps_t.next()
                for k in range(8):
                    I("pe", "transpose", [bxn, b_ident], [bpst], out=pst[:, k * 128:(k + 1) * 128],
                      in_=xnq[:, k * 128:(k + 1) * 128], identity=ident[:])
                for j in range(8):
                    eng = "act" if j % 2 == 0 else "dve"
                    if eng == "act":
                        I("act", "activation", [bpst], [buT], out=u4[:, :, j, s_ * 16:(s_ + 1) * 16],
                          in_=pst[:].rearrange("p (k n j) -> p k n j", k=8, j=8)[:, :, :, j], func=AF.Copy)
                    else:
                        I("dve", "tensor_copy", [bpst], [buT], out=u4[:, :, j, s_ * 16:(s_ + 1) * 16],
                          in_=pst[:].rearrange("p (k n j) -> p k n j", k=8, j=8)[:, :, :, j])
                if s_ == 3:
                    for k in range(8):
                        DMA(uTd2[k, :, seq, :, blk * 64:(blk + 1) * 64], u4[:, k], [buT], [], q="pool")
                    U_.pop(t5)

            skew([sA, sB], NT)
            phase_end()

        def s5_main2(i, apw, sb_negl, CPLr, CPLi, b_p):
            phase_begin()
            PFX = 256
            W = PFX + 512
            uT = Rot("uTf", 2, [128, 8 * 512], BF16)
            BT = Rot("BTs", 2, [128, 8 * 2 * 128], BF16)
            Ml = Rot("Mls", 2, [128, 1024], BF16)
            bufs = [[sb("sc%d%d" % (a, c), [128, W], F32) for c in range(2)] for a in range(2)]
            bsc = [Buf(), Buf()]
            for a in range(2):
                for c in range(2):
                    I("pool", "memset", [], [bsc[a]], bufs[a][c][:], 0.0)
            Zb = Rot("Zb", 2, [128, 4 * 2 * 512], BF16)
            psb = Rot("psb", 4, [128, 512], F32, psum=True)
            psya = Rot("psya", 2, [128, 512], F32, psum=True)
            psyb = Rot("psyb", 2, [128, 512], F32, psum=True)
            ysb = Rot("ysb", 2, [128, 1024], F32)

            def lev(k, c, gp):
                return apw[:, (k * 2 + c) * 32 + gp:(k * 2 + c) * 32 + gp + 1]

            for fc in range(8):
                BTi, bBT = BT.next()
                for pl in range(4):
                    DMA(BTi[pl * 32:(pl + 1) * 32, :].rearrange("p (d c q) -> p d c q", d=8, c=2),
                        BTd2[i, fc * 4 + pl].rearrange("d c r q -> r d c q"), [], [bBT])
                Mli, bMl = Ml.next()
                DMA(Mli[:], Mlagd[i, fc], [], [bMl])
                for seq in range(nseq):
                    uTi, buT = uT.next()
                    DMA(uTi[:], uTd2[fc, :, seq].rearrange("p j n -> p (j n)"), [], [buT])
                    Zi, bZ = Zb.next()
                    for pl in range(4):
                        gp = fc * 4 + pl
                        rows = slice(pl * 32, (pl + 1) * 32)
                        pcs = []
                        for c in range(2):
                            pb_, bpb_ = psb.next()
                            for j in range(8):
                                d = 7 - j
                                I("pe", "matmul", [bBT, buT], [bpb_], pb_[:],
                                  lhsT=BTi[rows, (d * 2 + c) * 128:(d * 2 + c + 1) * 128],
                                  rhs=uTi[rows, j * 512:(j + 1) * 512], start=(j == 0), stop=(j == 7),
                                  tile_position=(32 * pl, 0))
                            pcs.append((pb_, bpb_))
                        A = bufs[0]
                        I("act", "activation", [pcs[0][1]], [bsc[0]], out=A[0][:, PFX:W], in_=pcs[0][0][:], func=AF.Copy)
                        I("act", "activation", [pcs[1][1]], [bsc[0]], out=A[1][:, PFX:W], in_=pcs[1][0][:], func=AF.Copy)
                        src_i = 0
                        for k in range(NLEV):
                            sh = 1 << k
                            X, Y = bufs[src_i], bufs[1 - src_i]
                            bx, by = bsc[src_i], bsc[1 - src_i]
                            kr_, ki_ = lev(k, 0, gp), lev(k, 1, gp)
                            nki = sb_negl[:, k * 32 + gp:k * 32 + gp + 1]
                            cur_ = slice(PFX, W)
                            shf = slice(PFX - sh, W - sh)
                            I("dve", "scalar_tensor_tensor", [bx, b_p], [by], out=Y[0][:, cur_], in0=X[0][:, shf],
                              scalar=kr_, in1=X[0][:, cur_], op0=ALU.mult, op1=ALU.add)
                            I("dve", "scalar_tensor_tensor", [bx, b_p, by], [by], out=Y[0][:, cur_],
                              in0=X[1][:, shf], scalar=nki, in1=Y[0][:, cur_], op0=ALU.mult, op1=ALU.add)
                            I("dve", "scalar_tensor_tensor", [bx, b_p], [by], out=Y[1][:, cur_], in0=X[1][:, shf],
                              scalar=kr_, in1=X[1][:, cur_], op0=ALU.mult, op1=ALU.add)
                            I("dve", "scalar_tensor_tensor", [bx, b_p, by], [by], out=Y[1][:, cur_],
                              in0=X[0][:, shf], scalar=ki_, in1=Y[1][:, cur_], op0=ALU.mult, op1=ALU.add)
                            src_i = 1 - src_i
                        R_, bR = bufs[src_i], bsc[src_i]
                        I("act", "activation", [bR], [bZ], out=Zi[:, (pl * 2) * 512:(pl * 2 + 1) * 512],
                          in_=R_[0][:, PFX - 1:W - 1], func=AF.Copy)
                        I("pool", "tensor_copy", [bR], [bZ], out=Zi[:, (pl * 2 + 1) * 512:(pl * 2 + 2) * 512],
                          in_=R_[1][:, PFX - 1:W - 1])
                    for nb in range(4):
                        pa, bpa = psya.next()
                        pb, bpb = psyb.next()
                        ncol = slice(nb * 128, (nb + 1) * 128)
                        for j in range(8):
                            lhs = uTi[:, j * 512 + nb * 128:j * 512 + (nb + 1) * 128]
                            if j <= 3:
                                I("pe", "matmul", [buT, bMl], [bpa], pa[:, j * 128:512], lhsT=lhs,
                                  rhs=Mli[:, 0:(4 - j) * 128], start=(j == 0), stop=False, skip_group_check=True)
                                I("pe", "matmul", [buT, bMl], [bpb], pb[:, 0:512], lhsT=lhs,
                                  rhs=Mli[:, (4 - j) * 128:(8 - j) * 128], start=(j == 0), stop=False,
                                  skip_group_check=True)
                            else:
                                I("pe", "matmul", [buT, bMl], [bpb], pb[:, (j - 4) * 128:512], lhsT=lhs,
                                  rhs=Mli[:, 0:(8 - j) * 128], start=False, stop=False, skip_group_check=True)
                        for pl in range(4):
                            gp = fc * 4 + pl
                            for l in range(8):
                                pk, bpk = (pa, bpa) if l < 4 else (pb, bpb)
                                for c, CPL in enumerate((CPLr, CPLi)):
                                    I("pe", "matmul", [bZ, b_p], [bpk],
                                      pk[:, (l % 4) * 128 + pl * 32:(l % 4) * 128 + (pl + 1) * 32],
                                      lhsT=Zi[:, (pl * 2 + c) * 512 + nb * 128:(pl * 2 + c) * 512 + (nb + 1) * 128],
                                      rhs=CPL[:, (l * 32 + gp) * 32:(l * 32 + gp + 1) * 32], start=False, stop=False,
                                      skip_group_check=True)
                        yo, byo = ysb.next()
                        I("act", "activation", [bpa], [byo], out=yo[:, 0:512], in_=pa[:], func=AF.Copy)
                        I("act", "activation", [bpb], [byo], out=yo[:, 512:1024], in_=pb[:], func=AF.Copy)
                        r0 = seq * SEQ + nb * 1024
                        DMA(yd[r0:r0 + 1024, fc * 128:(fc + 1) * 128].rearrange("(n l) c -> n l c", l=8),
                            yo[:].rearrange("p (l c) -> p l c", l=8), [byo], [], q="act")
            phase_end()

        prep_phase()
        cur = x_in
        for li, L in enumerate(layers):
            last = (li == len(layers) - 1)
            if do_mix:
                mdst = xm if do_ffn or not last else out
                if L % 2 == 0:
                    ph = cfg.get("ph", "abc")
                    if "a" in ph:
                        attn_phase_a(L // 2, L, cur)
                    if "b" in ph:
                        attn_phase_b()
                    if "c" in ph:
                        attn_phase_c(L // 2, cur, mdst)
                    cur = mdst
                else:
                    i = L // 2
                    if cfg.get("s5v", 2) == 1:
                        apw, CPr, CPi, b_p, negl = s5_prep(i, L)
                        s5_pre(cur)
                        s5_main(i, apw, CPr, CPi, b_p, negl)
                    else:
                        sph = cfg.get("s5ph", "prmo")
                        phase_begin()
                        apw, negl, CPLr, CPLi, b_p = s5_prep2(i, L)
                        if "r" in sph:
                            s5_pre2(cur)
                        if "m" in sph:
                            s5_main2(i, apw, negl, CPLr, CPLi, b_p)
                        phase_end()
                    if "o" in cfg.get("s5ph", "prmo"):
                        s5_post(i, L, cur, mdst)
                    cur = mdst
            if do_ffn:
                dst = out if last else xa
                ffn_half(L, 0, cur, xh)
                ffn_half(L, 1, xh, dst)
                cur = dst
        with nc.Block() as block:
            S.emit(nc, block)
    return nc


def _consts():
    ident = np.eye(128, dtype=np.float32).astype(ml_dtypes.bfloat16)
    tri = (np.arange(128)[:, None] <= np.arange(128)[None, :]).astype(np.float32).astype(ml_dtypes.bfloat16)
    half = 16
    inv = (np.float32(10000.0) ** (-np.arange(half, dtype=np.float32) / np.float32(half))).astype(np.float32)
    ang = (np.arange(SEQ, dtype=np.float32)[:, None] * inv[None, :]).astype(np.float32)
    return {"ident": ident, "trimask": tri, "rope_cos": np.cos(ang).astype(np.float32),
            "rope_sin": np.sin(ang).astype(np.float32)}


PARAM_KEYS = ["mix_norm_g", "ffn_norm_g", "w_in", "g_cq", "w_uq", "g_ckv", "w_ukv", "g_qn_mla", "g_kn_mla",
              "g_qn_moba", "g_kn_moba", "w_o", "lam_re", "lam_im", "log_dt", "b_re", "b_im", "c_re", "c_im",
              "d_skip", "w_glu", "w_ff1", "w_ff2"]


def kernel(**inputs):
    nc = build({})
    consts = _consts()
    x = np.ascontiguousarray(inputs["x"], dtype=np.float32).reshape(NCORES, TOK, D)
    params = {k: np.ascontiguousarray(inputs[k], dtype=np.float32) for k in PARAM_KEYS}
    in_maps = []
    for c in range(NCORES):
        m = {"x": x[c]}
        m.update(params)
        m.update(consts)
        in_maps.append(m)
    res = run_bass_kernel_spmd(nc, in_maps, core_ids=list(range(NCORES)))
    outs = [np.asarray(r["out"]) for r in res.results]
    return np.stack(outs, 0).reshape(16, SEQ, D).astype(np.float32)
```

```python
import numpy as np
import ml_dtypes
import concourse.bass as bass
import concourse.mybir as mybir
from concourse.bass_utils import run_bass_kernel_spmd

F32 = mybir.dt.float32
BF16 = mybir.dt.bfloat16
ALU = mybir.AluOpType
AF = mybir.ActivationFunctionType
AX = mybir.AxisListType

NCORES = 8
D = 1024
SEQ = 4096
NSEQ = 2
TOK = NSEQ * SEQ
DFF = 4096
DEPTH = 4
EPS = 1e-6
IN_COLS = 1952

SAME_ENGINE_SYNC = True
N_DSEM = 40


class Buf:
    __slots__ = ("name", "lastw", "lastr", "dmar")

    def __init__(self, name=""):
        self.name = name
        self.lastw = {}
        self.lastr = {}
        self.dmar = []


class Op:
    __slots__ = ("eng", "fn", "deps", "is_dma", "signal", "sigval", "dsem", "dval", "idx")

    def __init__(self, eng, fn, is_dma):
        self.eng = eng
        self.fn = fn
        self.is_dma = is_dma
        self.deps = []
        self.signal = False
        self.sigval = None
        self.dsem = None
        self.dval = None


ENGS = ["pe", "act", "dve", "pool", "sp"]


class Sched:
    def __init__(self):
        self.ops = {e: [] for e in ENGS}
        self.n_dma = 0
        self.pending = {}
        self.dma_since = []

    def barrier(self):
        deps = [self.ops[e][-1] for e in ENGS if self.ops[e] and not self.ops[e][-1].is_dma]
        for e in ENGS:
            for o in reversed(self.ops[e]):
                if not o.is_dma:
                    if o not in deps:
                        deps.append(o)
                    break
        deps = deps + list(self.dma_since)
        self.dma_since = []
        for e in ENGS:
            self.pending[e] = list(self.pending.get(e, [])) + deps

    def op(self, eng, fn, reads=(), writes=(), dma=False):
        o = Op(eng, fn, dma)
        deps = {}
        for b in reads:
            for w in b.lastw.values():
                deps[id(w)] = w
        for b in writes:
            for w in b.lastw.values():
                deps[id(w)] = w
            for r in b.lastr.values():
                deps[id(r)] = r
            for r in b.dmar:
                deps[id(r)] = r
        if eng in self.pending:
            for d in self.pending.pop(eng):
                deps[id(d)] = d
        if dma:
            self.dma_since.append(o)
        o.deps = list(deps.values())
        for b in reads:
            if dma:
                b.dmar.append(o)
            else:
                b.lastr[eng] = o
        for b in writes:
            b.lastw = {("dma%d" % id(o)) if dma else eng: o}
            b.lastr = {}
            b.dmar = []
        self.ops[eng].append(o)
        return o

    def emit(self, nc, block):
        for e in ENGS:
            for o in self.ops[e]:
                for d in o.deps:
                    if d.is_dma:
                        continue
                    if d.eng == o.eng and not o.is_dma:
                        if d.eng == "pe" or not SAME_ENGINE_SYNC:
                            continue
                    d.signal = True
        dcount = [0] * N_DSEM
        k = 0
        for e in ENGS:
            c = 0
            for o in self.ops[e]:
                if o.is_dma:
                    pass
                elif o.signal:
                    c += 1
                    o.sigval = c
        sems = {e: nc.alloc_semaphore(name="sem_" + e) for e in ["pe", "act", "dve", "pool"]}
        dsems = [nc.alloc_semaphore(name="dsem%d" % i) for i in range(N_DSEM)]
        pools = {"sp": (0, 24), "act": (24, 8), "pool": (32, 8), "dve": (32, 8), "pe": (32, 8)}
        for e in ENGS:
            base, n = pools[e]
            k = 0
            for o in self.ops[e]:
                if o.is_dma:
                    o.dsem = base + k % n
                    dcount[o.dsem] += 16
                    o.dval = dcount[o.dsem]
                    k += 1
        self.final_dma = [(i, dcount[i]) for i in range(N_DSEM) if dcount[i] > 0]

        def run_engine(ename, eng):
            waited = {}

            def wait(key, sem, val):
                if waited.get(key, 0) >= val:
                    return
                eng.wait_ge(sem, val)
                waited[key] = val

            for o in self.ops[ename]:
                for d in o.deps:
                    if d.is_dma:
                        wait(("d", d.dsem), dsems[d.dsem], d.dval)
                    else:
                        if d.eng == ename and not o.is_dma:
                            if ename == "pe" or not SAME_ENGINE_SYNC:
                                continue
                        wait(d.eng, sems[d.eng], d.sigval)
                if o.is_dma:
                    if o.dval > 16:
                        wait(("d", o.dsem), dsems[o.dsem], o.dval - 16)
                    ins = o.fn(eng)
                    ins.then_inc(dsems[o.dsem], 16)
                else:
                    ins = o.fn(eng)
                    if o.signal:
                        ins.then_inc(sems[ename], 1)
            if ename == "sp":
                for i, v in self.final_dma:
                    eng.wait_ge(dsems[i], v)

        @block.tensor
        def _(eng):
            run_engine("pe", eng)

        @block.scalar
        def _(eng):
            run_engine("act", eng)

        @block.vector
        def _(eng):
            run_engine("dve", eng)

        @block.gpsimd
        def _(eng):
            run_engine("pool", eng)

        @block.sync
        def _(eng):
            run_engine("sp", eng)


import contextlib

NEG = -1.0e30
GROUPS_A = [(0, 416), (416, 928), (928, 1440), (1440, 1952)]


def build(cfg):
    nc = bass.Bass("TRN2", target_bir_lowering=False)
    S = Sched()
    nseq = cfg.get("nseq", NSEQ)
    tok = nseq * SEQ
    layers = cfg.get("layers", list(range(DEPTH)))
    do_ffn = cfg.get("ffn", True)
    do_mix = cfg.get("mix", True)
    dbg = cfg.get("dbg", ())
    NT = tok // 128
    NT5 = tok // 512

    def din(name, shape, dt=F32):
        return nc.dram_tensor(name, list(shape), dt, kind="ExternalInput").ap()

    def dscr(name, shape, dt):
        kind = "ExternalOutput" if name in dbg else "Internal"
        return nc.dram_tensor(name, list(shape), dt, kind=kind).ap()

    x_in = din("x", [tok, D])
    mix_g = din("mix_norm_g", [DEPTH, D])
    ffn_g = din("ffn_norm_g", [DEPTH, D])
    w_in = din("w_in", [2, D, IN_COLS])
    g_cq = din("g_cq", [2, 256])
    w_uq = din("w_uq", [2, 256, 768])
    g_ckv = din("g_ckv", [2, 128])
    w_ukv = din("w_ukv", [2, 128, 1024])
    g_qn_mla = din("g_qn_mla", [2, 96])
    g_kn_mla = din("g_kn_mla", [2, 96])
    g_qn_moba = din("g_qn_moba", [2, 64])
    g_kn_moba = din("g_kn_moba", [2, 64])
    w_o = din("w_o", [2, D, D])
    lam_re = din("lam_re", [2, 64, 64])
    lam_im = din("lam_im", [2, 64, 64])
    log_dt = din("log_dt", [2, 64])
    b_re = din("b_re", [2, 64, 64, 16])
    b_im = din("b_im", [2, 64, 64, 16])
    c_re = din("c_re", [2, 64, 16, 64])
    c_im = din("c_im", [2, 64, 16, 64])
    d_skip = din("d_skip", [2, D])
    w_glu = din("w_glu", [2, D, 2 * D])
    w_ff1 = din("w_ff1", [DEPTH, D, DFF])
    w_ff2 = din("w_ff2", [DEPTH, DFF, D])
    ident_in = din("ident", [128, 128], BF16)
    tri_in = din("trimask", [128, 128], BF16)
    cos_in = din("rope_cos", [SEQ, 16])
    sin_in = din("rope_sin", [SEQ, 16])
    out = nc.dram_tensor("out", [tok, D], F32, kind="ExternalOutput").ap()

    xa = dscr("xa", [tok, D], F32)
    xm = dscr("xm", [tok, D], F32)
    xh = dscr("xh", [tok, D], F32)
    hTd = dscr("hTd", [NT5, 128, 8 * 512], BF16)
    w1c = dscr("w1c", [DEPTH, 2, 128, 8 * 2048], BF16)
    w2c = dscr("w2c", [DEPTH, 2, 128, 16 * 1024], BF16)
    winc = dscr("winc", [2, 128, 8 * IN_COLS], BF16)
    wuqc = dscr("wuqc", [2, 128, 2 * 768], BF16)
    wukvc = dscr("wukvc", [2, 128, 1024], BF16)
    woc = dscr("woc", [2, 128, 8 * 1024], BF16)
    QTm = dscr("QTm", [nseq, 8, 96, SEQ], BF16)
    KTm = dscr("KTm", [nseq, 8, 96, SEQ], BF16)
    Vm = dscr("Vm", [nseq, 8, 128, 32 * 65], BF16)
    QTb = dscr("QTb", [nseq, 4, 128, SEQ], BF16)
    KTb = dscr("KTb", [nseq, 4, 128, SEQ], BF16)
    Vb = dscr("Vb", [nseq, 8, 128, 32 * 65], BF16)
    Od = dscr("Od", [tok, D], BF16)
    uTd = dscr("uTd", [8, 128, tok], BF16)
    yd = dscr("yd", [tok, D], F32)
    BTd = dscr("BTd", [2, 32, 2, 128, 128], BF16)
    wgluc = dscr("wgluc", [2, 128, 8 * 2048], BF16)
    BTd2 = dscr("BTd2", [2, 32, 16, 2, 32, 128], BF16)
    Mlagd = dscr("Mlagd", [2, 8, 128, 1024], BF16)
    uTd2 = dscr("uTd2", [8, 128, nseq, 8, 512], BF16)

    es = contextlib.ExitStack()
    stk = [es]
    uid = [0]

    def sb(name, shape, dt):
        uid[0] += 1
        return stk[-1].enter_context(nc.sbuf_tensor("s%d_%s" % (uid[0], name), list(shape), dt))

    def ps(name, shape, dt):
        uid[0] += 1
        return stk[-1].enter_context(nc.psum_tensor("p%d_%s" % (uid[0], name), list(shape), dt))

    def phase_begin():
        stk.append(contextlib.ExitStack())

    def phase_end():
        S.barrier()
        stk.pop().close()

    def I(eng, name, reads, writes, *args, **kw):
        return S.op(eng, lambda e: getattr(e, name)(*args, **kw), reads, writes)

    def DMA(out_, in_, reads, writes, q="sp", **kw):
        return S.op(q, lambda e: e.dma_start(out=out_, in_=in_, **kw), reads, writes, dma=True)

    class Rot:
        def __init__(self, name, n, shape, dt, psum=False):
            mk = ps if psum else sb
            self.t = [mk("%s%d" % (name, i), shape, dt) for i in range(n)]
            self.b = [Buf("%s%d" % (name, i)) for i in range(n)]
            self.i = 0

        def next(self):
            k = self.i % len(self.t)
            self.i += 1
            return self.t[k], self.b[k]

    def rstd_cols(st_ap, st_buf, div):
        I("dve", "tensor_scalar", [st_buf], [st_buf], out=st_ap, in0=st_ap, scalar1=1.0 / div, scalar2=EPS,
          op0=ALU.mult, op1=ALU.add)
        I("act", "activation", [st_buf], [st_buf], out=st_ap, in_=st_ap, func=AF.Ln)
        I("act", "activation", [st_buf], [st_buf], out=st_ap, in_=st_ap, func=AF.Exp, scale=-0.5)

    with es:
        ident = sb("ident", [128, 128], BF16)
        b_ident = Buf("ident")
        DMA(ident[:], ident_in, [], [b_ident])
        tri = sb("tri", [128, 128], BF16)
        b_tri = Buf("tri")
        DMA(tri[:], tri_in, [], [b_tri])

        def prep_phase():
            phase_begin()
            stage = Rot("stage", 4, [128, 4096], F32)
            stageo = Rot("stageo", 4, [128, 4096], BF16)
            gsb = sb("gsb", [128, 80], F32)
            b_gsb = Buf("gsb")
            DMA(gsb[:, 0:32].rearrange("p (l k) -> p l k", l=DEPTH), ffn_g.rearrange("l (k p) -> p l k", p=128),
                [], [b_gsb], allow_slow_non_contiguous=True)
            DMA(gsb[:, 32:64].rearrange("p (l k) -> p l k", l=DEPTH), mix_g.rearrange("l (k p) -> p l k", p=128),
                [], [b_gsb], allow_slow_non_contiguous=True)
            DMA(gsb[:, 64:68].rearrange("p (l k) -> p l k", l=2), g_cq.rearrange("l (k p) -> p l k", p=128),
                [], [b_gsb], allow_slow_non_contiguous=True)
            DMA(gsb[:, 68:70].rearrange("p (l k) -> p l k", l=2), g_ckv.rearrange("l (k p) -> p l k", p=128),
                [], [b_gsb], allow_slow_non_contiguous=True)
            cnt = [0]

            def prep(src_ap, src_view, ncols, gcol, dsts):
                st, bst = stage.next()
                so, bso = stageo.next()
                eng = ["dve", "act"][cnt[0] % 2]
                cnt[0] += 1
                DMA(src_view(st), src_ap, [], [bst])
                if eng == "act":
                    if gcol is None:
                        I("act", "activation", [bst], [bso], out=so[:, :ncols], in_=st[:, :ncols], func=AF.Copy)
                    else:
                        I("act", "activation", [bst, b_gsb], [bso], out=so[:, :ncols], in_=st[:, :ncols],
                          func=AF.Copy, scale=gcol)
                elif gcol is None:
                    I(eng, "tensor_copy", [bst], [bso], out=so[:, :ncols], in_=st[:, :ncols])
                else:
                    I(eng, "tensor_scalar", [bst, b_gsb], [bso], out=so[:, :ncols], in0=st[:, :ncols], scalar1=gcol,
                      scalar2=None, op0=ALU.mult)
                for dap, sl in dsts:
                    DMA(dap, so[:, sl], [bso], [], q="act")

            for L in layers:
                if do_ffn:
                    for k in range(8):
                        dsts = [(w1c[L, h].rearrange("p (k c) -> p k c", k=8)[:, k, :],
                                 slice(h * 2048, (h + 1) * 2048)) for h in range(2)]
                        prep(w_ff1[L, k * 128:(k + 1) * 128, :], lambda st: st[:, :4096], 4096,
                             gsb[:, L * 8 + k:L * 8 + k + 1], dsts)
                    for jj in range(8):
                        h, j0 = divmod(jj * 4, 16)
                        prep(w_ff2[L, jj * 512:(jj + 1) * 512, :].rearrange("(j p) c -> p j c", p=128),
                             lambda st: st[:].rearrange("p (j c) -> p j c", j=4), 4096, None,
                             [(w2c[L, h][:, j0 * 1024:(j0 + 4) * 1024], slice(0, 4096))])
                if do_mix and L % 2 == 0:
                    i = L // 2
                    for k in range(8):
                        prep(w_in[i, k * 128:(k + 1) * 128, :], lambda st: st[:, :IN_COLS], IN_COLS,
                             gsb[:, 32 + L * 8 + k:32 + L * 8 + k + 1],
                             [(winc[i].rearrange("p (k c) -> p k c", k=8)[:, k, :], slice(0, IN_COLS))])
                    for k in range(2):
                        prep(w_uq[i, k * 128:(k + 1) * 128, :], lambda st: st[:, :768], 768,
                             gsb[:, 64 + i * 2 + k:64 + i * 2 + k + 1],
                             [(wuqc[i][:, k * 768:(k + 1) * 768], slice(0, 768))])
                    prep(w_ukv[i], lambda st: st[:, :1024], 1024, gsb[:, 68 + i:69 + i],
                         [(wukvc[i], slice(0, 1024))])
                    for kk in range(2):
                        prep(w_o[i, kk * 512:(kk + 1) * 512, :].rearrange("(j p) c -> p j c", p=128),
                             lambda st: st[:].rearrange("p (j c) -> p j c", j=4), 4096, None,
                             [(woc[i][:, kk * 4096:(kk + 1) * 4096], slice(0, 4096))])
                if do_mix and L % 2 == 1:
                    i = L // 2
                    for k in range(8):
                        prep(w_glu[i, k * 128:(k + 1) * 128, :], lambda st: st[:, :2048], 2048, None,
                             [(wgluc[i].rearrange("p (k c) -> p k c", k=8)[:, k, :], slice(0, 2048))])
            phase_end()

        def ffn_half(L, h, src, dst):
            phase_begin()
            w1s = sb("w1s", [128, 8 * 2048], BF16)
            w2s = sb("w2s", [128, 16 * 1024], BF16)
            b_w1s, b_w2s = Buf(), Buf()
            for k in range(8):
                DMA(w1s[:, k * 2048:(k + 1) * 2048], w1c[L, h][:, k * 2048:(k + 1) * 2048], [], [b_w1s])
            for k in range(4):
                DMA(w2s[:, k * 4096:(k + 1) * 4096], w2c[L, h][:, k * 4096:(k + 1) * 4096], [], [b_w2s])
            xt = Rot("xt", 3, [128, 4 * 1024], F32)
            hT = Rot("hT", 2, [128, 8 * 512], BF16)
            aT = sb("aT", [128, 16 * 512], BF16)
            b_aT = [Buf() for _ in range(16)]
            xn = Rot("xn", 2, [128, 1024], BF16)
            junk = sb("junk", [128, 1024], F32)
            b_junk = Buf()
            rl = Rot("rl", 3, [128, 512], F32)
            stat = Rot("stat", 4, [128, 1], F32)
            ps_h = Rot("ps_h", 3, [128, 512], F32, psum=True)
            ps_y = Rot("ps_y", 2, [128, 512], F32, psum=True)
            ps_t = Rot("ps_t", 2, [128, 1024], BF16, psum=True)
            tiles = {}

            def stage_a(t):
                xti, bxt = xt.next()
                hTi, bhT = hT.next()
                tiles[t] = (xti, bxt, hTi, bhT)
                rows = slice(t * 512, (t + 1) * 512)
                DMA(xti[:].rearrange("p (s c) -> p s c", s=4), src[rows, :].rearrange("(s p) c -> p s c", p=128),
                    [], [bxt])
                if h == 0:
                    for s in range(4):
                        xs = xti[:, s * 1024:(s + 1) * 1024]
                        st, bst = stat.next()
                        I("act", "activation", [bxt], [b_junk, bst], out=junk[:], in_=xs, func=AF.Square,
                          accum_out=st[:, 0:1])
                        rstd_cols(st[:, 0:1], bst, 1024)
                        xnq, bxn = xn.next()
                        I("dve", "tensor_scalar", [bxt, bst], [bxn], out=xnq[:], in0=xs, scalar1=st[:, 0:1],
                          scalar2=None, op0=ALU.mult)
                        pst, bpst = ps_t.next()
                        for k in range(8):
                            I("pe", "transpose", [bxn, b_ident], [bpst], out=pst[:, k * 128:(k + 1) * 128],
                              in_=xnq[:, k * 128:(k + 1) * 128], identity=ident[:])
                        I("act", "activation", [bpst], [bhT],
                          out=hTi[:].rearrange("p (k t) -> p k t", k=8)[:, :, s * 128:(s + 1) * 128],
                          in_=pst[:].rearrange("p (k t) -> p k t", k=8), func=AF.Copy)
                    DMA(hTd[t], hTi[:], [bhT], [], q="act")
                else:
                    DMA(hTi[:], hTd[t], [], [bhT])

            def stage_b(t):
                xti, bxt, hTi, bhT = tiles[t]
                for j in range(16):
                    psh, bpsh = ps_h.next()
                    for k in range(8):
                        I("pe", "matmul", [b_w1s, bhT], [bpsh], psh[:],
                          lhsT=w1s[:, k * 2048 + j * 128:k * 2048 + (j + 1) * 128],
                          rhs=hTi[:, k * 512:(k + 1) * 512], start=(k == 0), stop=(k == 7))
                    rlq, brl = rl.next()
                    I("act", "activation", [bpsh], [brl], out=rlq[:], in_=psh[:], func=AF.Relu)
                    I("pool" if j % 4 == 3 else "dve", "tensor_tensor", [brl], [b_aT[j]],
                      out=aT[:, j * 512:(j + 1) * 512], in0=rlq[:], in1=rlq[:], op=ALU.mult)

            def stage_c(t):
                xti, bxt, hTi, bhT = tiles.pop(t)
                rows = slice(t * 512, (t + 1) * 512)
                for s in range(4):
                    for c in range(2):
                        psy, bpsy = ps_y.next()
                        for j in range(16):
                            I("pe", "matmul", [b_aT[j], b_w2s], [bpsy], psy[:],
                              lhsT=aT[:, j * 512 + s * 128:j * 512 + (s + 1) * 128],
                              rhs=w2s[:, j * 1024 + c * 512:j * 1024 + (c + 1) * 512],
                              start=(j == 0), stop=(j == 15))
                        cs = slice(s * 1024 + c * 512, s * 1024 + (c + 1) * 512)
                        I("dve", "tensor_tensor", [bpsy, bxt], [bxt], out=xti[:, cs], in0=psy[:], in1=xti[:, cs],
                          op=ALU.add)
                DMA(dst[rows, :].rearrange("(s p) c -> p s c", p=128), xti[:].rearrange("p (s c) -> p s c", s=4),
                    [bxt], [], q="pool")

            stage_a(0)
            for t in range(NT5):
                if t + 1 < NT5:
                    stage_a(t + 1)
                stage_b(t)
                stage_c(t)
            phase_end()

        def skew(stages, n):
            ns = len(stages)
            for k in range(n + ns - 1):
                for si in reversed(range(ns)):
                    t = k - si
                    if 0 <= t < n:
                        stages[si](t)

        def attn_phase_a(i, L, src):
            phase_begin()
            win = sb("win", [128, 8 * IN_COLS], BF16)
            wuq = sb("wuq", [128, 2 * 768], BF16)
            wukv = sb("wukv", [128, 1024], BF16)
            b_w = Buf()
            DMA(win[:], winc[i], [], [b_w])
            DMA(wuq[:], wuqc[i], [], [b_w])
            DMA(wukv[:], wukvc[i], [], [b_w])
            cosT = sb("cosT", [128, 32 * 16], F32)
            sinT = sb("sinT", [128, 32 * 16], F32)
            b_cs = Buf()
            DMA(cosT[:].rearrange("p (t f) -> p t f", f=16), cos_in.rearrange("(t p) f -> p t f", p=128), [], [b_cs])
            DMA(sinT[:].rearrange("p (t f) -> p t f", f=16), sin_in.rearrange("(t p) f -> p t f", p=128), [], [b_cs])
            gq = sb("gq", [128, 4], F32)
            b_gq = Buf()
            DMA(gq[0:96, 0:1], g_qn_mla[i].rearrange("(d o) -> d o", o=1), [], [b_gq], allow_slow_non_contiguous=True)
            DMA(gq[0:96, 1:2], g_kn_mla[i].rearrange("(d o) -> d o", o=1), [], [b_gq], allow_slow_non_contiguous=True)
            for hh in range(2):
                DMA(gq[hh * 64:(hh + 1) * 64, 2:3], g_qn_moba[i].rearrange("(d o) -> d o", o=1), [], [b_gq],
                    allow_slow_non_contiguous=True)
                DMA(gq[hh * 64:(hh + 1) * 64, 3:4], g_kn_moba[i].rearrange("(d o) -> d o", o=1), [], [b_gq],
                    allow_slow_non_contiguous=True)
            sq = sb("sq", [128, 2], F32)
            b_sq = Buf()
            I("dve", "scalar_tensor_tensor", [b_gq], [b_sq], out=sq[0:96, 0:1], in0=gq[0:96, 0:1], scalar=96.0 ** -0.5,
              in1=gq[0:96, 1:2], op0=ALU.mult, op1=ALU.mult)
            I("dve", "scalar_tensor_tensor", [b_gq], [b_sq], out=sq[:, 1:2], in0=gq[:, 2:3], scalar=64.0 ** -0.5,
              in1=gq[:, 3:4], op0=ALU.mult, op1=ALU.mult)

            xt = Rot("xt", 2, [128, 1024], F32)
            xn = Rot("xn", 2, [128, 1024], BF16)
            xnT = Rot("xnT", 3, [128, 1024], BF16)
            junk = sb("junk", [128, 1024], F32)
            b_junk = Buf()
            st1 = Rot("st1", 9, [128, 1], F32)
            st8 = Rot("st8", 8, [128, 8], F32)
            pm = Rot("pm", 6, [128, 512], F32, psum=True)
            ps_t = Rot("ps_t", 2, [128, 1024], BF16, psum=True)
            cA = Rot("cA", 3, [128, 416], F32)
            cn = Rot("cn", 2, [128, 384], BF16)
            cnT = Rot("cnT", 2, [128, 384], BF16)
            qf = Rot("qf", 3, [128, 768], F32)
            kf = Rot("kf", 3, [128, 768], F32)
            sqt = Rot("sqt", 2, [128, 768], F32)
            rt = Rot("rt", 2, [128, 4 * 128], F32)
            krt = Rot("krt", 2, [128, 32], F32)
            nb = Rot("nb", 12, [128, 768], BF16)
            TT = Rot("TT", 4, [128, 1024], BF16)
            vmr = Rot("vm", 2, [128, 8 * 65], BF16)
            vbr = Rot("vb", 2, [128, 8 * 65], BF16)
            bf = Rot("bf", 6, [128, 512], F32)
            for r in (vmr, vbr):
                for tt, bb in zip(r.t, r.b):
                    I("pool", "memset", [], [bb], tt[:], 1.0)

            def rope(xv, bx, x1, x2, cosv, sinv, nh):
                r, br = rt.next()
                rv = r[:].rearrange("p (a h f) -> p a h f", a=4, h=8)
                t1, t2, t3, t4 = (rv[:, a, 0:nh, :] for a in range(4))
                I("pool", "tensor_tensor", [bx, b_cs], [br], out=t1, in0=x1, in1=cosv, op=ALU.mult)
                I("pool", "tensor_tensor", [bx, b_cs], [br], out=t2, in0=x2, in1=sinv, op=ALU.mult)
                I("pool", "tensor_tensor", [bx, b_cs], [br], out=t3, in0=x1, in1=sinv, op=ALU.mult)
                I("pool", "tensor_tensor", [bx, b_cs], [br], out=t4, in0=x2, in1=cosv, op=ALU.mult)
                I("pool", "tensor_tensor", [br], [bx], out=x1, in0=t1, in1=t2, op=ALU.subtract)
                I("pool", "tensor_tensor", [br], [bx], out=x2, in0=t3, in1=t4, op=ALU.add)

            def head_norm(xf, bxf, nh, hd, dstb, bdst):
                s2, bs2 = sqt.next()
                n = nh * hd
                I("pool", "tensor_tensor", [bxf], [bs2], out=s2[:, :n], in0=xf[:, :n], in1=xf[:, :n], op=ALU.mult)
                st, bst = st8.next()
                I("dve", "tensor_reduce", [bs2], [bst], out=st[:, 0:nh],
                  in_=s2[:, :n].rearrange("p (h d) -> p h d", h=nh), axis=AX.X, op=ALU.add)
                rstd_cols(st[:, 0:nh], bst, hd)
                I("dve", "tensor_tensor", [bxf, bst], [bdst], out=dstb[:, :n].rearrange("p (h d) -> p h d", h=nh),
                  in0=xf[:, :n].rearrange("p (h d) -> p h d", h=nh),
                  in1=st[:, 0:nh].unsqueeze(2).broadcast_to([128, nh, hd]), op=ALU.mult)

            T_ = {}

            def sA(t):
                    seq, kt = divmod(t, 32)
                    pos = slice(kt * 128, (kt + 1) * 128)
                    cosv = cosT[:, kt * 16:(kt + 1) * 16]
                    sinv = sinT[:, kt * 16:(kt + 1) * 16]
                    xti, bxt = xt.next()
                    DMA(xti[:], src[t * 128:(t + 1) * 128, :], [], [bxt])
                    st, bst = st1.next()
                    I("act", "activation", [bxt], [b_junk, bst], out=junk[:], in_=xti[:], func=AF.Square,
                      accum_out=st[:, 0:1])
                    rstd_cols(st[:, 0:1], bst, 1024)
                    xnq, bxn = xn.next()
                    I("dve", "tensor_scalar", [bxt, bst], [bxn], out=xnq[:], in0=xti[:], scalar1=st[:, 0:1], scalar2=None,
                      op0=ALU.mult)
                    pst, bpst = ps_t.next()
                    for k in range(8):
                        I("pe", "transpose", [bxn, b_ident], [bpst], out=pst[:, k * 128:(k + 1) * 128],
                          in_=xnq[:, k * 128:(k + 1) * 128], identity=ident[:])
                    xT, bxT = xnT.next()
                    I("act", "activation", [bpst], [bxT], out=xT[:], in_=pst[:], func=AF.Copy)
                    T_[t] = dict(xT=(xT, bxT))

            def sB(t):
                    seq, kt = divmod(t, 32)
                    pos = slice(kt * 128, (kt + 1) * 128)
                    cosv = cosT[:, kt * 16:(kt + 1) * 16]
                    sinv = sinT[:, kt * 16:(kt + 1) * 16]
                    d = T_[t]
                    xT, bxT = d['xT']
                    pg = []
                    for (c0, c1) in GROUPS_A:
                        p, bp = pm.next()
                        for k in range(8):
                            I("pe", "matmul", [bxT, b_w], [bp], p[:, 0:c1 - c0], lhsT=xT[:, k * 128:(k + 1) * 128],
                              rhs=win[:, k * IN_COLS + c0:k * IN_COLS + c1], start=(k == 0), stop=(k == 7))
                        pg.append((p, bp))
                    cAi, bcA = cA.next()
                    I("act", "activation", [pg[0][1]], [bcA], out=cAi[:], in_=pg[0][0][:, 0:416], func=AF.Copy)
                    bfs = []
                    for which in range(2):
                        p, bp = pg[1 + which]
                        bfi, bbf = bf.next()
                        I("act", "activation", [bp], [bbf], out=bfi[:], in_=p[:], func=AF.Copy)
                        bfs.append((bfi, bbf))
                    vbi, bvb = vbr.next()
                    vb3 = vbi[:].rearrange("p (h c) -> p h c", h=8)
                    I("act", "activation", [pg[3][1]], [bvb], out=vb3[:, :, 0:64],
                      in_=pg[3][0][:].rearrange("p (h d) -> p h d", h=8), func=AF.Copy)
                    sa, bsa = st1.next()
                    sb_, bsb = st1.next()
                    I("act", "activation", [bcA], [b_junk, bsa], out=junk[:, 0:256], in_=cAi[:, 0:256], func=AF.Square,
                      accum_out=sa[:, 0:1])
                    I("act", "activation", [bcA], [b_junk, bsb], out=junk[:, 0:128], in_=cAi[:, 256:384], func=AF.Square,
                      accum_out=sb_[:, 0:1])
                    rstd_cols(sa[:, 0:1], bsa, 256)
                    rstd_cols(sb_[:, 0:1], bsb, 128)
                    cni, bcn = cn.next()
                    I("dve", "tensor_scalar", [bcA, bsa], [bcn], out=cni[:, 0:256], in0=cAi[:, 0:256], scalar1=sa[:, 0:1],
                      scalar2=None, op0=ALU.mult)
                    I("dve", "tensor_scalar", [bcA, bsb], [bcn], out=cni[:, 256:384], in0=cAi[:, 256:384],
                      scalar1=sb_[:, 0:1], scalar2=None, op0=ALU.mult)
                    pst, bpst = ps_t.next()
                    for k in range(3):
                        I("pe", "transpose", [bcn, b_ident], [bpst], out=pst[:, k * 128:(k + 1) * 128],
                          in_=cni[:, k * 128:(k + 1) * 128], identity=ident[:])
                    cT, bcT = cnT.next()
                    I("act", "activation", [bpst], [bcT], out=cT[:], in_=pst[:, 0:384], func=AF.Copy)
                    pq0, bpq0 = pm.next()
                    pq1, bpq1 = pm.next()
                    for k in range(2):
                        I("pe", "matmul", [bcT, b_w], [bpq0], pq0[:], lhsT=cT[:, k * 128:(k + 1) * 128],
                          rhs=wuq[:, k * 768:k * 768 + 512], start=(k == 0), stop=(k == 1))
                    for k in range(2):
                        I("pe", "matmul", [bcT, b_w], [bpq1], pq1[:, 0:256], lhsT=cT[:, k * 128:(k + 1) * 128],
                          rhs=wuq[:, k * 768 + 512:k * 768 + 768], start=(k == 0), stop=(k == 1))
                    pk = []
                    for c in range(2):
                        p, bp = pm.next()
                        I("pe", "matmul", [bcT, b_w], [bp], p[:], lhsT=cT[:, 256:384], rhs=wukv[:, c * 512:(c + 1) * 512],
                          start=True, stop=True)
                        pk.append((p, bp))
                    qfi, bqf = qf.next()
                    I("act", "activation", [bpq0], [bqf], out=qfi[:, 0:512], in_=pq0[:], func=AF.Copy)
                    I("act", "activation", [bpq1], [bqf], out=qfi[:, 512:768], in_=pq1[:, 0:256], func=AF.Copy)
                    kfi, bkf = kf.next()
                    k3 = kfi[:].rearrange("p (h d) -> p h d", h=8)
                    for c in range(2):
                        I("act", "activation", [pk[c][1]], [bkf], out=k3[:, c * 4:(c + 1) * 4, 0:64],
                          in_=pk[c][0][:].rearrange("p (h d) -> p h d", h=4)[:, :, 0:64], func=AF.Copy)
                    vmi, bvm = vmr.next()
                    v3 = vmi[:].rearrange("p (h c) -> p h c", h=8)
                    for c in range(2):
                        I("act", "activation", [pk[c][1]], [bvm], out=v3[:, c * 4:(c + 1) * 4, 0:64],
                          in_=pk[c][0][:].rearrange("p (h d) -> p h d", h=4)[:, :, 64:128], func=AF.Copy)
                    DMA(Vm[seq].rearrange("h p (k c) -> p h k c", c=65)[:, :, kt, :], v3, [bvm], [], q="act")
                    DMA(Vb[seq].rearrange("h p (k c) -> p h k c", c=65)[:, :, kt, :], vb3, [bvb], [], q="act")
                    d.update(cA=(cAi, bcA), bfs=bfs, qf=(qfi, bqf), kf=(kfi, bkf))

            def sC(t):
                    seq, kt = divmod(t, 32)
                    pos = slice(kt * 128, (kt + 1) * 128)
                    cosv = cosT[:, kt * 16:(kt + 1) * 16]
                    sinv = sinT[:, kt * 16:(kt + 1) * 16]
                    d = T_[t]
                    cAi, bcA = d['cA']
                    qfi, bqf = d['qf']
                    kfi, bkf = d['kf']
                    k3 = kfi[:].rearrange('p (h d) -> p h d', h=8)
                    q3 = qfi[:].rearrange("p (h d) -> p h d", h=8)
                    rope(qfi, bqf, q3[:, :, 64:80], q3[:, :, 80:96], cosv.unsqueeze(1).broadcast_to([128, 8, 16]),
                         sinv.unsqueeze(1).broadcast_to([128, 8, 16]), 8)
                    qnb, bqnb = nb.next()
                    head_norm(qfi, bqf, 8, 96, qnb, bqnb)
                    kr, bkr = krt.next()
                    I("pool", "tensor_copy", [bcA], [bkr], out=kr[:], in_=cAi[:, 384:416])
                    rope(kr, bkr, kr[:, 0:16].unsqueeze(1), kr[:, 16:32].unsqueeze(1), cosv.unsqueeze(1), sinv.unsqueeze(1), 1)
                    I("pool", "tensor_copy", [bkr], [bkf], out=k3[:, :, 64:96],
                      in_=kr[:].unsqueeze(1).broadcast_to([128, 8, 32]))
                    knb, bknb = nb.next()
                    head_norm(kfi, bkf, 8, 96, knb, bknb)
                    nbs = []
                    for which in range(2):
                        bfi, bbf = d['bfs'][which]
                        nbi, bnb = nb.next()
                        head_norm(bfi, bbf, 8, 64, nbi, bnb)
                        nbs.append((nbi, bnb))
                    d.update(qnb=(qnb, bqnb), knb=(knb, bknb), nbs=nbs)

            def sD(t):
                    seq, kt = divmod(t, 32)
                    pos = slice(kt * 128, (kt + 1) * 128)
                    cosv = cosT[:, kt * 16:(kt + 1) * 16]
                    sinv = sinT[:, kt * 16:(kt + 1) * 16]
                    d = T_.pop(t)
                    qnb, bqnb = d['qnb']
                    knb, bknb = d['knb']
                    pst, bpst = ps_t.next()
                    for hh in range(8):
                        I("pe", "transpose", [bqnb, b_ident], [bpst], out=pst[0:96, hh * 128:(hh + 1) * 128],
                          in_=qnb[:, hh * 96:(hh + 1) * 96], identity=ident[:])
                    Tq, bTq = TT.next()
                    I("act", "activation", [bpst, b_sq], [bTq], out=Tq[0:96, :], in_=pst[0:96, :], func=AF.Copy,
                      scale=sq[0:96, 0:1])
                    DMA(QTm[seq].rearrange("h d s -> d h s")[:, :, pos], Tq[0:96, :].rearrange("p (h s) -> p h s", h=8),
                        [bTq], [], q="act")
                    pst, bpst = ps_t.next()
                    for hh in range(8):
                        I("pe", "transpose", [bknb, b_ident], [bpst], out=pst[0:96, hh * 128:(hh + 1) * 128],
                          in_=knb[:, hh * 96:(hh + 1) * 96], identity=ident[:])
                    Tk, bTk = TT.next()
                    I("act", "activation", [bpst], [bTk], out=Tk[0:96, :], in_=pst[0:96, :], func=AF.Copy)
                    DMA(KTm[seq].rearrange("h d s -> d h s")[:, :, pos], Tk[0:96, :].rearrange("p (h s) -> p h s", h=8),
                        [bTk], [], q="act")
                    for which in range(2):
                        nbi, bnb = d['nbs'][which]
                        pst, bpst = ps_t.next()
                        for pr in range(4):
                            I("pe", "transpose", [bnb, b_ident], [bpst], out=pst[:, pr * 128:(pr + 1) * 128],
                              in_=nbi[:, pr * 128:(pr + 1) * 128], identity=ident[:])
                        Tb, bTb = TT.next()
                        if which == 0:
                            I("act", "activation", [bpst, b_sq], [bTb], out=Tb[:, 0:512], in_=pst[:, 0:512], func=AF.Copy,
                              scale=sq[:, 1:2])
                        else:
                            I("act", "activation", [bpst], [bTb], out=Tb[:, 0:512], in_=pst[:, 0:512], func=AF.Copy)
                        dstT = QTb if which == 0 else KTb
                        DMA(dstT[seq].rearrange("r p s -> p r s")[:, :, pos], Tb[:, 0:512].rearrange("p (r s) -> p r s", r=4),
                            [bTb], [], q="act")

            skew([sA, sB, sC, sD], NT)
            phase_end()

        def attn_phase_b():
            phase_begin()
            QT = Rot("QT", 3, [128, SEQ], BF16)
            KT = Rot("KT", 3, [128, SEQ], BF16)
            VV = Rot("VV", 3, [128, 32 * 65], BF16)
            pss = Rot("pss", 4, [128, 512], F32, psum=True)
            pso = Rot("pso", 3, [128, 512], F32, psum=True)
            psg = Rot("psg", 1, [128, 64], F32, psum=True)
            pT = Rot("pT", 5, [128, 512], BF16)
            osb = Rot("osb", 2, [128, 260], F32)
            rec = Rot("rec", 2, [128, 4], F32)
            obf = Rot("obf", 2, [128, 256], BF16)
            km = Rot("km", 3, [128, 16], F32)
            kmb = Rot("kmb", 3, [128, 16], BF16)
            gs = Rot("gs", 2, [128, 64], F32)
            mx = Rot("mx", 2, [128, 32], F32)
            sel = Rot("sel", 2, [128, 64], F32)
            acc = Rot("acc", 2, [128, 260], F32)
            tmpc = Rot("tmpc", 3, [128, 260], F32)

            def finalize(src_ap, bsrc, seq, qb, col0):
                r, br = rec.next()
                s3 = src_ap.rearrange("p (s c) -> p s c", s=4)
                I("dve", "reciprocal", [bsrc], [br], out=r[:].unsqueeze(2), in_=s3[:, :, 64:65])
                o, bo = obf.next()
                I("dve", "tensor_tensor", [bsrc, br], [bo], out=o[:].rearrange("p (s c) -> p s c", s=4),
                  in0=s3[:, :, 0:64], in1=r[:].unsqueeze(2).broadcast_to([128, 4, 64]), op=ALU.mult)
                r0 = seq * SEQ + qb * 512
                DMA(Od[r0:r0 + 512, col0:col0 + 64].rearrange("(s p) c -> p s c", p=128),
                    o[:].rearrange("p (s c) -> p s c", s=4), [bo], [], q="pool")

            def qk_exp(KTi, bKT, QTi, bQT, dk, kt, qb):
                p, bp = pss.next()
                I("pe", "matmul", [bKT, bQT], [bp], p[:], lhsT=KTi[0:dk, kt * 128:(kt + 1) * 128],
                  rhs=QTi[0:dk, qb * 512:(qb + 1) * 512], start=True, stop=True)
                e, be = pT.next()
                I("act", "activation", [bp], [be], out=e[:], in_=p[:], func=AF.Exp)
                if kt >= 4 * qb:
                    s = kt - 4 * qb
                    I("pool", "tensor_tensor", [be, b_tri], [be], out=e[:, s * 128:(s + 1) * 128],
                      in0=e[:, s * 128:(s + 1) * 128], in1=tri[:], op=ALU.mult)
                return e, be

            steps = []
            units = []
            for seq in range(nseq):
                for h in range(8):
                    units.append((seq, "mla", h))
                for h in range(8):
                    units.append((seq, "moba", h))
            loaded = {}

            def load_unit(u):
                seq, kind, h = units[u]
                QTi, bQT = QT.next()
                KTi, bKT = KT.next()
                Vi, bV = VV.next()
                if kind == "mla":
                    DMA(QTi[0:96, :], QTm[seq, h], [], [bQT])
                    DMA(KTi[0:96, :], KTm[seq, h], [], [bKT])
                    DMA(Vi[:], Vm[seq, h], [], [bV])
                else:
                    pr, half = divmod(h, 2)
                    DMA(QTi[0:64, :], QTb[seq, pr, half * 64:(half + 1) * 64, :], [], [bQT])
                    DMA(KTi[0:64, :], KTb[seq, pr, half * 64:(half + 1) * 64, :], [], [bKT])
                    DMA(Vi[:], Vb[seq, h], [], [bV])
                loaded[u] = (QTi, bQT, KTi, bKT, Vi, bV)

            def mk_mla(u, qb):
                seq, kind, h = units[u]
                state = {}

                def pre():
                    if qb == 0:
                        if u == 0:
                            load_unit(0)
                        if u + 1 < len(units):
                            load_unit(u + 1)
                    state["po"] = pso.next()

                def qk(kt):
                    def f():
                        if kt == 0:
                            pre()
                        QTi, bQT, KTi, bKT, Vi, bV = loaded[u]
                        state[kt] = qk_exp(KTi, bKT, QTi, bQT, 96, kt, qb)
                    return f

                def pv(kt):
                    def f():
                        QTi, bQT, KTi, bKT, Vi, bV = loaded[u]
                        po, bpo = state["po"]
                        e, be = state.pop(kt)
                        for s in range(4):
                            if kt <= 4 * qb + s:
                                I("pe", "matmul", [be, bV], [bpo], po[:, s * 65:(s + 1) * 65],
                                  lhsT=e[:, s * 128:(s + 1) * 128], rhs=Vi[:, kt * 65:(kt + 1) * 65],
                                  start=(kt == 0 and s == 0), stop=(kt == 4 * qb + 3 and s == 3),
                                  skip_group_check=True)
                        if kt == 4 * qb + 3:
                            o, bo = osb.next()
                            I("act", "activation", [bpo], [bo], out=o[:], in_=po[:, 0:260], func=AF.Copy)
                            finalize(o[:], bo, seq, qb, h * 64)
                    return f

                for kt in range(4 * qb + 4):
                    steps.append((qk(kt), pv(kt)))

            def mk_moba(u, qb):
                seq, kind, h = units[u]
                state = {}

                def pre():
                    if qb == 0:
                        if u == 0:
                            load_unit(0)
                        if u + 1 < len(units):
                            load_unit(u + 1)
                    QTi, bQT, KTi, bKT, Vi, bV = loaded[u]
                    if qb == 0:
                        kmi, bkm = km.next()
                        I("dve", "tensor_reduce", [bKT], [bkm], out=kmi[0:64, :],
                          in_=KTi[0:64, :].rearrange("p (n l) -> p n l", l=256), axis=AX.X, op=ALU.add)
                        kbi, bkb = kmb.next()
                        I("dve", "tensor_scalar", [bkm], [bkb], out=kbi[0:64, :], in0=kmi[0:64, :],
                          scalar1=1.0 / 256, scalar2=None, op0=ALU.mult)
                        loaded[("kmb", u)] = (kbi, bkb)
                    kbi, bkb = loaded[("kmb", u)]
                    pg_, bpg = psg.next()
                    for s in range(4):
                        I("pe", "matmul", [bQT, bkb], [bpg], pg_[:, s * 16:(s + 1) * 16],
                          lhsT=QTi[0:64, (4 * qb + s) * 128:(4 * qb + s + 1) * 128], rhs=kbi[0:64, :],
                          start=True, stop=True)
                    sl, bsl = sel.next()
                    curs = [2 * qb + s // 2 for s in range(4)]
                    if curs[3] <= 3:
                        I("pool", "memset", [], [bsl], sl[:], 1.0)
                    else:
                        g, bg = gs.next()
                        I("pool", "memset", [], [bg], g[:], NEG)
                        for s0 in (0, 2):
                            cur = curs[s0]
                            I("act", "activation", [bpg], [bg],
                              out=g[:].rearrange("p (s n) -> p s n", s=4)[:, s0:s0 + 2, 0:cur],
                              in_=pg_[:].rearrange("p (s n) -> p s n", s=4)[:, s0:s0 + 2, 0:cur], func=AF.Copy)
                        m8, bm8 = mx.next()
                        for s in range(4):
                            I("dve", "max", [bg], [bm8], out=m8[:, s * 8:(s + 1) * 8], in_=g[:, s * 16:(s + 1) * 16])
                        I("dve", "tensor_tensor", [bg, bm8], [bsl], out=sl[:].rearrange("p (s n) -> p s n", s=4),
                          in0=g[:].rearrange("p (s n) -> p s n", s=4),
                          in1=m8[:].rearrange("p (s e) -> p s e", s=4)[:, :, 2:3].broadcast_to([128, 4, 16]),
                          op=ALU.is_ge)
                        for s in range(4):
                            if curs[s] <= 3:
                                I("pool", "memset", [], [bsl], sl[:, s * 16:s * 16 + curs[s]], 1.0)
                            I("pool", "memset", [], [bsl], sl[:, s * 16 + curs[s]:s * 16 + curs[s] + 1], 1.0)
                    state["sl"] = (sl, bsl)
                    ac, bac = acc.next()
                    I("pool", "memset", [], [bac], ac[:], 0.0)
                    state["acc"] = (ac, bac)

                def qk(kt):
                    def f():
                        if kt == 0:
                            pre()
                        QTi, bQT, KTi, bKT, Vi, bV = loaded[u]
                        if kt % 2 == 0:
                            state[("po", kt // 2)] = pso.next()
                        state[kt] = qk_exp(KTi, bKT, QTi, bQT, 64, kt, qb)
                    return f

                def pv(kt):
                    def f():
                        QTi, bQT, KTi, bKT, Vi, bV = loaded[u]
                        n = kt // 2
                        po, bpo = state[("po", n)]
                        e, be = state.pop(kt)
                        for s in range(4):
                            cur = 2 * qb + s // 2
                            qt = 4 * qb + s
                            if n < cur or (n == cur and kt <= qt):
                                first = (kt == 2 * n and s == (0 if n <= 2 * qb else 2))
                                I("pe", "matmul", [be, bV], [bpo], po[:, s * 65:(s + 1) * 65],
                                  lhsT=e[:, s * 128:(s + 1) * 128], rhs=Vi[:, kt * 65:(kt + 1) * 65],
                                  start=first, stop=False, skip_group_check=True)
                        if kt == 2 * n + 1:
                            sl, bsl = state["sl"]
                            ac, bac = state["acc"]
                            s0 = 0 if n <= 2 * qb else 2
                            tm, btm = tmpc.next()
                            cs = slice(s0 * 65, 260)
                            ns = 4 - s0
                            I("dve", "tensor_tensor", [bpo, bsl], [btm],
                              out=tm[:, cs].rearrange("p (s c) -> p s c", s=ns),
                              in0=po[:, cs].rearrange("p (s c) -> p s c", s=ns),
                              in1=sl[:].rearrange("p (s n) -> p s n", s=4)[:, s0:4, n:n + 1].broadcast_to([128, ns, 65]),
                              op=ALU.mult)
                            I("pool", "tensor_tensor", [btm, bac], [bac], out=ac[:, cs], in0=ac[:, cs], in1=tm[:, cs],
                              op=ALU.add)
                            state.pop(("po", n))
                            if n == 2 * qb + 1:
                                finalize(ac[:], bac, seq, qb, 512 + h * 64)
                    return f

                for kt in range(4 * qb + 4):
                    steps.append((qk(kt), pv(kt)))

            for u, (seq, kind, h) in enumerate(units):
                for qb in range(8):
                    (mk_mla if kind == "mla" else mk_moba)(u, qb)
            LOOK = 3
            for idx in range(len(steps) + LOOK):
                if idx < len(steps):
                    steps[idx][0]()
                if idx - LOOK >= 0:
                    steps[idx - LOOK][1]()
            phase_end()

        def attn_phase_c(i, src, dst):
            phase_begin()
            wo = sb("wo", [128, 8 * 1024], BF16)
            b_wo = Buf()
            DMA(wo[:], woc[i], [], [b_wo])
            xt = Rot("xt", 4, [128, 1024], F32)
            ot = Rot("ot", 3, [128, 1024], BF16)
            oT = Rot("oT", 3, [128, 1024], BF16)
            ps_t = Rot("ps_t", 2, [128, 1024], BF16, psum=True)
            ps_y = Rot("ps_y", 4, [128, 512], F32, psum=True)
            T_ = {}

            def sA(t):
                rows = slice(t * 128, (t + 1) * 128)
                xti, bxt = xt.next()
                oti, bot = ot.next()
                DMA(xti[:], src[rows, :], [], [bxt])
                DMA(oti[:], Od[rows, :], [], [bot])
                pst, bpst = ps_t.next()
                for k in range(8):
                    I("pe", "transpose", [bot, b_ident], [bpst], out=pst[:, k * 128:(k + 1) * 128],
                      in_=oti[:, k * 128:(k + 1) * 128], identity=ident[:])
                oTi, boT = oT.next()
                I("act", "activation", [bpst], [boT], out=oTi[:], in_=pst[:], func=AF.Copy)
                T_[t] = (xti, bxt, oTi, boT)

            def sB(t):
                rows = slice(t * 128, (t + 1) * 128)
                xti, bxt, oTi, boT = T_.pop(t)
                for c in range(2):
                    py, bpy = ps_y.next()
                    for k in range(8):
                        I("pe", "matmul", [boT, b_wo], [bpy], py[:], lhsT=oTi[:, k * 128:(k + 1) * 128],
                          rhs=wo[:, k * 1024 + c * 512:k * 1024 + (c + 1) * 512], start=(k == 0), stop=(k == 7))
                    I("dve", "tensor_tensor", [bpy, bxt], [bxt], out=xti[:, c * 512:(c + 1) * 512], in0=py[:],
                      in1=xti[:, c * 512:(c + 1) * 512], op=ALU.add)
                DMA(dst[rows, :], xti[:], [bxt], [], q=cfg.get("stq", "pool"))

            skew([sA, sB], NT)
            phase_end()

        NLEV = 9
        PI = float(np.pi)

        def s5_prep(i, L):
            apw = sb("apw", [128, NLEV * 2 * 32], F32)
            CPr = sb("CPr", [128, 32 * 32], BF16)
            CPi = sb("CPi", [128, 32 * 32], BF16)
            negl = sb("negl", [128, NLEV * 32], F32)
            b_p = Buf()
            phase_begin()
            lr = sb("lr", [128, 32], F32)
            li = sb("li", [128, 32], F32)
            dt = sb("dt", [128, 32], F32)
            bt = Buf()
            for two in range(2):
                ps_ = slice(two * 64, (two + 1) * 64)
                DMA(lr[ps_, :], lam_re[i].rearrange("(gp two) p -> two p gp", two=2)[two], [], [bt],
                    allow_slow_non_contiguous=True)
                DMA(li[ps_, :], lam_im[i].rearrange("(gp two) p -> two p gp", two=2)[two], [], [bt],
                    allow_slow_non_contiguous=True)
                DMA(dt[ps_, :], log_dt[i].rearrange("(gp two) -> two gp", two=2)[two:two + 1, :].broadcast_to([64, 32]),
                    [], [bt], allow_slow_non_contiguous=True)
            tl = [sb("t%d" % k, [128, 32], F32) for k in range(12)]
            er, ang, kk, sn, cs, ca, m2, t0, t1, wr, wi, den = tl

            def V(name, out_, **kw):
                I("dve", name, [bt], [bt], out=out_, **kw)

            I("act", "activation", [bt], [bt], out=dt[:], in_=dt[:], func=AF.Exp)
            V("tensor_tensor", er[:], in0=lr[:], in1=dt[:], op=ALU.mult)
            I("act", "activation", [bt], [bt], out=er[:], in_=er[:], func=AF.Exp)
            V("tensor_tensor", ang[:], in0=li[:], in1=dt[:], op=ALU.mult)
            I("pool", "memset", [bt], [bt], kk[:], 0.0)
            for m in range(1, 5):
                V("scalar_tensor_tensor", kk[:], in0=ang[:], scalar=(2 * m - 1) * PI, in1=kk[:], op0=ALU.is_gt,
                  op1=ALU.add)
            V("scalar_tensor_tensor", ang[:], in0=kk[:], scalar=-2 * PI, in1=ang[:], op0=ALU.mult, op1=ALU.add)
            I("act", "activation", [bt], [bt], out=sn[:], in_=ang[:], func=AF.Sin)
            V("tensor_scalar", ca[:], in0=ang[:], scalar1=PI / 2, scalar2=None, op0=ALU.add)
            V("tensor_scalar", m2[:], in0=ca[:], scalar1=PI, scalar2=None, op0=ALU.is_gt)
            V("scalar_tensor_tensor", ca[:], in0=m2[:], scalar=-2 * PI, in1=ca[:], op0=ALU.mult, op1=ALU.add)
            I("act", "activation", [bt], [bt], out=cs[:], in_=ca[:], func=AF.Sin)

            def lev(k, c):
                return apw[:, (k * 2 + c) * 32:(k * 2 + c + 1) * 32]

            I("dve", "tensor_tensor", [bt], [b_p], out=lev(0, 0), in0=er[:], in1=cs[:], op=ALU.mult)
            I("dve", "tensor_tensor", [bt], [b_p], out=lev(0, 1), in0=er[:], in1=sn[:], op=ALU.mult)
            for k in range(1, NLEV):
                pr_, pi_ = lev(k - 1, 0), lev(k - 1, 1)
                I("dve", "tensor_tensor", [b_p], [bt], out=t0[:], in0=pr_, in1=pr_, op=ALU.mult)
                I("dve", "tensor_tensor", [b_p], [bt], out=t1[:], in0=pi_, in1=pi_, op=ALU.mult)
                I("dve", "tensor_tensor", [bt], [b_p], out=lev(k, 0), in0=t0[:], in1=t1[:], op=ALU.subtract)
                I("dve", "scalar_tensor_tensor", [b_p], [b_p], out=lev(k, 1), in0=pr_, scalar=2.0, in1=pi_,
                  op0=ALU.mult, op1=ALU.mult)
            I("dve", "tensor_scalar", [b_p], [b_p], out=negl[:].rearrange("p (k g) -> p k g", k=NLEV),
              in0=apw[:].rearrange("p (k c g) -> p k c g", k=NLEV, c=2)[:, :, 1, :], scalar1=-1.0, scalar2=None,
              op0=ALU.mult)
            ar, ai = lev(0, 0), lev(0, 1)
            I("dve", "tensor_tensor", [bt], [bt], out=t0[:], in0=lr[:], in1=lr[:], op=ALU.mult)
            I("dve", "tensor_tensor", [bt], [bt], out=t1[:], in0=li[:], in1=li[:], op=ALU.mult)
            I("dve", "tensor_tensor", [bt], [bt], out=den[:], in0=t0[:], in1=t1[:], op=ALU.add)
            I("dve", "reciprocal", [bt], [bt], out=den[:], in_=den[:])
            I("dve", "tensor_scalar", [b_p], [bt], out=t0[:], in0=ar, scalar1=-1.0, scalar2=None, op0=ALU.add)
            I("dve", "tensor_tensor", [bt], [bt], out=wr[:], in0=t0[:], in1=lr[:], op=ALU.mult)
            I("dve", "tensor_tensor", [bt, b_p], [bt], out=t1[:], in0=ai, in1=li[:], op=ALU.mult)
            I("dve", "tensor_tensor", [bt], [bt], out=wr[:], in0=wr[:], in1=t1[:], op=ALU.add)
            I("dve", "tensor_tensor", [bt], [bt], out=wr[:], in0=wr[:], in1=den[:], op=ALU.mult)
            I("dve", "tensor_tensor", [bt, b_p], [bt], out=wi[:], in0=ai, in1=lr[:], op=ALU.mult)
            I("dve", "tensor_tensor", [bt], [bt], out=t1[:], in0=t0[:], in1=li[:], op=ALU.mult)
            I("dve", "tensor_tensor", [bt], [bt], out=wi[:], in0=wi[:], in1=t1[:], op=ALU.subtract)
            I("dve", "tensor_tensor", [bt], [bt], out=wi[:], in0=wi[:], in1=den[:], op=ALU.mult)
            Br = sb("Br", [128, 512], F32)
            Bi = sb("Bi", [128, 512], F32)
            gm = sb("gm", [128, 512], F32)
            bb_ = Buf()
            for two in range(2):
                ps_ = slice(two * 64, (two + 1) * 64)
                for gp in range(32):
                    g = 2 * gp + two
                    DMA(Br[ps_, gp * 16:(gp + 1) * 16], b_re[i, g], [], [bb_])
                    DMA(Bi[ps_, gp * 16:(gp + 1) * 16], b_im[i, g], [], [bb_])
                DMA(gm[ps_, :].rearrange("p (gp i) -> p gp i", i=16),
                    mix_g[L].rearrange("(gp two i) -> two gp i", two=2, i=16)[two:two + 1].broadcast_to([64, 32, 16]),
                    [], [bb_], allow_slow_non_contiguous=True)
            Bbr = sb("Bbr", [128, 512], F32)
            Bbi = sb("Bbi", [128, 512], F32)
            tb = sb("tb", [128, 512], F32)
            Bb16 = sb("Bb16", [128, 2 * 512], BF16)
            wrb = wr[:].unsqueeze(2).broadcast_to([128, 32, 16])
            wib = wi[:].unsqueeze(2).broadcast_to([128, 32, 16])

            def v3(t_):
                return t_[:].rearrange("p (gp i) -> p gp i", i=16)

            I("dve", "tensor_tensor", [bt, bb_], [bb_], out=v3(Bbr), in0=v3(Br), in1=wrb, op=ALU.mult)
            I("dve", "tensor_tensor", [bt, bb_], [bb_], out=v3(tb), in0=v3(Bi), in1=wib, op=ALU.mult)
            I("dve", "tensor_tensor", [bb_], [bb_], out=Bbr[:], in0=Bbr[:], in1=tb[:], op=ALU.subtract)
            I("dve", "tensor_tensor", [bt, bb_], [bb_], out=v3(Bbi), in0=v3(Bi), in1=wrb, op=ALU.mult)
            I("dve", "tensor_tensor", [bt, bb_], [bb_], out=v3(tb), in0=v3(Br), in1=wib, op=ALU.mult)
            I("dve", "tensor_tensor", [bb_], [bb_], out=Bbi[:], in0=Bbi[:], in1=tb[:], op=ALU.add)
            I("dve", "tensor_tensor", [bb_], [bb_], out=Bb16[:, 0:512], in0=Bbr[:], in1=gm[:], op=ALU.mult)
            I("dve", "tensor_tensor", [bb_], [bb_], out=Bb16[:, 512:1024], in0=Bbi[:], in1=gm[:], op=ALU.mult)
            zt = sb("zt", [128, 128], BF16)
            bz = Buf()
            I("pool", "memset", [], [bz], zt[:], 0.0)
            bD = Buf()
            for gp in range(32):
                for c in range(2):
                    DMA(BTd[i, gp, c], zt[:], [bz], [bD])
            pT_ = Rot("pTp", 2, [128, 1024], BF16, psum=True)
            TB = Rot("TB", 2, [16, 256], BF16)
            for gp in range(32):
                p_, bp_ = pT_.next()
                for c in range(2):
                    I("pe", "transpose", [bb_, b_ident], [bp_], out=p_[0:16, c * 128:(c + 1) * 128],
                      in_=Bb16[:, c * 512 + gp * 16:c * 512 + (gp + 1) * 16], identity=ident[:])
                tbb, btb = TB.next()
                I("act", "activation", [bp_], [btb], out=tbb[:], in_=p_[0:16, 0:256], func=AF.Copy)
                for two in range(2):
                    g = 2 * gp + two
                    r0 = (g % 8) * 16
                    for c in range(2):
                        DMA(BTd[i, gp, c, r0:r0 + 16, two * 64:(two + 1) * 64],
                            tbb[:, c * 128 + two * 64:c * 128 + (two + 1) * 64], [btb, bD], [])
            Cr = sb("Cr", [128, 512], F32)
            Ci = sb("Ci", [128, 512], F32)
            bc_ = Buf()
            for two in range(2):
                ps_ = slice(two * 64, (two + 1) * 64)
                for gp in range(32):
                    g = 2 * gp + two
                    DMA(Cr[ps_, gp * 16:(gp + 1) * 16], c_re[i, g].rearrange("o p -> p o"), [], [bc_],
                        allow_slow_non_contiguous=True)
                    DMA(Ci[ps_, gp * 16:(gp + 1) * 16], c_im[i, g].rearrange("o p -> p o"), [], [bc_],
                        allow_slow_non_contiguous=True)
            I("pool", "memset", [], [b_p], CPr[:], 0.0)
            I("pool", "memset", [], [b_p], CPi[:], 0.0)
            for two in range(2):
                ps_ = slice(two * 64, (two + 1) * 64)
                I("dve", "tensor_copy", [bc_], [b_p],
                  out=CPr[ps_, :].rearrange("p (gp t o) -> p gp t o", t=2, o=16)[:, :, two, :],
                  in_=Cr[ps_, :].rearrange("p (gp o) -> p gp o", o=16))
                I("dve", "tensor_scalar", [bc_], [b_p],
                  out=CPi[ps_, :].rearrange("p (gp t o) -> p gp t o", t=2, o=16)[:, :, two, :],
                  in0=Ci[ps_, :].rearrange("p (gp o) -> p gp o", o=16), scalar1=-1.0, scalar2=None, op0=ALU.mult)
            phase_end()
            return apw, CPr, CPi, b_p, negl

        def s5_pre(src):
            phase_begin()
            xt = Rot("xt", 2, [128, 1024], F32)
            xn = Rot("xn", 2, [128, 1024], BF16)
            uT = Rot("uT", 2, [128, 8 * 512], BF16)
            junk = sb("junk", [128, 1024], F32)
            b_junk = Buf()
            st1 = Rot("st1", 4, [128, 1], F32)
            ps_t = Rot("ps_t", 2, [128, 1024], BF16, psum=True)
            for t5 in range(NT5):
                uTi, buT = uT.next()
                for s_ in range(4):
                    t = t5 * 4 + s_
                    xti, bxt = xt.next()
                    DMA(xti[:], src[t * 128:(t + 1) * 128, :], [], [bxt])
                    st, bst = st1.next()
                    I("act", "activation", [bxt], [b_junk, bst], out=junk[:], in_=xti[:], func=AF.Square,
                      accum_out=st[:, 0:1])
                    rstd_cols(st[:, 0:1], bst, 1024)
                    xnq, bxn = xn.next()
                    I("dve", "tensor_scalar", [bxt, bst], [bxn], out=xnq[:], in0=xti[:], scalar1=st[:, 0:1],
                      scalar2=None, op0=ALU.mult)
                    pst, bpst = ps_t.next()
                    for k in range(8):
                        I("pe", "transpose", [bxn, b_ident], [bpst], out=pst[:, k * 128:(k + 1) * 128],
                          in_=xnq[:, k * 128:(k + 1) * 128], identity=ident[:])
                    I("act", "activation", [bpst], [buT],
                      out=uTi[:].rearrange("p (k t) -> p k t", k=8)[:, :, s_ * 128:(s_ + 1) * 128],
                      in_=pst[:].rearrange("p (k t) -> p k t", k=8), func=AF.Copy)
                DMA(uTd[:, :, t5 * 512:(t5 + 1) * 512].rearrange("k p t -> p k t"),
                    uTi[:].rearrange("p (k t) -> p k t", k=8), [buT], [])
            phase_end()

        def s5_main(i, apw, CPr, CPi, b_p, sb_negl):
            phase_begin()
            PFX = 256
            W = PFX + 512
            uT = Rot("uTf", 2, [128, 512], BF16)
            BT = Rot("BTs", 2, [128, 4 * 2 * 128], BF16)
            bufs = [[sb("sc%d%d" % (a, c), [128, W], F32) for c in range(2)] for a in range(2)]
            bsc = [Buf(), Buf()]
            for a in range(2):
                for c in range(2):
                    I("pool", "memset", [], [bsc[a]], bufs[a][c][:], 0.0)
            carry = sb("carry", [128, 32 * nseq * 2], F32)
            b_car = Buf()
            I("pool", "memset", [], [b_car], carry[:], 0.0)
            HSb = Rot("HSb", 2, [128, 2 * 512], BF16)
            psb = Rot("psb", 4, [128, 512], F32, psum=True)
            psy = Rot("psy", 2, [128, 512], F32, psum=True)
            ysb = Rot("ysb", 2, [128, 512], F32)

            def lev(k, c, gp):
                return apw[:, (k * 2 + c) * 32 + gp:(k * 2 + c) * 32 + gp + 1]

            for fc in range(8):
                BTi, bBT = BT.next()
                DMA(BTi[:].rearrange("p (a c q) -> p a c q", a=4, c=2),
                    BTd[i, fc * 4:(fc + 1) * 4].rearrange("a c p q -> p a c q"), [], [bBT])
                for seq in range(nseq):
                    for blk in range(8):
                        t0_ = seq * SEQ + blk * 512
                        uTi, buT = uT.next()
                        DMA(uTi[:], uTd[fc, :, t0_:t0_ + 512], [], [buT])
                        py, bpy = psy.next()
                        firsty = True
                        for pl in range(4):
                            gp = fc * 4 + pl
                            pbr, bpbr = psb.next()
                            pbi, bpbi = psb.next()
                            for c, (pb_, bpb_) in enumerate(((pbr, bpbr), (pbi, bpbi))):
                                I("pe", "matmul", [bBT, buT], [bpb_], pb_[:],
                                  lhsT=BTi[:, (pl * 2 + c) * 128:(pl * 2 + c + 1) * 128], rhs=uTi[:],
                                  start=True, stop=True)
                            A, Bf = bufs[0], bufs[1]
                            I("act", "activation", [bpbr], [bsc[0]], out=A[0][:, PFX:W], in_=pbr[:], func=AF.Copy)
                            I("act", "activation", [bpbi], [bsc[0]], out=A[1][:, PFX:W], in_=pbi[:], func=AF.Copy)
                            cc = (gp * nseq + seq) * 2
                            cr, ci = carry[:, cc:cc + 1], carry[:, cc + 1:cc + 2]
                            ar, ai = lev(0, 0, gp), lev(0, 1, gp)
                            nai = sb_negl[:, gp:gp + 1]
                            f = slice(PFX, PFX + 1)
                            rd = [bsc[0], b_car, b_p]
                            I("dve", "scalar_tensor_tensor", rd, [bsc[0]], out=A[0][:, f], in0=cr, scalar=ar,
                              in1=A[0][:, f], op0=ALU.mult, op1=ALU.add)
                            I("dve", "scalar_tensor_tensor", rd, [bsc[0]], out=A[0][:, f], in0=ci, scalar=nai,
                              in1=A[0][:, f], op0=ALU.mult, op1=ALU.add)
                            I("dve", "scalar_tensor_tensor", rd, [bsc[0]], out=A[1][:, f], in0=ci, scalar=ar,
                              in1=A[1][:, f], op0=ALU.mult, op1=ALU.add)
                            I("dve", "scalar_tensor_tensor", rd, [bsc[0]], out=A[1][:, f], in0=cr, scalar=ai,
                              in1=A[1][:, f], op0=ALU.mult, op1=ALU.add)
                            src_i = 0
                            for k in range(NLEV):
                                sh = 1 << k
                                X, Y = bufs[src_i], bufs[1 - src_i]
                                bx, by = bsc[src_i], bsc[1 - src_i]
                                kr_, ki_ = lev(k, 0, gp), lev(k, 1, gp)
                                nki = sb_negl[:, k * 32 + gp:k * 32 + gp + 1]
                                cur_ = slice(PFX, W)
                                shf = slice(PFX - sh, W - sh)
                                I("dve", "scalar_tensor_tensor", [bx, b_p], [by], out=Y[0][:, cur_], in0=X[0][:, shf],
                                  scalar=kr_, in1=X[0][:, cur_], op0=ALU.mult, op1=ALU.add)
                                I("dve", "scalar_tensor_tensor", [bx, b_p, by], [by], out=Y[0][:, cur_],
                                  in0=X[1][:, shf], scalar=nki, in1=Y[0][:, cur_], op0=ALU.mult, op1=ALU.add)
                                I("dve", "scalar_tensor_tensor", [bx, b_p], [by], out=Y[1][:, cur_], in0=X[1][:, shf],
                                  scalar=kr_, in1=X[1][:, cur_], op0=ALU.mult, op1=ALU.add)
                                I("dve", "scalar_tensor_tensor", [bx, b_p, by], [by], out=Y[1][:, cur_],
                                  in0=X[0][:, shf], scalar=ki_, in1=Y[1][:, cur_], op0=ALU.mult, op1=ALU.add)
                                src_i = 1 - src_i
                            R_, bR = bufs[src_i], bsc[src_i]
                            I("pool", "tensor_copy", [bR], [b_car], out=carry[:, cc:cc + 1], in_=R_[0][:, W - 1:W])
                            I("pool", "tensor_copy", [bR], [b_car], out=carry[:, cc + 1:cc + 2], in_=R_[1][:, W - 1:W])
                            hs, bhs = HSb.next()
                            I("act", "activation", [bR], [bhs], out=hs[:, 0:512], in_=R_[0][:, PFX:W], func=AF.Copy)
                            I("pool", "tensor_copy", [bR], [bhs], out=hs[:, 512:1024], in_=R_[1][:, PFX:W])
                            for s_ in range(4):
                                for c, CP in enumerate((CPr, CPi)):
                                    I("pe", "matmul", [bhs, b_p], [bpy],
                                      py[:, s_ * 128 + pl * 32:s_ * 128 + (pl + 1) * 32],
                                      lhsT=hs[:, c * 512 + s_ * 128:c * 512 + (s_ + 1) * 128],
                                      rhs=CP[:, gp * 32:(gp + 1) * 32], start=firsty, stop=False,
                                      skip_group_check=True)
                                    firsty = False
                        yo, byo = ysb.next()
                        I("act", "activation", [bpy], [byo], out=yo[:], in_=py[:], func=AF.Copy)
                        DMA(yd[t0_:t0_ + 512, fc * 128:(fc + 1) * 128].rearrange("(s p) c -> p s c", p=128),
                            yo[:].rearrange("p (s c) -> p s c", s=4), [byo], [])
            phase_end()

        def s5_post(i, L, src, dst):
            phase_begin()
            wg = sb("wg", [128, 8 * 2048], BF16)
            b_wg = Buf()
            for k in range(4):
                DMA(wg[:, k * 4096:(k + 1) * 4096], wgluc[i][:, k * 4096:(k + 1) * 4096], [], [b_wg])
            dp = sb("dp", [128, 1024], F32)
            gmb = sb("gmb", [128, 1024], F32)
            b_dp = Buf()
            DMA(dp[:], d_skip[i:i + 1, :].broadcast_to([128, 1024]), [], [b_dp], allow_slow_non_contiguous=True)
            DMA(gmb[:], mix_g[L:L + 1, :].broadcast_to([128, 1024]), [], [b_dp], allow_slow_non_contiguous=True)
            I("dve", "tensor_tensor", [b_dp], [b_dp], out=dp[:], in0=dp[:], in1=gmb[:], op=ALU.mult)
            xt = Rot("xt", 6, [128, 1024], F32)
            yt = Rot("yt", 4, [128, 1024], F32)
            u_ = Rot("u", 3, [128, 1024], F32)
            w_ = Rot("w", 3, [128, 1024], F32)
            gb = Rot("gb", 3, [128, 1024], BF16)
            gT = Rot("gT", 3, [128, 1024], BF16)
            sg = Rot("sg", 2, [128, 1024], F32)
            junk = sb("junk", [128, 1024], F32)
            b_junk = Buf()
            st1 = Rot("st1", 6, [128, 1], F32)
            ps_t = Rot("ps_t", 2, [128, 1024], BF16, psum=True)
            pm = Rot("pm", 6, [128, 512], F32, psum=True)
            T_ = {}

            def sA(t):
                rows = slice(t * 128, (t + 1) * 128)
                xti, bxt = xt.next()
                yti, byt = yt.next()
                DMA(xti[:], src[rows, :], [], [bxt])
                DMA(yti[:], yd[rows, :], [], [byt])
                st, bst = st1.next()
                I("act", "activation", [bxt], [b_junk, bst], out=junk[:], in_=xti[:], func=AF.Square,
                  accum_out=st[:, 0:1])
                rstd_cols(st[:, 0:1], bst, 1024)
                ui, bu = u_.next()
                I("dve", "scalar_tensor_tensor", [bxt, bst, b_dp], [bu], out=ui[:], in0=xti[:], scalar=st[:, 0:1],
                  in1=dp[:], op0=ALU.mult, op1=ALU.mult)
                T_[t] = dict(x=(xti, bxt), y=(yti, byt), u=(ui, bu))

            def sB(t):
                d = T_[t]
                yti, byt = d["y"]
                ui, bu = d["u"]
                I("pool", "tensor_tensor", [bu, byt], [byt], out=yti[:], in0=yti[:], in1=ui[:], op=ALU.add)
                wi_, bw = w_.next()
                I("act", "activation", [byt], [bw], out=wi_[:], in_=yti[:], func=AF.Square)
                I("dve", "tensor_scalar", [bw], [bw], out=wi_[:], in0=wi_[:], scalar1=0.044715, scalar2=1.0,
                  op0=ALU.mult, op1=ALU.add)
                I("pool", "tensor_tensor", [bw, byt], [bw], out=wi_[:], in0=wi_[:], in1=yti[:], op=ALU.mult)
                I("act", "activation", [bw], [bw], out=wi_[:], in_=wi_[:], func=AF.Sigmoid, scale=1.5957691216057308)
                gbi, bgb = gb.next()
                I("dve", "tensor_tensor", [bw, byt], [bgb], out=gbi[:], in0=wi_[:], in1=yti[:], op=ALU.mult)
                d["g"] = (gbi, bgb)

            def sC(t):
                d = T_[t]
                gbi, bgb = d["g"]
                pst, bpst = ps_t.next()
                for k in range(8):
                    I("pe", "transpose", [bgb, b_ident], [bpst], out=pst[:, k * 128:(k + 1) * 128],
                      in_=gbi[:, k * 128:(k + 1) * 128], identity=ident[:])
                gTi, bgT = gT.next()
                I("act", "activation", [bpst], [bgT], out=gTi[:], in_=pst[:], func=AF.Copy)
                pvd = {}
                for c in (2, 3, 0, 1):
                    p_, bp_ = pm.next()
                    for k in range(8):
                        I("pe", "matmul", [bgT, b_wg], [bp_], p_[:], lhsT=gTi[:, k * 128:(k + 1) * 128],
                          rhs=wg[:, k * 2048 + c * 512:k * 2048 + (c + 1) * 512], start=(k == 0), stop=(k == 7))
                    pvd[c] = (p_, bp_)
                d["pv"] = pvd

            def sD(t):
                rows = slice(t * 128, (t + 1) * 128)
                d = T_.pop(t)
                pvd = d["pv"]
                xti, bxt = d["x"]
                sgi, bsg = sg.next()
                for c in range(2):
                    cs_ = slice(c * 512, (c + 1) * 512)
                    I("act", "activation", [pvd[2 + c][1]], [bsg], out=sgi[:, cs_], in_=pvd[2 + c][0][:], func=AF.Sigmoid)
                    I("dve", "tensor_tensor", [pvd[c][1], bsg], [bsg], out=sgi[:, cs_], in0=pvd[c][0][:],
                      in1=sgi[:, cs_], op=ALU.mult)
                    I("pool", "tensor_tensor", [bsg, bxt], [bxt], out=xti[:, cs_], in0=sgi[:, cs_], in1=xti[:, cs_],
                      op=ALU.add)
                DMA(dst[rows, :], xti[:], [bxt], [], q="pool")

            skew([sA, sB, sC, sD], NT)
            phase_end()

        def s5_prep2(i, L):
            apw = sb("apw", [128, NLEV * 2 * 32], F32)
            negl = sb("negl", [128, NLEV * 32], F32)
            CPLr = sb("CPLr", [128, 8 * 1024], BF16)
            CPLi = sb("CPLi", [128, 8 * 1024], BF16)
            a8 = sb("a8", [128, 3 * 32], F32)
            b_p = Buf()
            phase_begin()
            lr = sb("lr", [128, 32], F32)
            li = sb("li", [128, 32], F32)
            dt = sb("dt", [128, 32], F32)
            bt = Buf()
            for two in range(2):
                ps_ = slice(two * 64, (two + 1) * 64)
                DMA(lr[ps_, :], lam_re[i].rearrange("(gp two) p -> two p gp", two=2)[two], [], [bt],
                    allow_slow_non_contiguous=True)
                DMA(li[ps_, :], lam_im[i].rearrange("(gp two) p -> two p gp", two=2)[two], [], [bt],
                    allow_slow_non_contiguous=True)
                DMA(dt[ps_, :], log_dt[i].rearrange("(gp two) -> two gp", two=2)[two:two + 1, :].broadcast_to([64, 32]),
                    [], [bt], allow_slow_non_contiguous=True)
            tl = [sb("t%d" % k, [128, 32], F32) for k in range(12)]
            er, ang, kk, sn, cs, ca, m2, t0, t1, wr, wi, den = tl
            APD = sb("APD", [128, 17 * 2 * 32], F32)

            def apd(d, c):
                return APD[:, (d * 2 + c) * 32:(d * 2 + c + 1) * 32]

            def V(name, out_, **kw):
                I("dve", name, [bt], [bt], out=out_, **kw)

            I("act", "activation", [bt], [bt], out=dt[:], in_=dt[:], func=AF.Exp)
            V("tensor_tensor", er[:], in0=lr[:], in1=dt[:], op=ALU.mult)
            I("act", "activation", [bt], [bt], out=er[:], in_=er[:], func=AF.Exp)
            V("tensor_tensor", ang[:], in0=li[:], in1=dt[:], op=ALU.mult)
            I("pool", "memset", [bt], [bt], kk[:], 0.0)
            for m in range(1, 5):
                V("scalar_tensor_tensor", kk[:], in0=ang[:], scalar=(2 * m - 1) * PI, in1=kk[:], op0=ALU.is_gt,
                  op1=ALU.add)
            V("scalar_tensor_tensor", ang[:], in0=kk[:], scalar=-2 * PI, in1=ang[:], op0=ALU.mult, op1=ALU.add)
            I("act", "activation", [bt], [bt], out=sn[:], in_=ang[:], func=AF.Sin)
            V("tensor_scalar", ca[:], in0=ang[:], scalar1=PI / 2, scalar2=None, op0=ALU.add)
            V("tensor_scalar", m2[:], in0=ca[:], scalar1=PI, scalar2=None, op0=ALU.is_gt)
            V("scalar_tensor_tensor", ca[:], in0=m2[:], scalar=-2 * PI, in1=ca[:], op0=ALU.mult, op1=ALU.add)
            I("act", "activation", [bt], [bt], out=cs[:], in_=ca[:], func=AF.Sin)
            I("pool", "memset", [bt], [bt], apd(0, 0), 1.0)
            I("pool", "memset", [bt], [bt], apd(0, 1), 0.0)
            V("tensor_tensor", apd(1, 0), in0=er[:], in1=cs[:], op=ALU.mult)
            V("tensor_tensor", apd(1, 1), in0=er[:], in1=sn[:], op=ALU.mult)
            ar, ai = apd(1, 0), apd(1, 1)
            for d in range(2, 17):
                V("tensor_tensor", t0[:], in0=apd(d - 1, 0), in1=ar, op=ALU.mult)
                V("tensor_tensor", t1[:], in0=apd(d - 1, 1), in1=ai, op=ALU.mult)
                V("tensor_tensor", apd(d, 0), in0=t0[:], in1=t1[:], op=ALU.subtract)
                V("tensor_tensor", t0[:], in0=apd(d - 1, 0), in1=ai, op=ALU.mult)
                V("tensor_tensor", t1[:], in0=apd(d - 1, 1), in1=ar, op=ALU.mult)
                V("tensor_tensor", apd(d, 1), in0=t0[:], in1=t1[:], op=ALU.add)

            def lev(k, c):
                return apw[:, (k * 2 + c) * 32:(k * 2 + c + 1) * 32]

            I("dve", "tensor_copy", [bt], [b_p], out=lev(0, 0), in_=apd(16, 0))
            I("dve", "tensor_copy", [bt], [b_p], out=lev(0, 1), in_=apd(16, 1))
            I("dve", "tensor_copy", [bt], [b_p], out=a8[:, 0:32], in_=apd(8, 0))
            I("dve", "tensor_copy", [bt], [b_p], out=a8[:, 32:64], in_=apd(8, 1))
            I("dve", "tensor_scalar", [bt], [b_p], out=a8[:, 64:96], in0=apd(8, 1), scalar1=-1.0, scalar2=None,
              op0=ALU.mult)
            for k in range(1, NLEV):
                pr_, pi_ = lev(k - 1, 0), lev(k - 1, 1)
                I("dve", "tensor_tensor", [b_p], [bt], out=t0[:], in0=pr_, in1=pr_, op=ALU.mult)
                I("dve", "tensor_tensor", [b_p], [bt], out=t1[:], in0=pi_, in1=pi_, op=ALU.mult)
                I("dve", "tensor_tensor", [bt], [b_p], out=lev(k, 0), in0=t0[:], in1=t1[:], op=ALU.subtract)
                I("dve", "scalar_tensor_tensor", [b_p], [b_p], out=lev(k, 1), in0=pr_, scalar=2.0, in1=pi_,
                  op0=ALU.mult, op1=ALU.mult)
            I("dve", "tensor_scalar", [b_p], [b_p], out=negl[:].rearrange("p (k g) -> p k g", k=NLEV),
              in0=apw[:].rearrange("p (k c g) -> p k c g", k=NLEV, c=2)[:, :, 1, :], scalar1=-1.0, scalar2=None,
              op0=ALU.mult)
            V("tensor_tensor", t0[:], in0=lr[:], in1=lr[:], op=ALU.mult)
            V("tensor_tensor", t1[:], in0=li[:], in1=li[:], op=ALU.mult)
            V("tensor_tensor", den[:], in0=t0[:], in1=t1[:], op=ALU.add)
            V("reciprocal", den[:], in_=den[:])
            V("tensor_scalar", t0[:], in0=ar, scalar1=-1.0, scalar2=None, op0=ALU.add)
            V("tensor_tensor", wr[:], in0=t0[:], in1=lr[:], op=ALU.mult)
            V("tensor_tensor", t1[:], in0=ai, in1=li[:], op=ALU.mult)
            V("tensor_tensor", wr[:], in0=wr[:], in1=t1[:], op=ALU.add)
            V("tensor_tensor", wr[:], in0=wr[:], in1=den[:], op=ALU.mult)
            V("tensor_tensor", wi[:], in0=ai, in1=lr[:], op=ALU.mult)
            V("tensor_tensor", t1[:], in0=t0[:], in1=li[:], op=ALU.mult)
            V("tensor_tensor", wi[:], in0=wi[:], in1=t1[:], op=ALU.subtract)
            V("tensor_tensor", wi[:], in0=wi[:], in1=den[:], op=ALU.mult)
            Br = sb("Br", [128, 512], F32)
            Bi = sb("Bi", [128, 512], F32)
            gm = sb("gm", [128, 512], F32)
            bb_ = Buf()
            for two in range(2):
                ps_ = slice(two * 64, (two + 1) * 64)
                for gp in range(32):
                    g = 2 * gp + two
                    DMA(Br[ps_, gp * 16:(gp + 1) * 16], b_re[i, g], [], [bb_])
                    DMA(Bi[ps_, gp * 16:(gp + 1) * 16], b_im[i, g], [], [bb_])
                DMA(gm[ps_, :].rearrange("p (gp i) -> p gp i", i=16),
                    mix_g[L].rearrange("(gp two i) -> two gp i", two=2, i=16)[two:two + 1].broadcast_to([64, 32, 16]),
                    [], [bb_], allow_slow_non_contiguous=True)
            Bd = [[sb("Bd%d%d" % (a_, c), [128, 512], F32) for c in range(2)] for a_ in range(2)]
            tb = sb("tb", [128, 512], F32)
            tb2 = sb("tb2", [128, 512], F32)
            BbA = sb("BbA", [128, 16 * 2 * 1024], BF16)
            bA = Buf()
            I("pool", "memset", [], [bA], BbA[:], 0.0)

            def v3(t_):
                return t_[:].rearrange("p (gp i) -> p gp i", i=16)

            def bc(ap32):
                return ap32.unsqueeze(2).broadcast_to([128, 32, 16])

            rd = [bt, bb_]
            I("dve", "tensor_tensor", rd, [bb_], out=v3(tb), in0=v3(Br), in1=bc(wr[:]), op=ALU.mult)
            I("dve", "tensor_tensor", rd, [bb_], out=v3(tb2), in0=v3(Bi), in1=bc(wi[:]), op=ALU.mult)
            I("dve", "tensor_tensor", [bb_], [bb_], out=tb[:], in0=tb[:], in1=tb2[:], op=ALU.subtract)
            I("dve", "tensor_tensor", [bb_], [bb_], out=Bd[0][0][:], in0=tb[:], in1=gm[:], op=ALU.mult)
            I("dve", "tensor_tensor", rd, [bb_], out=v3(tb), in0=v3(Bi), in1=bc(wr[:]), op=ALU.mult)
            I("dve", "tensor_tensor", rd, [bb_], out=v3(tb2), in0=v3(Br), in1=bc(wi[:]), op=ALU.mult)
            I("dve", "tensor_tensor", [bb_], [bb_], out=tb[:], in0=tb[:], in1=tb2[:], op=ALU.add)
            I("dve", "tensor_tensor", [bb_], [bb_], out=Bd[0][1][:], in0=tb[:], in1=gm[:], op=ALU.mult)
            for d in range(16):
                X = Bd[d % 2]
                Y = Bd[1 - d % 2]
                for c in range(2):
                    for two in range(2):
                        ps_ = slice(two * 64, (two + 1) * 64)
                        I("pool" if c else "act", "tensor_copy" if c else "copy", [bb_], [bA],
                          out=BbA[ps_, :].rearrange("p (d c gp t i) -> p d c gp t i", d=16, c=2, gp=32, t=2)[:, d, c, :, two, :],
                          in_=X[c][ps_, :].rearrange("p (gp i) -> p gp i", i=16))
                if d < 15:
                    I("dve", "tensor_tensor", rd, [bb_], out=v3(tb), in0=v3(X[0]), in1=bc(ar), op=ALU.mult)
                    I("dve", "tensor_tensor", rd, [bb_], out=v3(tb2), in0=v3(X[1]), in1=bc(ai), op=ALU.mult)
                    I("dve", "tensor_tensor", [bb_], [bb_], out=Y[0][:], in0=tb[:], in1=tb2[:], op=ALU.subtract)
                    I("dve", "tensor_tensor", rd, [bb_], out=v3(tb), in0=v3(X[0]), in1=bc(ai), op=ALU.mult)
                    I("dve", "tensor_tensor", rd, [bb_], out=v3(tb2), in0=v3(X[1]), in1=bc(ar), op=ALU.mult)
                    I("dve", "tensor_tensor", [bb_], [bb_], out=Y[1][:], in0=tb[:], in1=tb2[:], op=ALU.add)
            pT_ = Rot("pTp", 2, [128, 2048], BF16, psum=True)
            TB = Rot("TB", 3, [32, 2048], BF16)
            for gp in range(32):
                for c in range(2):
                    p_, bp_ = pT_.next()
                    for d in range(16):
                        off = ((d * 2 + c) * 32 + gp) * 32
                        I("pe", "transpose", [bA, b_ident], [bp_], out=p_[0:32, d * 128:(d + 1) * 128],
                          in_=BbA[:, off:off + 32], identity=ident[:])
                    tbb, btb = TB.next()
                    I("act", "activation", [bp_], [btb], out=tbb[:], in_=p_[0:32, :], func=AF.Copy)
                    DMA(BTd2[i, gp, :, c].rearrange("d r q -> r d q"), tbb[:].rearrange("r (d q) -> r d q", d=16),
                        [btb], [])
            Cr = sb("Cr", [128, 512], F32)
            Ci = sb("Ci", [128, 512], F32)
            bc_ = Buf()
            for two in range(2):
                ps_ = slice(two * 64, (two + 1) * 64)
                for gp in range(32):
                    g = 2 * gp + two
                    DMA(Cr[ps_, gp * 16:(gp + 1) * 16], c_re[i, g].rearrange("o p -> p o"), [], [bc_],
                        allow_slow_non_contiguous=True)
                    DMA(Ci[ps_, gp * 16:(gp + 1) * 16], c_im[i, g].rearrange("o p -> p o"), [], [bc_],
                        allow_slow_non_contiguous=True)
            CP0r = sb("CP0r", [128, 1024], BF16)
            CP0i = sb("CP0i", [128, 1024], BF16)
            b0 = Buf()
            I("pool", "memset", [], [b0], CP0r[:], 0.0)
            I("pool", "memset", [], [b0], CP0i[:], 0.0)
            I("pool", "memset", [], [b_p], CPLr[:], 0.0)
            I("pool", "memset", [], [b_p], CPLi[:], 0.0)
            for two in range(2):
                ps_ = slice(two * 64, (two + 1) * 64)
                I("dve", "tensor_copy", [bc_], [b0],
                  out=CP0r[ps_, :].rearrange("p (gp t o) -> p gp t o", t=2, o=16)[:, :, two, :],
                  in_=Cr[ps_, :].rearrange("p (gp o) -> p gp o", o=16))
                I("dve", "tensor_scalar", [bc_], [b0],
                  out=CP0i[ps_, :].rearrange("p (gp t o) -> p gp t o", t=2, o=16)[:, :, two, :],
                  in0=Ci[ps_, :].rearrange("p (gp o) -> p gp o", o=16), scalar1=-1.0, scalar2=None, op0=ALU.mult)
            rdc = [bt, bc_, bb_]
            for l in range(8):
                Ar, Ai = apd(l + 1, 0), apd(l + 1, 1)
                I("dve", "tensor_tensor", rdc, [bb_], out=v3(tb), in0=v3(Cr), in1=bc(Ar), op=ALU.mult)
                I("dve", "tensor_tensor", rdc, [bb_], out=v3(tb2), in0=v3(Ci), in1=bc(Ai), op=ALU.mult)
                I("dve", "tensor_tensor", [bb_], [bb_], out=tb[:], in0=tb[:], in1=tb2[:], op=ALU.subtract)
                for two in range(2):
                    ps_ = slice(two * 64, (two + 1) * 64)
                    I("pool", "tensor_copy", [bb_], [b_p],
                      out=CPLr[ps_, :].rearrange("p (l gp t o) -> p l gp t o", l=8, t=2, o=16)[:, l, :, two, :],
                      in_=tb[ps_, :].rearrange("p (gp o) -> p gp o", o=16))
                I("dve", "tensor_tensor", rdc, [bb_], out=v3(tb), in0=v3(Cr), in1=bc(Ai), op=ALU.mult)
                I("dve", "tensor_tensor", rdc, [bb_], out=v3(tb2), in0=v3(Ci), in1=bc(Ar), op=ALU.mult)
                I("dve", "tensor_tensor", [bb_], [bb_], out=tb[:], in0=tb[:], in1=tb2[:], op=ALU.add)
                for two in range(2):
                    ps_ = slice(two * 64, (two + 1) * 64)
                    I("pool", "tensor_scalar", [bb_], [b_p],
                      out=CPLi[ps_, :].rearrange("p (l gp t o) -> p l gp t o", l=8, t=2, o=16)[:, l, :, two, :],
                      in0=tb[ps_, :].rearrange("p (gp o) -> p gp o", o=16), scalar1=-1.0, scalar2=None,
                      op0=ALU.mult)
            pMa = Rot("pMa", 2, [128, 512], F32, psum=True)
            pMb = Rot("pMb", 2, [128, 512], F32, psum=True)
            Ml = Rot("Mlp", 2, [128, 1024], BF16)
            for fc in range(8):
                pa, bpa = pMa.next()
                pb, bpb = pMb.next()
                I("dve", "memset", [], [bpa], pa[:], 0.0)
                I("dve", "memset", [], [bpb], pb[:], 0.0)
                for pl in range(4):
                    gp = fc * 4 + pl
                    for d in range(8):
                        pk, bpk = (pa, bpa) if d < 4 else (pb, bpb)
                        for c in range(2):
                            off = ((d * 2 + c) * 32 + gp) * 32
                            CP0 = CP0r if c == 0 else CP0i
                            I("pe", "matmul", [bA, b0], [bpk],
                              pk[pl * 32:(pl + 1) * 32, (d % 4) * 128 + pl * 32:(d % 4) * 128 + (pl + 1) * 32],
                              lhsT=BbA[:, off:off + 32], rhs=CP0[:, gp * 32:(gp + 1) * 32],
                              start=(d % 4 == 0 and c == 0), stop=False, skip_group_check=True,
                              tile_position=(0, 32 * pl))
                mli, bml = Ml.next()
                I("act", "activation", [bpa], [bml], out=mli[:, 0:512], in_=pa[:], func=AF.Copy)
                I("act", "activation", [bpb], [bml], out=mli[:, 512:1024], in_=pb[:], func=AF.Copy)
                DMA(Mlagd[i, fc], mli[:], [bml], [])
            phase_end()
            return apw, negl, CPLr, CPLi, b_p, a8

        def s5_pre2(src):
            phase_begin()
            xt = Rot("xt", 3, [128, 1024], F32)
            xn = Rot("xn", 3, [128, 1024], BF16)
            uT = Rot("uT", 3, [128, 8 * 8 * 64], BF16)
            junk = sb("junk", [128, 1024], F32)
            b_junk = Buf()
            st1 = Rot("st1", 6, [128, 1], F32)
            ps_t = Rot("ps_t", 3, [128, 1024], BF16, psum=True)
            T_ = {}
            U_ = {}

            def sA(t):
                xti, bxt = xt.next()
                DMA(xti[:], src[t * 128:(t + 1) * 128, :], [], [bxt])
                st, bst = st1.next()
                I("act", "activation", [bxt], [b_junk, bst], out=junk[:], in_=xti[:], func=AF.Square,
                  accum_out=st[:, 0:1])
                rstd_cols(st[:, 0:1], bst, 1024)
                xnq, bxn = xn.next()
                I("dve", "tensor_scalar", [bxt, bst], [bxn], out=xnq[:], in0=xti[:], scalar1=st[:, 0:1],
                  scalar2=None, op0=ALU.mult)
                T_[t] = (xnq, bxn)

            def sB(t):
                t5, s_ = divmod(t, 4)
                seq, blk = divmod(t5, 8)
                if s_ == 0:
                    U_[t5] = uT.next()
                uTi, buT = U_[t5]
                u4 = uTi[:].rearrange("p (k j n) -> p k j n", k=8, j=8)
                xnq, bxn = T_.pop(t)
                pst, bpst = ps_t.next()
                for k in range(8):
                    I("pe", "transpose", [bxn, b_ident], [bpst], out=pst[:, k * 128:(k + 1) * 128],
                      in_=xnq[:, k * 128:(k + 1) * 128], identity=ident[:])
                for j in range(8):
                    eng = "act" if j % 2 == 0 else "dve"
                    if eng == "act":
                        I("act", "activation", [bpst], [buT], out=u4[:, :, j, s_ * 16:(s_ + 1) * 16],
                          in_=pst[:].rearrange("p (k n j) -> p k n j", k=8, j=8)[:, :, :, j], func=AF.Copy)
                    else:
                        I("dve", "tensor_copy", [bpst], [buT], out=u4[:, :, j, s_ * 16:(s_ + 1) * 16],
                          in_=pst[:].rearrange("p (k n j) -> p k n j", k=8, j=8)[:, :, :, j])
                if s_ == 3:
                    for k in range(8):
                        DMA(uTd2[k, :, seq, :, blk * 64:(blk + 1) * 64], u4[:, k], [buT], [], q="pool")
                    U_.pop(t5)

            skew([sA, sB], NT)
            phase_end()

        def s5_main2(i, apw, sb_negl, CPLr, CPLi, b_p, a8):
            phase_begin()
            PFX = 128
            NS = 256
            W = PFX + NS
            uT = Rot("uTf", 2, [128, 8 * 512], BF16)
            BT = Rot("BTs", 2, [128, 16 * 2 * 128], BF16)
            s8e = Rot("s8e", 2, [128, 2 * NS], F32)
            otmp = Rot("otmp", 2, [128, 2 * NS], F32)
            Ml = Rot("Mls", 2, [128, 1024], BF16)
            bufs = [[sb("sc%d%d" % (a, c), [128, W], F32) for c in range(2)] for a in range(2)]
            bsc = [Buf(), Buf()]
            for a in range(2):
                for c in range(2):
                    I("pool", "memset", [], [bsc[a]], bufs[a][c][:], 0.0)
            Zb = Rot("Zb", 2, [128, 4 * 2 * 512], BF16)
            psb = Rot("psb", 4, [128, 512], F32, psum=True)
            psya = Rot("psya", 2, [128, 512], F32, psum=True)
            psyb = Rot("psyb", 2, [128, 512], F32, psum=True)
            ysb = Rot("ysb", 2, [128, 1024], F32)

            def lev(k, c, gp):
                return apw[:, (k * 2 + c) * 32 + gp:(k * 2 + c) * 32 + gp + 1]

            for fc in range(8):
                BTi, bBT = BT.next()
                for pl in range(4):
                    DMA(BTi[pl * 32:(pl + 1) * 32, :].rearrange("p (d c q) -> p d c q", d=16, c=2),
                        BTd2[i, fc * 4 + pl].rearrange("d c r q -> r d c q"), [], [bBT])
                Mli, bMl = Ml.next()
                DMA(Mli[:], Mlagd[i, fc], [], [bMl])
                for seq in range(nseq):
                    uTi, buT = uT.next()
                    DMA(uTi[:], uTd2[fc, :, seq].rearrange("p j n -> p (j n)"), [], [buT])
                    Zi, bZ = Zb.next()
                    for pl in range(4):
                        gp = fc * 4 + pl
                        rows = slice(pl * 32, (pl + 1) * 32)
                        pcs = []
                        pes = []
                        for c in range(2):
                            pb_, bpb_ = psb.next()
                            for j16 in range(16):
                                d = 15 - j16
                                j, hi = j16 % 8, j16 // 8
                                I("pe", "matmul", [bBT, buT], [bpb_], pb_[:, 0:NS],
                                  lhsT=BTi[rows, (d * 2 + c) * 128:(d * 2 + c + 1) * 128],
                                  rhs=uTi[rows, j * 512 + hi:(j + 1) * 512:2], start=(j16 == 0), stop=(j16 == 15),
                                  tile_position=(32 * pl, 0))
                            pcs.append((pb_, bpb_))
                        for c in range(2):
                            pb_, bpb_ = psb.next()
                            for j in range(8):
                                d = 7 - j
                                I("pe", "matmul", [bBT, buT], [bpb_], pb_[:, 0:NS],
                                  lhsT=BTi[rows, (d * 2 + c) * 128:(d * 2 + c + 1) * 128],
                                  rhs=uTi[rows, j * 512:(j + 1) * 512:2], start=(j == 0), stop=(j == 7),
                                  tile_position=(32 * pl, 0))
                            pes.append((pb_, bpb_))
                        A = bufs[0]
                        I("act", "activation", [pcs[0][1]], [bsc[0]], out=A[0][:, PFX:W], in_=pcs[0][0][:, 0:NS], func=AF.Copy)
                        I("act", "activation", [pcs[1][1]], [bsc[0]], out=A[1][:, PFX:W], in_=pcs[1][0][:, 0:NS], func=AF.Copy)
                        se, bse = s8e.next()
                        I("act", "activation", [pes[0][1]], [bse], out=se[:, 0:NS], in_=pes[0][0][:, 0:NS], func=AF.Copy)
                        I("act", "activation", [pes[1][1]], [bse], out=se[:, NS:2 * NS], in_=pes[1][0][:, 0:NS], func=AF.Copy)
                        src_i = 0
                        for k in range(8):
                            sh = 1 << k
                            X, Y = bufs[src_i], bufs[1 - src_i]
                            bx, by = bsc[src_i], bsc[1 - src_i]
                            kr_, ki_ = lev(k, 0, gp), lev(k, 1, gp)
                            nki = sb_negl[:, k * 32 + gp:k * 32 + gp + 1]
                            cur_ = slice(PFX, W)
                            shf = slice(PFX - sh, W - sh)
                            I("dve", "scalar_tensor_tensor", [bx, b_p], [by], out=Y[0][:, cur_], in0=X[0][:, shf],
                              scalar=kr_, in1=X[0][:, cur_], op0=ALU.mult, op1=ALU.add)
                            I("dve", "scalar_tensor_tensor", [bx, b_p, by], [by], out=Y[0][:, cur_],
                              in0=X[1][:, shf], scalar=nki, in1=Y[0][:, cur_], op0=ALU.mult, op1=ALU.add)
                            I("dve", "scalar_tensor_tensor", [bx, b_p], [by], out=Y[1][:, cur_], in0=X[1][:, shf],
                              scalar=kr_, in1=X[1][:, cur_], op0=ALU.mult, op1=ALU.add)
                            I("dve", "scalar_tensor_tensor", [bx, b_p, by], [by], out=Y[1][:, cur_],
                              in0=X[0][:, shf], scalar=ki_, in1=Y[1][:, cur_], op0=ALU.mult, op1=ALU.add)
                            src_i = 1 - src_i
                        R_, bR = bufs[src_i], bsc[src_i]
                        zr, zi_ = R_[0][:, PFX - 1:W - 1], R_[1][:, PFX - 1:W - 1]
                        or_ = slice((pl * 2) * 512, (pl * 2 + 1) * 512)
                        oi_ = slice((pl * 2 + 1) * 512, (pl * 2 + 2) * 512)
                        I("act", "activation", [bR], [bZ], out=Zi[:, (pl * 2) * 512:(pl * 2 + 1) * 512:2], in_=zr, func=AF.Copy)
                        I("act", "activation", [bR], [bZ], out=Zi[:, (pl * 2 + 1) * 512:(pl * 2 + 2) * 512:2], in_=zi_,
                          func=AF.Copy)
                        ot, bot_ = otmp.next()
                        ar8, ai8, nai8 = (a8[:, gp:gp + 1], a8[:, 32 + gp:33 + gp], a8[:, 64 + gp:65 + gp])
                        I("dve", "scalar_tensor_tensor", [bR, bse, b_p], [bot_], out=ot[:, 0:NS], in0=zr, scalar=ar8,
                          in1=se[:, 0:NS], op0=ALU.mult, op1=ALU.add)
                        I("dve", "scalar_tensor_tensor", [bR, bot_, b_p], [bZ],
                          out=Zi[:, (pl * 2) * 512 + 1:(pl * 2 + 1) * 512:2], in0=zi_, scalar=nai8, in1=ot[:, 0:NS],
                          op0=ALU.mult, op1=ALU.add)
                        I("dve", "scalar_tensor_tensor", [bR, bse, b_p], [bot_], out=ot[:, NS:2 * NS], in0=zi_, scalar=ar8,
                          in1=se[:, NS:2 * NS], op0=ALU.mult, op1=ALU.add)
                        I("dve", "scalar_tensor_tensor", [bR, bot_, b_p], [bZ],
                          out=Zi[:, (pl * 2 + 1) * 512 + 1:(pl * 2 + 2) * 512:2], in0=zr, scalar=ai8,
                          in1=ot[:, NS:2 * NS], op0=ALU.mult, op1=ALU.add)
                    for nb in range(4):
                        pa, bpa = psya.next()
                        pb, bpb = psyb.next()
                        ncol = slice(nb * 128, (nb + 1) * 128)
                        for j in range(8):
                            lhs = uTi[:, j * 512 + nb * 128:j * 512 + (nb + 1) * 128]
                            if j <= 3:
                                I("pe", "matmul", [buT, bMl], [bpa], pa[:, j * 128:512], lhsT=lhs,
                                  rhs=Mli[:, 0:(4 - j) * 128], start=(j == 0), stop=False, skip_group_check=True)
                                I("pe", "matmul", [buT, bMl], [bpb], pb[:, 0:512], lhsT=lhs,
                                  rhs=Mli[:, (4 - j) * 128:(8 - j) * 128], start=(j == 0), stop=False,
                                  skip_group_check=True)
                            else:
                                I("pe", "matmul", [buT, bMl], [bpb], pb[:, (j - 4) * 128:512], lhsT=lhs,
                                  rhs=Mli[:, 0:(8 - j) * 128], start=False, stop=False, skip_group_check=True)
                        for pl in range(4):
                            gp = fc * 4 + pl
                            for l in range(8):
                                pk, bpk = (pa, bpa) if l < 4 else (pb, bpb)
                                for c, CPL in enumerate((CPLr, CPLi)):
                                    I("pe", "matmul", [bZ, b_p], [bpk],
                                      pk[:, (l % 4) * 128 + pl * 32:(l % 4) * 128 + (pl + 1) * 32],
                                      lhsT=Zi[:, (pl * 2 + c) * 512 + nb * 128:(pl * 2 + c) * 512 + (nb + 1) * 128],
                                      rhs=CPL[:, (l * 32 + gp) * 32:(l * 32 + gp + 1) * 32], start=False, stop=False,
                                      skip_group_check=True)
                        yo, byo = ysb.next()
                        I("act", "activation", [bpa], [byo], out=yo[:, 0:512], in_=pa[:], func=AF.Copy)
                        I("act", "activation", [bpb], [byo], out=yo[:, 512:1024], in_=pb[:], func=AF.Copy)
                        r0 = seq * SEQ + nb * 1024
                        DMA(yd[r0:r0 + 1024, fc * 128:(fc + 1) * 128].rearrange("(n l) c -> n l c", l=8),
                            yo[:].rearrange("p (l c) -> p l c", l=8), [byo], [], q="act")
            phase_end()

        prep_phase()
        cur = x_in
        for li, L in enumerate(layers):
            last = (li == len(layers) - 1)
            if do_mix:
                mdst = xm if do_ffn or not last else out
                if L % 2 == 0:
                    ph = cfg.get("ph", "abc")
                    if "a" in ph:
                        attn_phase_a(L // 2, L, cur)
                    if "b" in ph:
                        attn_phase_b()
                    if "c" in ph:
                        attn_phase_c(L // 2, cur, mdst)
                    cur = mdst
                else:
                    i = L // 2
                    if cfg.get("s5v", 2) == 1:
                        apw, CPr, CPi, b_p, negl = s5_prep(i, L)
                        s5_pre(cur)
                        s5_main(i, apw, CPr, CPi, b_p, negl)
                    else:
                        sph = cfg.get("s5ph", "prmo")
                        phase_begin()
                        apw, negl, CPLr, CPLi, b_p, a8 = s5_prep2(i, L)
                        if "r" in sph:
                            s5_pre2(cur)
                        if "m" in sph:
                            s5_main2(i, apw, negl, CPLr, CPLi, b_p, a8)
                        phase_end()
                    if "o" in cfg.get("s5ph", "prmo"):
                        s5_post(i, L, cur, mdst)
                    cur = mdst
            if do_ffn:
                dst = out if last else xa
                ffn_half(L, 0, cur, xh)
                ffn_half(L, 1, xh, dst)
                cur = dst
        with nc.Block() as block:
            S.emit(nc, block)
    return nc


def _consts():
    ident = np.eye(128, dtype=np.float32).astype(ml_dtypes.bfloat16)
    tri = (np.arange(128)[:, None] <= np.arange(128)[None, :]).astype(np.float32).astype(ml_dtypes.bfloat16)
    half = 16
    inv = (np.float32(10000.0) ** (-np.arange(half, dtype=np.float32) / np.float32(half))).astype(np.float32)
    ang = (np.arange(SEQ, dtype=np.float32)[:, None] * inv[None, :]).astype(np.float32)
    return {"ident": ident, "trimask": tri, "rope_cos": np.cos(ang).astype(np.float32),
            "rope_sin": np.sin(ang).astype(np.float32)}


PARAM_KEYS = ["mix_norm_g", "ffn_norm_g", "w_in", "g_cq", "w_uq", "g_ckv", "w_ukv", "g_qn_mla", "g_kn_mla",
              "g_qn_moba", "g_kn_moba", "w_o", "lam_re", "lam_im", "log_dt", "b_re", "b_im", "c_re", "c_im",
              "d_skip", "w_glu", "w_ff1", "w_ff2"]


def kernel(**inputs):
    nc = build({})
    consts = _consts()
    x = np.ascontiguousarray(inputs["x"], dtype=np.float32).reshape(NCORES, TOK, D)
    params = {k: np.ascontiguousarray(inputs[k], dtype=np.float32) for k in PARAM_KEYS}
    in_maps = []
    for c in range(NCORES):
        m = {"x": x[c]}
        m.update(params)
        m.update(consts)
        in_maps.append(m)
    res = run_bass_kernel_spmd(nc, in_maps, core_ids=list(range(NCORES)))
    outs = [np.asarray(r["out"]) for r in res.results]
    return np.stack(outs, 0).reshape(16, SEQ, D).astype(np.float32)
```
